# Optimizing a Trainium2 kernel written in Bass

```python
import math
import jax, jax.numpy as jnp
from jax import lax
import numpy as np

D_MODEL = 1024
BATCH = 8
SEQ = 4096
DEPTH = 1

D_SSM = D_MODEL
SSM_GROUP = 16
N_GROUPS = D_SSM // SSM_GROUP
STATE = 64
N_DIR = 2
DT_MIN = 1e-3
DT_MAX = 1e-1
D_CONV = D_MODEL
CONV_W = 3
EPS = 1e-6
SPLIT_SIZES = (D_SSM, D_SSM, D_CONV, D_CONV, D_CONV, D_CONV, D_MODEL, D_MODEL)
D_IN_PROJ = sum(SPLIT_SIZES)

kernel_name = "hybrid_s5_shortconv_gated_block"


def rmsnorm(x, g):
    xf = x.astype(jnp.float32)
    xf = xf * lax.rsqrt(jnp.mean(xf * xf, axis=-1, keepdims=True) + EPS)
    return (xf * g.astype(jnp.float32)).astype(x.dtype)


def cmul(ar, ai, br, bi):
    return ar * br - ai * bi, ar * bi + ai * br


def s5_scan(u, lam_re, lam_im, log_dt, b_re, b_im, c_re, c_im, reverse):
    lam_re = lam_re.astype(jnp.float32)
    lam_im = lam_im.astype(jnp.float32)
    dt = jnp.exp(log_dt.astype(jnp.float32))[:, None]
    mag = jnp.exp(lam_re * dt)
    ab_re, ab_im = mag * jnp.cos(lam_im * dt), mag * jnp.sin(lam_im * dt)
    den = lam_re * lam_re + lam_im * lam_im
    nr, ni = ab_re - 1.0, ab_im
    coef_re = (nr * lam_re + ni * lam_im) / den
    coef_im = (ni * lam_re - nr * lam_im) / den
    bb_re, bb_im = cmul(coef_re[..., None], coef_im[..., None],
                        b_re.astype(jnp.float32), b_im.astype(jnp.float32))
    bu_re = jnp.einsum('blgh,gph->blgp', u, bb_re)
    bu_im = jnp.einsum('blgh,gph->blgp', u, bb_im)
    a_re = jnp.broadcast_to(ab_re, bu_re.shape)
    a_im = jnp.broadcast_to(ab_im, bu_re.shape)

    def combine(e1, e2):
        a1r, a1i, b1r, b1i = e1
        a2r, a2i, b2r, b2i = e2
        ar, ai = cmul(a2r, a2i, a1r, a1i)
        tr, ti = cmul(a2r, a2i, b1r, b1i)
        return ar, ai, tr + b2r, ti + b2i

    _, _, s_re, s_im = lax.associative_scan(combine, (a_re, a_im, bu_re, bu_im),
                                            axis=1, reverse=reverse)
    return (jnp.einsum('blgp,ghp->blgh', s_re, c_re.astype(jnp.float32))
            - jnp.einsum('blgp,ghp->blgh', s_im, c_im.astype(jnp.float32)))


def setup_inputs(seed: int = 0) -> dict:
    key = jax.random.key(seed)
    ks = jax.random.split(key, 20)
    f32 = jnp.float32
    x = jax.random.normal(ks[0], (BATCH, SEQ, D_MODEL), f32)
    norm_g = 1.0 + 0.02 * jax.random.normal(ks[1], (DEPTH, D_MODEL), f32)
    w_in = jax.random.normal(ks[2], (DEPTH, D_MODEL, D_IN_PROJ), f32) * D_MODEL ** -0.5
    n = jnp.arange(STATE, dtype=f32)
    lam_re = -0.5 + 0.01 * jax.random.normal(ks[3], (DEPTH, N_DIR, N_GROUPS, STATE), f32)
    lam_im = math.pi * n + 0.01 * jax.random.normal(ks[4], (DEPTH, N_DIR, N_GROUPS, STATE), f32)
    log_dt = jax.random.uniform(ks[5], (DEPTH, N_DIR, N_GROUPS), f32,
                                minval=math.log(DT_MIN), maxval=math.log(DT_MAX))
    b_scale = (2.0 * SSM_GROUP) ** -0.5
    c_scale = (2.0 * STATE) ** -0.5
    ssm_b_re = jax.random.normal(ks[6], (DEPTH, N_DIR, N_GROUPS, STATE, SSM_GROUP), f32) * b_scale
    ssm_b_im = jax.random.normal(ks[7], (DEPTH, N_DIR, N_GROUPS, STATE, SSM_GROUP), f32) * b_scale
    ssm_c_re = jax.random.normal(ks[8], (DEPTH, N_DIR, N_GROUPS, SSM_GROUP, STATE), f32) * c_scale
    ssm_c_im = jax.random.normal(ks[9], (DEPTH, N_DIR, N_GROUPS, SSM_GROUP, STATE), f32) * c_scale
    ssm_d = jax.random.normal(ks[10], (DEPTH, D_SSM), f32)
    w_glu = jax.random.normal(ks[11], (DEPTH, D_SSM, D_SSM), f32) * D_SSM ** -0.5
    conv_w = jax.random.normal(ks[12], (DEPTH, CONV_W, D_CONV), f32) * CONV_W ** -0.5
    conv_b = 0.01 * jax.random.normal(ks[13], (DEPTH, D_CONV), f32)
    w_branch_a = jax.random.normal(ks[14], (DEPTH, D_SSM, D_MODEL), f32) * D_SSM ** -0.5
    w_branch_b = jax.random.normal(ks[15], (DEPTH, D_CONV, D_MODEL), f32) * D_CONV ** -0.5
    w_out = jax.random.normal(ks[16], (DEPTH, D_MODEL, D_MODEL), f32) * D_MODEL ** -0.5
    final_g = 1.0 + 0.02 * jax.random.normal(ks[17], (D_MODEL,), f32)
    return {"x": x, "norm_g": norm_g, "w_in": w_in, "lam_re": lam_re, "lam_im": lam_im,
            "log_dt": log_dt, "ssm_b_re": ssm_b_re, "ssm_b_im": ssm_b_im,
            "ssm_c_re": ssm_c_re, "ssm_c_im": ssm_c_im, "ssm_d": ssm_d, "w_glu": w_glu,
            "conv_w": conv_w, "conv_b": conv_b, "w_branch_a": w_branch_a,
            "w_branch_b": w_branch_b, "w_out": w_out, "final_g": final_g}


def reference(x, norm_g, w_in, lam_re, lam_im, log_dt, ssm_b_re, ssm_b_im, ssm_c_re,
              ssm_c_im, ssm_d, w_glu, conv_w, conv_b, w_branch_a, w_branch_b, w_out,
              final_g):
    B, L, _ = x.shape
    split_idx = [int(i) for i in np.cumsum(np.array(SPLIT_SIZES))[:-1]]
    h = x
    for layer in range(DEPTH):
        xn = rmsnorm(h, norm_g[layer])
        proj = jnp.einsum('bld,de->ble', xn, w_in[layer])
        u, z_a, v, b_g, c_g, z_b, g_a, g_b = jnp.split(proj, split_idx, axis=-1)

        uf = u.astype(jnp.float32).reshape(B, L, N_GROUPS, SSM_GROUP)
        y = ssm_d[layer].astype(jnp.float32).reshape(N_GROUPS, SSM_GROUP) * uf
        for d in range(N_DIR):
            y = y + s5_scan(uf, lam_re[layer, d], lam_im[layer, d], log_dt[layer, d],
                            ssm_b_re[layer, d], ssm_b_im[layer, d],
                            ssm_c_re[layer, d], ssm_c_im[layer, d], reverse=(d == 1))
        y = jax.nn.gelu(y.reshape(B, L, D_SSM).astype(x.dtype))
        y = y * jax.nn.sigmoid(jnp.einsum('ble,ef->blf', y, w_glu[layer]))
        y_a = y * jax.nn.silu(z_a)

        cv = c_g * v
        pad = (CONV_W - 1) // 2
        cvp = jnp.pad(cv, ((0, 0), (pad, CONV_W - 1 - pad), (0, 0)))
        conv = conv_b[layer]
        for k in range(CONV_W):
            conv = conv + cvp[:, k:k + L] * conv_w[layer, k]
        y_b = b_g * conv * jax.nn.silu(z_b)

        o_a = jnp.einsum('ble,ed->bld', y_a, w_branch_a[layer])
        o_b = jnp.einsum('ble,ed->bld', y_b, w_branch_b[layer])
        merged = jax.nn.sigmoid(g_a) * o_a + jax.nn.sigmoid(g_b) * o_b
        h = h + jnp.einsum('bld,de->ble', merged, w_out[layer])
    return rmsnorm(h, final_g)
```

```python
import math
import numpy as np
import concourse.bass as bass
import concourse.mybir as mybir
from concourse.bass_utils import run_bass_kernel_spmd
from concourse.ap import AP

F32 = mybir.dt.float32
BF16 = mybir.dt.bfloat16
I32 = mybir.dt.int32
AF = mybir.ActivationFunctionType
ALU = mybir.AluOpType

ENGS = ("tensor", "vector", "scalar", "gpsimd", "sync")
L = 4096
D = 1024
NT = 8
EPS = 1e-6


class Prog:
    def __init__(self, nc):
        self.nc = nc
        self.ops = {e: [] for e in ENGS}
        self.esem = {e: nc.alloc_semaphore(name=f"es_{e}") for e in ENGS}
        self.ecnt = {e: 0 for e in ENGS}
        self.dsem = {}
        self.last_w = {}
        self.readers = {}
        self.waited = {e: {} for e in ENGS}
        self.final_tokens = []

    def _need(self, eng, tok, waits):
        if tok is None:
            return
        sem, val, weng = tok
        if weng == eng and eng == "tensor":
            return
        key = id(sem)
        if self.waited[eng].get(key, 0) >= val:
            return
        self.waited[eng][key] = val
        waits.append((sem, val))

    def barrier(self):
        toks = [(self.esem[e], self.ecnt[e], e) for e in ENGS if self.ecnt[e] > 0]
        toks += [(ent[0], ent[1], "dma") for ent in self.dsem.values()]
        self.pending = {e: list(toks) for e in ENGS}

    def op(self, eng, fn, reads=(), writes=(), dma_key=None, final=False, inc=True):
        waits = []
        for t in getattr(self, "pending", {}).pop(eng, []):
            if t[2] != eng:
                self._need(eng, t, waits)
        for r in reads:
            self._need(eng, self.last_w.get(r), waits)
        for w in writes:
            self._need(eng, self.last_w.get(w), waits)
            for t in self.readers.get(w, ()):
                self._need(eng, t, waits)
        if dma_key is not None:
            if dma_key not in self.dsem:
                self.dsem[dma_key] = [self.nc.alloc_semaphore(name=f"ds_{len(self.dsem)}"), 0]
            ent = self.dsem[dma_key]
            ent[1] += 16
            tok = (ent[0], ent[1], "dma")
            inc = (ent[0], 16)
        elif not inc:
            tok = (self.esem[eng], self.ecnt[eng] + 1, eng)
            inc = None
        else:
            self.ecnt[eng] += 1
            tok = (self.esem[eng], self.ecnt[eng], eng)
            inc = (self.esem[eng], 1)
        for r in reads:
            self.readers.setdefault(r, []).append(tok)
        for w in writes:
            self.last_w[w] = tok
            self.readers[w] = []
        self.ops[eng].append((fn, waits, inc))
        if final:
            self.final_tokens.append(tok)
        return tok

    def emit(self):
        nc = self.nc
        finals = self.final_tokens
        with nc.Block() as block:
            def run(engname, e):
                for fn, waits, inc in self.ops[engname]:
                    for sem, val in waits:
                        e.wait_ge(sem, val)
                    ins = fn(e)
                    if inc is not None:
                        ins.then_inc(inc[0], inc[1])
                if engname == "sync":
                    for sem, val, _ in finals:
                        e.wait_ge(sem, val)

            @block.tensor
            def _(e):
                run("tensor", e)

            @block.vector
            def _(e):
                run("vector", e)

            @block.scalar
            def _(e):
                run("scalar", e)

            @block.gpsimd
            def _(e):
                run("gpsimd", e)

            @block.sync
            def _(e):
                run("sync", e)


def rev_last(ap2d):
    (ps, pc), (fs, fc) = ap2d.ap
    return AP(ap2d.tensor, ap2d.offset + fs * (fc - 1), [[ps, pc], [-fs, fc]])


def cols_strided(ap2d, start, step, n):
    (ps, pc), (fs, fc) = ap2d.ap
    return AP(ap2d.tensor, ap2d.offset + fs * start, [[ps, pc], [fs * step, n]])


def build_program(debug=False):
    nc = bass.Bass("TRN2", target_bir_lowering=False)

    def din(name, shape, dt=F32):
        return nc.dram_tensor(name, list(shape), dt, kind="ExternalInput").ap()

    x = din("x", [L, D])
    w_in = din("w_in", [D, 8 * D])
    w_glu = din("w_glu", [D, D]); w_a = din("w_a", [D, D]); w_b = din("w_b", [D, D]); w_out = din("w_out", [D, D])
    g_bc_d = din("g_bc", [128, D]); fg_bc_d = din("fg_bc", [128, D])
    dcol_d = din("dcol", [128, 8]); cw_d = din("cw", [128, 24]); cb_d = din("cb", [128, 8])
    lre_d = din("lre", [128, 64]); lim_d = din("lim", [128, 64]); ldt_d = din("ldt", [128, 64])
    bre_d = din("b_re", [128, 1024]); bim_d = din("b_im", [128, 1024])
    cre_d = din("c_re", [128, 1024]); cim_d = din("c_im", [128, 1024])
    ident_d = din("ident", [128, 128]); iidx_d = din("iidx", [128, 512])
    jR_d = din("jR", [128, 512]); jD_d = din("jD", [128, 512])
    bdm_d = din("bdmask", [128, 128]); pm_d = din("parmask", [128, 2])
    out = nc.dram_tensor("out", [L, D], F32, kind="ExternalOutput").ap()
    wsc = nc.dram_tensor("wsc", [44, 128, 2048], BF16, kind="Internal").ap()

    P = Prog(nc)

    def sb(name, shape, dt=F32):
        return nc.alloc_sbuf_tensor(name, list(shape), dt).ap()

    u_all = sb("u_all", [128, 8, L], BF16)
    xnT = sb("xnT", [128, 8, 512], BF16)
    xnh = sb("xnh", [128, 8, 2], BF16)
    identb = sb("identb", [128, 128], BF16); identf = sb("identf", [128, 128])
    bdmask = sb("bdmask_s", [128, 128]); parmask = sb("parmask_s", [128, 2])
    iidx = sb("iidx_s", [128, 512])
    dcol = sb("dcol_s", [128, 8]); cw = sb("cw_s", [128, 24]); cb = sb("cb_s", [128, 8])
    thn = sb("thn", [128, 64]); thn8 = sb("thn8", [128, 64]); rho8 = sb("rho8", [128, 64])
    coef_re = sb("coef_re", [128, 64]); coef_im = sb("coef_im", [128, 64])
    sm = [sb(f"sm{i}", [128, 64]) for i in range(8)]
    ssq = sb("ssq", [128, 1]); std = sb("std", [128, 1]); rstd = sb("rstd", [128, 1])
    epsb = sb("epsb", [128, 1]); halfpi = sb("halfpi", [128, 1])
    F = [sb(f"F{i}", [128, 514]) for i in range(9)]
    FI = sb("FI", [128, 512], I32)
    smi = FI
    Hall = sb("Hall", [128, 6 * 512], BF16)
    H = [Hall[:, i * 512:(i + 1) * 512] for i in range(6)]
    ARENA = 49664
    arena = sb("arena", [128, ARENA], BF16)

    def carve(off_el, shape, dt=BF16):
        n = int(np.prod(shape[1:]))
        if dt == F32:
            assert off_el + 2 * n <= ARENA
            a = arena[:, off_el:off_el + 2 * n].bitcast(F32)
        else:
            assert off_el + n <= ARENA
            a = arena[:, off_el:off_el + n]
        if len(shape) == 3:
            a = a.rearrange("p (a b) -> p a b", b=shape[2])
        if len(shape) == 4:
            a = a.rearrange("p (a b c) -> p a b c", b=shape[2], c=shape[3])
        return a

    xt = [carve(40960 + 2048 * i, [128, D], F32) for i in range(2)]
    xs = carve(45056, [128, D])
    g_bc = carve(46080, [128, D], F32)
    xh = carve(48128, [128, 512], F32)
    Wu = carve(0, [128, 64, 128])
    Braw_re = carve(0, [128, 1024], F32); Braw_im = carve(2048, [128, 1024], F32)
    Cp_re = carve(4096, [128, 1024], F32); Cp_im = carve(6144, [128, 1024], F32)
    pwR_re = carve(8192, [128, 512], F32); pwR_im = carve(9216, [128, 512], F32)
    pwD_re = carve(10240, [128, 512], F32); pwD_im = carve(11264, [128, 512], F32)
    T1 = carve(12288, [128, 1024], F32); T2 = carve(14336, [128, 1024], F32)
    T3 = carve(16384, [128, 1024], F32); T4 = carve(18432, [128, 1024], F32)
    urev = carve(20480, [128, L])
    yf_acc = carve(24576, [128, L]); yb_acc = carve(28672, [128, L])
    BKT = {(par, ri): carve(32768 + 1024 * (2 * par + ri), [128, 8, 128]) for par in range(2) for ri in range(2)}
    DCz_re = carve(36864, [128, 8, 8, 32]); DCz_imn = carve(38912, [128, 8, 8, 32])
    Sst = carve(40960, [128, 16, 514])
    Wout = carve(0, [128, 8, 1024])
    ya = carve(8192, [128, 8, 512]); yb = carve(12288, [128, 8, 512]); mg = carve(16384, [128, 8, 512])
    wslot = [carve(20480 + 2048 * i, [128, 16, 128]) for i in range(2)] + [carve(36864 + 2048 * i, [128, 16, 128]) for i in range(2)]
    xnT2 = carve(32768, [128, 8, 512])
    otile = carve(24576, [128, D], F32); hbuf = carve(26624, [128, D], F32)
    fg_bc = carve(28672, [128, D], F32)
    xh = carve(30720, [128, D], F32)[0:2, :]
    wstage = [carve(20480 + 1024 * i, [128, 8, 128]) for i in range(6)]
    wstage_f = [carve(8192 + 2048 * i, [128, 8, 128], F32) for i in range(6)]
    TZf = Hall[:, 2 * 512:4 * 512].rearrange("p (a b) -> p a b", b=128)
    TZb = Hall[:, 4 * 512:6 * 512].rearrange("p (a b) -> p a b", b=128)

    ps = [nc.alloc_psum_tensor(f"ps{i}", [128, 512], F32).ap() for i in range(7)]
    psT = nc.alloc_psum_tensor("psT", [128, 8, 128], BF16).ap()

    TWO_PI = 2.0 * math.pi

    cap = [None]

    def _op(eng, fn, r, w, inc=True):
        if cap[0] is not None:
            cap[0].append(lambda: P.op(eng, fn, reads=r, writes=w, inc=inc))
            return None
        return P.op(eng, fn, reads=r, writes=w, inc=inc)

    def captured(f, *args):
        cap[0] = []
        f(*args)
        ops = cap[0]
        cap[0] = None
        return ops

    def V(fn, r=(), w=()):
        return _op("vector", fn, r, w)

    def A(fn, r=(), w=()):
        return _op("scalar", fn, r, w)

    def G(fn, r=(), w=()):
        return _op("gpsimd", fn, r, w)

    def T(fn, r=(), w=(), inc=True):
        return _op("tensor", fn, r, w, inc)

    def load(dst, src, key, eng="sync", extra_w=()):
        return P.op(eng, lambda e: e.dma_start(out=dst, in_=src), writes=[key] + list(extra_w), dma_key=key)

    load(g_bc, g_bc_d, "g_bc")
    load(identb, ident_d, "identb", eng="gpsimd"); load(identf, ident_d, "identf")
    load(bdmask, bdm_d, "bdmask"); load(parmask, pm_d, "parmask")
    load(iidx, iidx_d, "iidx")
    load(dcol, dcol_d, "dcol"); load(cw, cw_d, "cw"); load(cb, cb_d, "cb")
    V(lambda e: e.memset(epsb, EPS), w=["epsb"])
    V(lambda e: e.memset(halfpi, math.pi / 2), w=["halfpi"])
    w_in_v = w_in.rearrange("(kc p) n -> p kc n", p=128)
    srcs = [("Wu", w_in_v, i, None) for i in range(8)]
    w_glu_v = w_glu.rearrange("(kc p) n -> p kc n", p=128); w_a_v = w_a.rearrange("(kc p) n -> p kc n", p=128)
    w_b_v = w_b.rearrange("(kc p) n -> p kc n", p=128); w_out_v = w_out.rearrange("(kc p) n -> p kc n", p=128)
    for f in range(8):
        srcs += [("sc", w_in_v, 8 + f, (3 * f, 0)), ("sc", w_glu_v, f, (3 * f, 1)),
                 ("sc", w_in_v, 16 + f, (3 * f + 1, 0)), ("sc", w_in_v, 32 + f, (3 * f + 1, 1)),
                 ("sc", w_in_v, 24 + f, (3 * f + 2, 0)), ("sc", w_in_v, 40 + f, (3 * f + 2, 1))]
    for dch in range(8):
        srcs += [("sc", w_a_v, dch, (24 + 2 * dch, 0)), ("sc", w_in_v, 48 + dch, (24 + 2 * dch, 1)),
                 ("sc", w_b_v, dch, (25 + 2 * dch, 0)), ("sc", w_in_v, 56 + dch, (25 + 2 * dch, 1))]
    for nb in range(8):
        srcs.append(("sc", w_out_v, nb, (40 + nb // 2, nb % 2)))

    def conv_chunk(n_):
        kind, wv, cbk, dst = srcs[n_]
        sl = n_ % 6
        sf = wstage_f[sl]
        kf = f"wstf{sl}"
        P.op("sync", (lambda sf=sf, wv=wv, cbk=cbk: lambda e: e.dma_start(out=sf, in_=wv[:, :, cbk * 128:(cbk + 1) * 128]))(),
             writes=[kf], dma_key=kf)
        if kind == "Wu":
            V((lambda sf=sf, cbk=cbk: lambda e: e.tensor_copy(Wu[:, cbk * 8:(cbk + 1) * 8, :], sf))(), r=[kf], w=[f"Wu{cbk}"])
        else:
            st = wstage[sl]
            kb = f"wstb{sl}"
            pr, hf = dst
            (G if n_ % 4 == 3 else V)((lambda sf=sf, st=st: lambda e: e.tensor_copy(st, sf))(), r=[kf], w=[kb])
            P.op("gpsimd", (lambda st=st, pr=pr, hf=hf: lambda e: e.dma_start(
                out=wsc[pr][:, hf * 1024:(hf + 1) * 1024].rearrange("p (a b) -> p a b", b=128), in_=st))(),
                 reads=[kb], writes=[f"wsc{pr}_{hf}"], dma_key=kb + "_st")

    for n_ in range(8):
        conv_chunk(n_)

    xcnt = [0]

    def norm_rows(src_rows_ap, nrows, dst_xnT_ap, dst_key="xnT", defer=False):
        slot = xcnt[0] % 2
        xcnt[0] += 1
        xb = xt[slot][0:nrows, :]
        kx = f"xt{slot}"
        load(xb, src_rows_ap, kx)
        return finish_norm(xb, kx, nrows, dst_xnT_ap, dst_key=dst_key, defer=defer)

    def finish_norm(xb, kx, nrows, dst, dst_key="xnT", defer=False):
        A(lambda e: e.activation(F[7][0:nrows, 0:512], xb[:, 0:512], AF.Square, accum_out=ssq[0:nrows, :]),
          r=[kx], w=["ssq", "F7"])
        A(lambda e: e.activation(F[7][0:nrows, 0:512], xb[:, 512:1024], AF.Square, accum_out=std[0:nrows, :]),
          r=[kx], w=["std", "F7"])
        V(lambda e: e.tensor_tensor(ssq[0:nrows, :], ssq[0:nrows, :], std[0:nrows, :], ALU.add), r=["ssq", "std"], w=["ssq"])
        A(lambda e: e.activation(std[0:nrows, :], ssq[0:nrows, :], AF.Sqrt, bias=epsb[0:nrows, :], scale=1.0 / D),
          r=["ssq", "epsb"], w=["std"])
        V(lambda e: e.reciprocal(rstd[0:nrows, :], std[0:nrows, :]), r=["std"], w=["rstd"])
        V(lambda e: e.scalar_tensor_tensor(xs[0:nrows, :], xb, rstd[0:nrows, :], g_bc[0:nrows, :], ALU.mult, ALU.mult),
          r=[kx, "rstd", "g_bc"], w=["xs"])

        def post():
            for kc in range(8):
                T((lambda kc=kc: lambda e: e.transpose(psT[:, kc, 0:nrows], xs[0:nrows, kc * 128:(kc + 1) * 128], identb[0:nrows, 0:nrows]))(),
                  r=["xs", "identb"], w=["psT"], inc=(kc == 7))
            A(lambda e: e.activation(dst, psT[:, :, 0:nrows], AF.Copy), r=["psT"], w=[dst_key])
        if defer:
            return post
        post()
        return None

    def make_xn_block(tt, s, dstT, kT, defer=False):
        r0 = tt * 512 + s * 128
        return norm_rows(x[r0:r0 + 128, :], 128, dstT[:, :, s * 128:(s + 1) * 128], dst_key=kT, defer=defer)

    def make_xn(tt, with_halo, dstT=None, dsth=None, kT="xnT", kh="xnh", blocks=True, defer_halo=False):
        dstT = xnT if dstT is None else dstT
        dsth = xnh if dsth is None else dsth
        if blocks:
            for s in range(4):
                make_xn_block(tt, s, dstT, kT)
        if with_halo:
            V(lambda e: e.memset(xh, 0.0), w=["xh"])
            if tt > 0:
                load(xh[0:1, :], x[tt * 512 - 1:tt * 512, :], "xh")
            if tt < NT - 1:
                load(xh[1:2, :], x[(tt + 1) * 512:(tt + 1) * 512 + 1, :], "xh")
            return finish_norm(xh, "xh", 2, dsth, dst_key=kh, defer=defer_halo)
        return None

    for tt in range(NT):
        make_xn(tt, False)
        for n_ in range(8 + 11 * tt, 8 + 11 * (tt + 1)):
            conv_chunk(n_)
        for j in range(8):
            pb = j % 2
            for kc in range(8):
                T((lambda j=j, kc=kc, pb=pb: lambda e: e.matmul(ps[pb], Wu[:, j * 8 + kc, :], xnT[:, kc, :], start=(kc == 0), stop=(kc == 7)))(),
                  r=[f"Wu{j}", "xnT"], w=[f"ps{pb}"], inc=(kc == 7))
            V((lambda j=j, pb=pb, tt=tt: lambda e: e.tensor_copy(
                u_all[:, j, :].rearrange("p (k c) -> p k c", k=8)[:, :, 64 * tt:64 * tt + 64],
                ps[pb].rearrange("p (c k) -> p k c", k=8)))(),
              r=[f"ps{pb}"], w=[f"u{j}"])

    P.barrier()
    lre, lim, dtt, lr, th, t0, t1, t2 = sm
    load(lre, lre_d, "lre"); load(lim, lim_d, "lim"); load(dtt, ldt_d, "ldt")
    load(Braw_re, bre_d, "Braw_re"); load(Braw_im, bim_d, "Braw_im")
    load(T1, cre_d, "T1"); load(T2, cim_d, "T2")
    load(T3[:, 0:512], jR_d, "T3"); load(T4[:, 0:512], jD_d, "T4")
    A(lambda e: e.activation(dtt, dtt, AF.Exp), r=["ldt"], w=["dtt"])
    V(lambda e: e.tensor_tensor(lr, lre, dtt, ALU.mult), r=["lre", "dtt"], w=["lr"])
    V(lambda e: e.tensor_tensor(th, lim, dtt, ALU.mult), r=["lim", "dtt"], w=["th"])
    V(lambda e: e.tensor_scalar(thn, th, 1.0 / TWO_PI, None, ALU.mult), r=["th"], w=["thn"])
    A(lambda e: e.activation(rho8, lr, AF.Exp, scale=8.0), r=["lr"], w=["rho8"])

    def frac_op(dst, src, n, kd, ks):
        V(lambda e: e.tensor_copy(smi[:, 0:n], src), r=[ks], w=["FI"])
        V(lambda e: e.tensor_copy(F[8][:, 0:n], smi[:, 0:n]), r=["FI"], w=["F8"])
        V(lambda e: e.tensor_tensor(dst, src, F[8][:, 0:n], ALU.subtract), r=[ks, "F8"], w=[kd])

    def sincos(sin_dst, cos_dst, frac_src, tmp, ks, kk_s, kk_c, kt):
        A(lambda e: e.activation(sin_dst, frac_src, AF.Sin, scale=TWO_PI), r=[ks], w=[kk_s])
        A(lambda e: e.activation(tmp, frac_src, AF.Abs), r=[ks], w=[kt])
        A(lambda e: e.activation(cos_dst, tmp, AF.Sin, bias=halfpi, scale=-TWO_PI), r=[kt, "halfpi"], w=[kk_c])

    V(lambda e: e.tensor_scalar(thn8, thn, 8.0, None, ALU.mult), r=["thn"], w=["thn8"])
    frac_op(thn8, thn8, 64, "thn8", "thn8")
    rho1 = sm[4]
    A(lambda e: e.activation(rho1, lr, AF.Exp), r=["lr", "thn"], w=["rho1"])
    frac_op(t0, thn, 64, "t0", "thn")
    sincos(t1, t2, t0, F[6][:, 0:64], "t0", "t1", "t2", "F6")
    V(lambda e: e.tensor_tensor(t1, t1, rho1, ALU.mult), r=["t1", "rho1"], w=["t1"])
    V(lambda e: e.tensor_tensor(t2, t2, rho1, ALU.mult), r=["t2", "rho1"], w=["t2"])
    V(lambda e: e.tensor_scalar(t2, t2, -1.0, None, ALU.add), r=["t2"], w=["t2"])
    V(lambda e: e.tensor_tensor(t0, lre, lre, ALU.mult), r=["lre"], w=["t0"])
    V(lambda e: e.tensor_tensor(dtt, lim, lim, ALU.mult), r=["lim"], w=["dtt"])
    V(lambda e: e.tensor_tensor(t0, t0, dtt, ALU.add), r=["t0", "dtt"], w=["t0"])
    V(lambda e: e.reciprocal(dtt, t0), r=["t0"], w=["dtt"])
    V(lambda e: e.tensor_tensor(t0, t2, lre, ALU.mult), r=["t2", "lre"], w=["t0"])
    V(lambda e: e.tensor_tensor(rho1, t1, lim, ALU.mult), r=["t1", "lim"], w=["rho1"])
    V(lambda e: e.tensor_tensor(t0, t0, rho1, ALU.add), r=["t0", "rho1"], w=["t0"])
    V(lambda e: e.tensor_tensor(coef_re, t0, dtt, ALU.mult), r=["t0", "dtt"], w=["coef_re"])
    V(lambda e: e.tensor_tensor(t0, t1, lre, ALU.mult), r=["t1", "lre"], w=["t0"])
    V(lambda e: e.tensor_tensor(rho1, t2, lim, ALU.mult), r=["t2", "lim"], w=["rho1"])
    V(lambda e: e.tensor_tensor(t0, t0, rho1, ALU.subtract), r=["t0", "rho1"], w=["t0"])
    V(lambda e: e.tensor_tensor(coef_im, t0, dtt, ALU.mult), r=["t0", "dtt"], w=["coef_im"])

    c3 = lambda a: a.rearrange("p (g h) -> p g h", h=16)
    bch = lambda a: a.unsqueeze(2).broadcast_to([128, 64, 16])
    V(lambda e: e.tensor_tensor(c3(Cp_re), c3(T1), bch(coef_re), ALU.mult), r=["T1", "coef_re"], w=["Cp_re"])
    V(lambda e: e.tensor_tensor(c3(Cp_im), c3(T2), bch(coef_im), ALU.mult), r=["T2", "coef_im"], w=["Cp_im"])
    V(lambda e: e.tensor_tensor(Cp_re, Cp_re, Cp_im, ALU.subtract), r=["Cp_re", "Cp_im"], w=["Cp_re"])
    V(lambda e: e.tensor_tensor(c3(Cp_im), c3(T1), bch(coef_im), ALU.mult), r=["T1", "coef_im"], w=["Cp_im"])
    V(lambda e: e.tensor_tensor(c3(T1), c3(T2), bch(coef_re), ALU.mult), r=["T2", "coef_re", "Cp_re"], w=["T1"])
    V(lambda e: e.tensor_tensor(Cp_im, Cp_im, T1, ALU.add), r=["Cp_im", "T1"], w=["Cp_im"])
    V(lambda e: e.tensor_scalar(Cp_im, Cp_im, -1.0, None, ALU.mult), r=["Cp_im"], w=["Cp_im"])

    g8 = lambda a: a.rearrange("p (g k) -> p g k", k=8)
    bck = lambda a: a.unsqueeze(2).broadcast_to([128, 64, 8])
    for (jt, kj, p_re, p_im, kre, kim) in ((T3, "T3", pwR_re, pwR_im, "pwR_re", "pwR_im"), (T4, "T4", pwD_re, pwD_im, "pwD_re", "pwD_im")):
        jv = jt[:, 0:512]
        mag = jt[:, 512:1024]
        V((lambda jv=jv, mag=mag: lambda e: e.tensor_tensor(g8(mag), g8(jv), bck(lr), ALU.mult))(), r=[kj, "lr"], w=[kj + "m"])
        A((lambda mag=mag: lambda e: e.activation(mag, mag, AF.Exp))(), r=[kj + "m"], w=[kj + "m"])
        V((lambda jv=jv: lambda e: e.tensor_tensor(g8(F[0][:, 0:512]), g8(jv), bck(thn), ALU.mult))(), r=[kj, "thn"], w=["F0"])
        frac_op(F[0][:, 0:512], F[0][:, 0:512], 512, "F0", "F0")
        sincos(F[1][:, 0:512], F[2][:, 0:512], F[0][:, 0:512], F[3][:, 0:512], "F0", "F1", "F2", "F3")
        V((lambda p_re=p_re, mag=mag: lambda e: e.tensor_tensor(p_re, mag, F[2][:, 0:512], ALU.mult))(), r=[kj + "m", "F2"], w=[kre])
        V((lambda p_im=p_im, mag=mag: lambda e: e.tensor_tensor(p_im, mag, F[1][:, 0:512], ALU.mult))(), r=[kj + "m", "F1"], w=[kim])

    V(lambda e: e.memset(DCz_re, 0.0), w=["DCz_re"])
    V(lambda e: e.memset(DCz_imn, 0.0), w=["DCz_imn"])
    V(lambda e: e.memset(Sst, 0.0), w=["Sst"])

    Xre, Xim, Rre, Rim, Tc, Ts, Fr, Ta, Tb = F[0], F[1], F[2], F[3], F[4], F[5], F[6], F[7], F[8]
    W5 = lambda a: a[:, 0:512]
    xnT_f = xnT.rearrange("p a b -> p (a b)").bitcast(F32)
    tabs = [(Tc, Ts, Fr, Ta, "F4", "F5", "F6", "F7"),
            (xnT_f[:, 0:512], xnT_f[:, 512:1024], xnT_f[:, 1024:1536], xnT_f[:, 1536:2048], "X4", "X5", "X6", "X7")]
    kgh = lambda a: a.rearrange("p (k g h) -> p k g h", k=8, g=8)
    def do_urev(j):
        un = u_all[:, j, :]
        A((lambda un=un: lambda e: e.activation(urev, rev_last(un), AF.Copy))(), r=[f"u{j}"], w=["urev"])
    def tab_xb(j):
        un = u_all[:, j, :]
        def bv(a, j=j):
            return a.rearrange("p (g h) -> p g h", h=16)[:, j * 8:(j + 1) * 8, :].unsqueeze(1).broadcast_to([128, 8, 8, 16])

        def pv(a, j=j):
            return a.rearrange("p (g k) -> p k g", k=8)[:, :, j * 8:(j + 1) * 8].unsqueeze(3).broadcast_to([128, 8, 8, 16])

        V(lambda e, bv=bv, pv=pv: e.tensor_tensor(kgh(T1), bv(Braw_re), pv(pwR_re), ALU.mult), r=["Braw_re", "pwR_re", "TZgen", "BKTgen"], w=["T1"])
        V(lambda e, bv=bv, pv=pv: e.tensor_tensor(kgh(T3), bv(Braw_im), pv(pwR_im), ALU.mult), r=["Braw_im", "pwR_im"], w=["T3"])
        V(lambda e: e.tensor_tensor(T1, T1, T3, ALU.subtract), r=["T1", "T3"], w=["T1"])
        V(lambda e, bv=bv, pv=pv: e.tensor_tensor(kgh(T2), bv(Braw_re), pv(pwR_im), ALU.mult), r=["Braw_re", "pwR_im", "TZgen", "BKTgen"], w=["T2"])
        V(lambda e, bv=bv, pv=pv: e.tensor_tensor(kgh(T3), bv(Braw_im), pv(pwR_re), ALU.mult), r=["Braw_im", "pwR_re"], w=["T3"])
        V(lambda e: e.tensor_tensor(T2, T2, T3, ALU.add), r=["T2", "T3"], w=["T2"])
    def tab_dcz(j):
        un = u_all[:, j, :]
        def cv4(a, j=j):
            return a.rearrange("p (g h) -> p g h", h=16)[:, j * 8:(j + 1) * 8, :].unsqueeze(2).broadcast_to([128, 8, 8, 16])

        def pd4(a, j=j):
            return a.rearrange("p (g k) -> p g k", k=8)[:, j * 8:(j + 1) * 8, :].unsqueeze(3).broadcast_to([128, 8, 8, 16])

        gkh = lambda a: a.rearrange("p (g k h) -> p g k h", g=8, k=8)
        V(lambda e, cv4=cv4, pd4=pd4: e.tensor_tensor(gkh(T3), cv4(Cp_re), pd4(pwD_re), ALU.mult), r=["Cp_re", "pwD_re", "T3"], w=["T3"])
        V(lambda e, cv4=cv4, pd4=pd4: e.tensor_tensor(gkh(T4), cv4(Cp_im), pd4(pwD_im), ALU.mult), r=["Cp_im", "pwD_im"], w=["T4"])
        for par in range(2):
            dv = DCz_re.rearrange("p (i e) k c -> p i e k c", e=2)[:, :, par, :, par * 16:(par + 1) * 16]
            a3 = gkh(T3).rearrange("p (i e) k h -> p i e k h", e=2)[:, :, par, :, :]
            a4 = gkh(T4).rearrange("p (i e) k h -> p i e k h", e=2)[:, :, par, :, :]
            V((lambda dv=dv, a3=a3, a4=a4: lambda e: e.tensor_tensor(dv, a3, a4, ALU.add))(), r=["T3", "T4", "stageB"], w=["DCz_re"])
        V(lambda e, cv4=cv4, pd4=pd4: e.tensor_tensor(gkh(T3), cv4(Cp_im), pd4(pwD_re), ALU.mult), r=["Cp_im", "pwD_re", "DCz_re"], w=["T3"])
        V(lambda e, cv4=cv4, pd4=pd4: e.tensor_tensor(gkh(T4), cv4(Cp_re), pd4(pwD_im), ALU.mult), r=["Cp_re", "pwD_im", "DCz_re"], w=["T4"])
        for par in range(2):
            dv = DCz_imn.rearrange("p (i e) k c -> p i e k c", e=2)[:, :, par, :, par * 16:(par + 1) * 16]
            a3 = gkh(T3).rearrange("p (i e) k h -> p i e k h", e=2)[:, :, par, :, :]
            a4 = gkh(T4).rearrange("p (i e) k h -> p i e k h", e=2)[:, :, par, :, :]
            V((lambda dv=dv, a3=a3, a4=a4: lambda e: e.tensor_tensor(dv, a3, a4, ALU.subtract))(), r=["T3", "T4", "stageB"], w=["DCz_imn"])
    def tab_bkt(j):
        un = u_all[:, j, :]
        for ri, (XBt, kxb) in enumerate(((T1, "T1"), (T2, "T2"))):
            for kh in range(2):
                pz = ps[4 + kh]
                for kk in range(4):
                    k = kh * 4 + kk
                    T((lambda pz=pz, kk=kk, k=k, XBt=XBt: lambda e: e.transpose(pz[:, kk * 128:(kk + 1) * 128], XBt[:, k * 128:(k + 1) * 128], identf))(),
                      r=[kxb, "identf"], w=[f"ps{4 + kh}"], inc=(kk == 3))
                for par in range(2):
                    dst = BKT[(par, ri)][:, kh * 4:(kh + 1) * 4, :]
                    A((lambda dst=dst, pz=pz, par=par: lambda e: e.activation(dst, pz.rearrange("p (a b) -> p a b", b=128), AF.Copy, scale=parmask[:, par:par + 1]))(),
                      r=[f"ps{4 + kh}", "parmask", "stageA"], w=[f"BKT{par}{ri}", "BKTgen"])
    def tab_tz(j):
        un = u_all[:, j, :]
        for (TZ, kbase, ktz) in ((TZf, 0, "TZf"), (TZb, 64, "TZb")):
            for lh in range(2):
                pz = ps[4 + lh]
                for ll in range(4):
                    lag = lh * 4 + ll
                    k = 7 - lag
                    T((lambda pz=pz, ll=ll, k=k, kbase=kbase, j=j: lambda e: e.matmul(pz[:, ll * 128:(ll + 1) * 128], T1[kbase:kbase + 64, k * 128:(k + 1) * 128],
                                                                                 Cp_re[kbase:kbase + 64, j * 128:(j + 1) * 128], start=True, stop=False))(),
                      r=["T1", "Cp_re"], w=[f"ps{4 + lh}"], inc=False)
                    T((lambda pz=pz, ll=ll, k=k, kbase=kbase, j=j: lambda e: e.matmul(pz[:, ll * 128:(ll + 1) * 128], T2[kbase:kbase + 64, k * 128:(k + 1) * 128],
                                                                                 Cp_im[kbase:kbase + 64, j * 128:(j + 1) * 128], start=False, stop=True))(),
                      r=["T2", "Cp_im"], w=[f"ps{4 + lh}"], inc=(ll == 3))
                dst = TZ[:, lh * 4:(lh + 1) * 4, :]
                V((lambda dst=dst, pz=pz: lambda e: e.tensor_tensor(dst, pz.rearrange("p (a b) -> p a b", b=128),
                                                                      bdmask.unsqueeze(1).broadcast_to([128, 4, 128]), ALU.mult))(),
                  r=[f"ps{4 + lh}", "bdmask", "stageB"], w=[ktz, "TZgen"])
        V((lambda j=j: lambda e: e.scalar_tensor_tensor(TZf[:, 0, :], identf, dcol[:, j:j + 1], TZf[:, 0, :], ALU.mult, ALU.add))(),
          r=["TZf", "identf", "dcol"], w=["TZf"])
    def gen_tables(j, gl):
        g = j * 8 + gl
        Tc, Ts, Fr, Ta, kTc, kTs, kFr, kTa = tabs[gl % 2]
        if True:
            A((lambda g=g, Fr=Fr: lambda e: e.activation(W5(Fr), iidx, AF.Copy, scale=thn8[:, g:g + 1]))(), r=["iidx", "thn8"], w=[kFr])
            G((lambda Fr=Fr: lambda e: e.tensor_copy(FI, W5(Fr)))(), r=[kFr], w=["FI"])
            G((lambda Ta=Ta: lambda e: e.tensor_copy(W5(Ta), FI))(), r=["FI"], w=[kTa])
            G((lambda Fr=Fr, Ta=Ta: lambda e: e.tensor_tensor(W5(Fr), W5(Fr), W5(Ta), ALU.subtract))(), r=[kFr, kTa], w=[kFr])
            A((lambda Ts=Ts, Fr=Fr: lambda e: e.activation(W5(Ts), W5(Fr), AF.Sin, scale=TWO_PI))(), r=[kFr], w=[kTs])
            A((lambda Ta=Ta, Fr=Fr: lambda e: e.activation(W5(Ta), W5(Fr), AF.Abs))(), r=[kFr], w=[kTa])
            A((lambda Tc=Tc, Ta=Ta: lambda e: e.activation(W5(Tc), W5(Ta), AF.Sin, bias=halfpi, scale=-TWO_PI))(), r=[kTa, "halfpi"], w=[kTc])

    def stage_A(j, hook=None, skip_tables=(), after=None):
        un = u_all[:, j, :]
        for gl in range(8):
            if hook is not None:
                hook(gl)
            g = j * 8 + gl
            r4 = gl // 2
            par = gl % 2
            pS = (ps[0], ps[1]) if gl % 2 == 0 else (ps[4], ps[5])
            kS = ("ps0", "ps1") if gl % 2 == 0 else ("ps4", "ps5")
            Tc, Ts, Fr, Ta, kTc, kTs, kFr, kTa = tabs[gl % 2]
            rows = slice(32 * r4, 32 * r4 + 32)
            for k in range(8):
                for ri in range(2):
                    bk = BKT[(par, ri)]
                    T((lambda ri=ri, bk=bk, k=k, rows=rows, r4=r4, un=un, pS=pS: lambda e: e.matmul(
                        pS[ri][0:64, :], bk[rows, k, 0:64], un[rows, k * 512:(k + 1) * 512],
                        start=(k == 0), stop=(k == 7), tile_position=(32 * r4, 0)))(),
                      r=[f"BKT{par}{ri}", f"u{j}"], w=[kS[ri]], inc=False)
                    T((lambda ri=ri, bk=bk, k=k, rows=rows, r4=r4, pS=pS: lambda e: e.matmul(
                        pS[ri][64:128, :], bk[rows, k, 64:128], urev[rows, k * 512:(k + 1) * 512],
                        start=(k == 0), stop=(k == 7), tile_position=(32 * r4, 64)))(),
                      r=[f"BKT{par}{ri}", "urev"], w=[kS[ri]], inc=(k == 7))
            if gl not in skip_tables:
                A((lambda g=g, Fr=Fr: lambda e: e.activation(W5(Fr), iidx, AF.Copy, scale=thn8[:, g:g + 1]))(), r=["iidx", "thn8"], w=[kFr])
                G((lambda Fr=Fr: lambda e: e.tensor_copy(FI, W5(Fr)))(), r=[kFr], w=["FI"])
                G((lambda Ta=Ta: lambda e: e.tensor_copy(W5(Ta), FI))(), r=["FI"], w=[kTa])
                G((lambda Fr=Fr, Ta=Ta: lambda e: e.tensor_tensor(W5(Fr), W5(Fr), W5(Ta), ALU.subtract))(), r=[kFr, kTa], w=[kFr])
                A((lambda Ts=Ts, Fr=Fr: lambda e: e.activation(W5(Ts), W5(Fr), AF.Sin, scale=TWO_PI))(), r=[kFr], w=[kTs])
                A((lambda Ta=Ta, Fr=Fr: lambda e: e.activation(W5(Ta), W5(Fr), AF.Abs))(), r=[kFr], w=[kTa])
                A((lambda Tc=Tc, Ta=Ta: lambda e: e.activation(W5(Tc), W5(Ta), AF.Sin, bias=halfpi, scale=-TWO_PI))(), r=[kTa, "halfpi"], w=[kTc])
            V((lambda pS=pS, Tc=Tc: lambda e: e.tensor_tensor(W5(Xre), pS[0], W5(Tc), ALU.mult))(), r=[kS[0], kTc], w=["F0"])
            V((lambda pS=pS, Ts=Ts: lambda e: e.tensor_tensor(W5(Tb), pS[1], W5(Ts), ALU.mult))(), r=[kS[1], kTs], w=["F8"])
            V(lambda e: e.tensor_tensor(W5(Xre), W5(Xre), W5(Tb), ALU.add), r=["F0", "F8"], w=["F0"])
            V((lambda pS=pS, Tc=Tc: lambda e: e.tensor_tensor(W5(Xim), pS[1], W5(Tc), ALU.mult))(), r=[kS[1], kTc], w=["F1"])
            V((lambda pS=pS, Ts=Ts: lambda e: e.tensor_tensor(W5(Tb), pS[0], W5(Ts), ALU.mult))(), r=[kS[0], kTs], w=["F8"])
            V(lambda e: e.tensor_tensor(W5(Xim), W5(Xim), W5(Tb), ALU.subtract), r=["F1", "F8"], w=["F1"])
            rb = rho8[:, g:g + 1].broadcast_to([128, 512])
            V((lambda rb=rb: lambda e: e.tensor_tensor_scan(W5(Rre), rb, W5(Xre), 0.0, ALU.mult, ALU.add))(), r=["F0", "rho8"], w=["F2"])
            V((lambda rb=rb: lambda e: e.tensor_tensor_scan(W5(Rim), rb, W5(Xim), 0.0, ALU.mult, ALU.add))(), r=["F1", "rho8"], w=["F3"])
            sre = Sst[:, 2 * gl, 1:513]
            sim = Sst[:, 2 * gl + 1, 1:513]
            V((lambda Tc=Tc: lambda e: e.tensor_tensor(W5(Xre), W5(Rre), W5(Tc), ALU.mult))(), r=["F2", kTc], w=["F0"])
            V((lambda Ts=Ts: lambda e: e.tensor_tensor(W5(Tb), W5(Rim), W5(Ts), ALU.mult))(), r=["F3", kTs], w=["F8"])
            V((lambda sre=sre: lambda e: e.tensor_tensor(sre, W5(Xre), W5(Tb), ALU.subtract))(), r=["F0", "F8"], w=["Sst"])
            V((lambda Ts=Ts: lambda e: e.tensor_tensor(W5(Xim), W5(Rre), W5(Ts), ALU.mult))(), r=["F2", kTs], w=["F1"])
            V((lambda Tc=Tc: lambda e: e.tensor_tensor(W5(Tb), W5(Rim), W5(Tc), ALU.mult))(), r=["F3", kTc], w=["F8"])
            V((lambda sim=sim: lambda e: e.tensor_tensor(sim, W5(Xim), W5(Tb), ALU.add))(), r=["F1", "F8"], w=["Sst"])
            if after is not None:
                after(gl)

    def stage_B(j, mid_hook=None):
        un = u_all[:, j, :]
        for (kbase, usrc, ku, TZ, ktz, yacc, kacc) in (
                (64, urev, "urev", TZb, "TZb", yb_acc, "yb_acc"),
                (0, un, f"u{j}", TZf, "TZf", yf_acc, "yf_acc")):
            if kbase == 0 and mid_hook is not None:
                mid_hook()
            for k in range(8):
                py, kpy = ps[2 + k % 2], f"ps{2 + k % 2}"
                for k2 in range(k + 1):
                    T((lambda py=py, TZ=TZ, k=k, k2=k2, usrc=usrc: lambda e: e.matmul(
                        py, TZ[:, k - k2, :], usrc[:, k2 * 512:(k2 + 1) * 512], start=(k2 == 0), stop=False))(),
                      r=[ktz, ku], w=[kpy], inc=False)
                order = [(gl, ri) for gg in range(2) for ri in range(2) for gl in range(gg, 8, 2)]
                for n_, (gl, ri) in enumerate(order):
                    r4 = gl // 2
                    dz = DCz_re if ri == 0 else DCz_imn
                    T((lambda py=py, dz=dz, gl=gl, ri=ri, k=k, kbase=kbase, r4=r4, last=(n_ >= len(order) - 4): lambda e: e.matmul(
                        py[32 * r4:32 * r4 + 32, :], dz[kbase:kbase + 64, gl, k, :], Sst[kbase:kbase + 64, 2 * gl + ri, 0:512],
                        start=False, stop=last, tile_position=(kbase, 32 * r4)))(),
                      r=["DCz_re", "DCz_imn", "Sst"], w=[kpy], inc=(n_ == len(order) - 1))
                A((lambda py=py, yacc=yacc, k=k: lambda e: e.activation(yacc[:, k * 512:(k + 1) * 512], py, AF.Copy))(), r=[kpy], w=[kacc])
    def final_piece(j, hh, rotate=True):
        un = u_all[:, j, :]
        (pstep, _), (fstep, _) = yb_acc.ap
        if True:
            yfv = yf_acc.rearrange("p (k c) -> p k c", k=8)[:, :, 64 * hh:64 * hh + 64]
            ybv = AP(yb_acc.tensor, yb_acc.offset + fstep * (L - 1 - 64 * hh), [[pstep, 128], [-512 * fstep, 8], [-fstep, 64]])
            fb_list = [(W5(F[0]), "F0"), (W5(F[8]), "F8"), (W5(F[1]), "F1"), (W5(F[2]), "F2"), (W5(F[3]), "F3"),
                       (Hall[:, 0:1024].bitcast(F32), "HF"), (W5(F[2]), "F2"), (W5(F[3]), "F3")]
            fbuf, kfb = fb_list[hh] if rotate else fb_list[5]
            f0v = fbuf.rearrange("p (k c) -> p k c", k=8)
            V((lambda yfv=yfv, ybv=ybv, f0v=f0v: lambda e: e.tensor_tensor(f0v, yfv, ybv, ALU.add))(),
              r=["yf_acc", "yb_acc"], w=[kfb])
            A((lambda hh=hh, un=un, f0v=f0v: lambda e: e.activation(
                un[:, hh * 512:(hh + 1) * 512].rearrange("p (c k) -> p k c", k=8), f0v, AF.Gelu_apprx_tanh))(), r=[kfb], w=[f"u{j}"])


    tab_xb(0); do_urev(0); tab_bkt(0)
    gen_tables(0, 0); gen_tables(0, 1)
    for j in range(8):
        tab_tz(j)
        dcz_ops = captured(tab_dcz, j)
        assert len(dcz_ops) == 8

        def after(gl, dcz_ops=dcz_ops, j=j):
            dcz_ops[gl]()
            if j > 0:
                final_piece(j - 1, gl, rotate=False)
        stage_A(j, None, skip_tables=(0, 1), after=after)
        if j + 1 < 8:
            tab_xb(j + 1)
            tab_bkt(j + 1)
            gen_tables(j + 1, 0); gen_tables(j + 1, 1)
            stage_B(j, mid_hook=(lambda j=j: do_urev(j + 1)))
        else:
            stage_B(j)
    for hh in range(NT):
        final_piece(7, hh)

    if debug:
        dbg = nc.dram_tensor("dbg", [128, 8 * L], F32, kind="ExternalOutput").ap()
        dbg2 = nc.dram_tensor("dbg2", [128, 2 * L], F32, kind="ExternalOutput").ap()
        P.barrier()
        P.op("gpsimd", lambda e: e.dma_start(out=dbg, in_=u_all.rearrange("p a b -> p (a b)")), reads=[f"u{k}" for k in range(8)], dma_key="dbg", final=True)
        P.op("gpsimd", lambda e: e.dma_start(out=dbg2[:, 0:L], in_=yf_acc), reads=["yf_acc"], dma_key="dbg2", final=True)
        P.op("gpsimd", lambda e: e.dma_start(out=dbg2[:, L:2 * L], in_=yb_acc), reads=["yb_acc"], dma_key="dbg2", final=True)
        P.emit()
        return nc
    P.barrier()
    load(g_bc, g_bc_d, "g_bc"); load(fg_bc, fg_bc_d, "fg_bc")
    for pr4 in range(4):
        for hf in range(2):
            nb = 2 * pr4 + hf
            P.op("sync", (lambda nb=nb, pr4=pr4, hf=hf: lambda e: e.dma_start(out=Wout[:, :, nb * 128:(nb + 1) * 128],
                 in_=wsc[40 + pr4][:, hf * 1024:(hf + 1) * 1024].rearrange("p (a b) -> p a b", b=128)))(),
                 reads=[f"wsc{40 + pr4}_{hf}"], writes=["Wout"], dma_key=f"Wout{nb}")
    wcnt = [0]

    def wloadp(pr):
        slot = wcnt[0] % 4
        wcnt[0] += 1
        key = f"wslot{slot}"
        P.op("sync", (lambda slot=slot, pr=pr: lambda e: e.dma_start(out=wslot[slot], in_=wsc[pr].rearrange("p (a b) -> p a b", b=128)))(),
             reads=[f"wsc{pr}_0", f"wsc{pr}_1"], writes=[key], dma_key=key)
        return wslot[slot], key

    def proj(pz, kz, wk, hf, rhs_fn, rkeys, n=512, also_halo=None, halo_src=None):
        wt, kw = wk
        for kc in range(8):
            T((lambda kc=kc, wt=wt: lambda e: e.matmul(pz[:, 0:n], wt[:, hf * 8 + kc, :], rhs_fn(kc), start=(kc == 0), stop=(kc == 7)))(),
              r=[kw] + rkeys, w=[kz], inc=(kc == 7))
        if also_halo is not None:
            hz, c0 = also_halo
            for kc in range(8):
                T((lambda kc=kc, wt=wt, hs_=halo_src[0]: lambda e: e.matmul(hz[:, c0:c0 + 2], wt[:, hf * 8 + kc, :], hs_[:, kc, :], start=(kc == 0), stop=(kc == 7)))(),
                  r=[kw, halo_src[1]], w=["psH"], inc=(kc == 7))

    cv = F[0]
    cvh = cols_strided(cv, 0, 513, 2)
    xnh2 = H[5][:, 0:16].rearrange("p (a b) -> p a b", b=2)
    xbufs = [(xnT, xnh, "xnT", "xnh"), (xnT2, xnh2, "xnT2", "xnh2")]
    make_xn(0, True, *xbufs[0])
    pend = [None]
    for tt in range(NT):
        tsl = slice(tt * 512, (tt + 1) * 512)
        cxT, cxh, kxT, kxh = xbufs[tt % 2]
        xr = (lambda cxT: lambda kc: cxT[:, kc, :])(cxT)
        for f in range(8):
            yg = lambda kc, tsl=tsl: u_all[:, kc, tsl]
            wk = wloadp(3 * f)
            proj(ps[0], "ps0", wk, 0, xr, [kxT])
            A(lambda e: e.activation(H[2], ps[0], AF.Silu), r=["ps0"], w=["H2"])
            proj(ps[1], "ps1", wk, 1, yg, [f"u{k}" for k in range(8)])
            A(lambda e: e.activation(H[3], ps[1], AF.Sigmoid), r=["ps1"], w=["H3"])
            V((lambda f=f, tsl=tsl: lambda e: e.tensor_tensor(H[4], u_all[:, f, tsl], H[3], ALU.mult))(), r=[f"u{f}", "H3"], w=["H4"])
            V((lambda f=f: lambda e: e.tensor_tensor(ya[:, f, :], H[4], H[2], ALU.mult))(), r=["H4", "H2"], w=["ya"])
            wk = wloadp(3 * f + 1)
            proj(ps[2], "ps2", wk, 0, xr, [kxT], also_halo=(ps[6], 0), halo_src=(cxh, kxh))
            proj(ps[3], "ps3", wk, 1, xr, [kxT], also_halo=(ps[6], 2), halo_src=(cxh, kxh))
            A(lambda e: e.activation(F[1][:, 0:512], ps[2], AF.Copy), r=["ps2"], w=["F1"])
            A(lambda e: e.activation(F[2][:, 0:2], ps[6][:, 0:2], AF.Copy), r=["psH"], w=["F2"])
            V(lambda e: e.tensor_tensor(cv[:, 1:513], ps[3], F[1][:, 0:512], ALU.mult), r=["ps3", "F1"], w=["F0"])
            V(lambda e: e.tensor_tensor(cvh, ps[6][:, 2:4], F[2][:, 0:2], ALU.mult), r=["psH", "F2"], w=["F0"])
            V((lambda f=f: lambda e: e.tensor_scalar(F[3][:, 0:512], cv[:, 0:512], cw[:, f * 3:f * 3 + 1], cb[:, f:f + 1], ALU.mult, ALU.add))(),
              r=["F0", "cw", "cb"], w=["F3"])
            V((lambda f=f: lambda e: e.scalar_tensor_tensor(F[4][:, 0:512], cv[:, 1:513], cw[:, f * 3 + 1:f * 3 + 2], F[3][:, 0:512], ALU.mult, ALU.add))(),
              r=["F0", "F3"], w=["F4"])
            V((lambda f=f: lambda e: e.scalar_tensor_tensor(F[3][:, 0:512], cv[:, 2:514], cw[:, f * 3 + 2:f * 3 + 3], F[4][:, 0:512], ALU.mult, ALU.add))(),
              r=["F0", "F4"], w=["F3"])
            wk = wloadp(3 * f + 2)
            proj(ps[4], "ps4", wk, 0, xr, [kxT])
            A(lambda e: e.activation(F[6][:, 0:512], ps[4], AF.Copy), r=["ps4"], w=["F6"])
            V(lambda e: e.tensor_tensor(F[4][:, 0:512], F[6][:, 0:512], F[3][:, 0:512], ALU.mult), r=["F6", "F3"], w=["F4"])
            proj(ps[5], "ps5", wk, 1, xr, [kxT])
            A(lambda e: e.activation(F[5][:, 0:512], ps[5], AF.Silu), r=["ps5"], w=["F5"])
            V((lambda f=f: lambda e: e.tensor_tensor(yb[:, f, :], F[4][:, 0:512], F[5][:, 0:512], ALU.mult))(), r=["F4", "F5"], w=["yb"])
            if tt + 1 < NT and f % 2 == 0 and f > 0 and pend[0] is not None:
                pend[0]()
                pend[0] = None
            if tt + 1 < NT and f % 2 == 1:
                nxT, _, nkT, _ = xbufs[(tt + 1) % 2]
                pend[0] = make_xn_block(tt + 1, f // 2, nxT, nkT, defer=True)
        if pend[0] is not None:
            pend[0]()
            pend[0] = None
        halo_post = None
        if tt + 1 < NT:
            halo_post = make_xn(tt + 1, True, *xbufs[(tt + 1) % 2], blocks=False, defer_halo=True)
        xslots = {}
        for s4 in range(2):
            slot = xcnt[0] % 2
            xcnt[0] += 1
            xslots[s4] = slot
            load(xt[slot], x[tt * 512 + s4 * 128:tt * 512 + s4 * 128 + 128, :], f"xt{slot}")
        for dch in range(8):
            wk = wloadp(24 + 2 * dch)
            proj(ps[0], "ps0", wk, 0, lambda kc: ya[:, kc, :], ["ya"])
            proj(ps[1], "ps1", wk, 1, xr, [kxT])
            A(lambda e: e.activation(F[1][:, 0:512], ps[1], AF.Sigmoid), r=["ps1"], w=["F1"])
            V(lambda e: e.tensor_tensor(F[2][:, 0:512], ps[0], F[1][:, 0:512], ALU.mult), r=["ps0", "F1"], w=["F2"])
            wk = wloadp(25 + 2 * dch)
            proj(ps[2], "ps2", wk, 0, lambda kc: yb[:, kc, :], ["yb"])
            proj(ps[3], "ps3", wk, 1, xr, [kxT])
            A(lambda e: e.activation(F[3][:, 0:512], ps[3], AF.Sigmoid), r=["ps3"], w=["F3"])
            V(lambda e: e.tensor_tensor(F[4][:, 0:512], ps[2], F[3][:, 0:512], ALU.mult), r=["ps2", "F3"], w=["F4"])
            V((lambda dch=dch: lambda e: e.tensor_tensor(mg[:, dch, :], F[2][:, 0:512], F[4][:, 0:512], ALU.add))(), r=["F2", "F4"], w=["mg"])
        if halo_post is not None:
            halo_post()
        for s4 in range(4):
            r0 = tt * 512 + s4 * 128
            if s4 in xslots:
                slot = xslots[s4]
            else:
                slot = xcnt[0] % 2
                xcnt[0] += 1
                load(xt[slot], x[r0:r0 + 128, :], f"xt{slot}")
            kx = f"xt{slot}"
            for half in range(2):
                bi = (4 + half) if s4 % 2 == 0 else (0 + half)
                pz = ps[bi]
                for dch in range(8):
                    T((lambda pz=pz, dch=dch, s4=s4, half=half: lambda e: e.matmul(pz, mg[:, dch, s4 * 128:(s4 + 1) * 128], Wout[:, dch, half * 512:(half + 1) * 512], start=(dch == 0), stop=(dch == 7)))(),
                      r=["mg", "Wout"], w=[f"ps{bi}"], inc=(dch == 7))
                V((lambda pz=pz, half=half, slot=slot: lambda e: e.tensor_tensor(hbuf[:, half * 512:(half + 1) * 512], pz, xt[slot][:, half * 512:(half + 1) * 512], ALU.add))(),
                  r=[f"ps{bi}", kx], w=["hbuf"])
            A(lambda e: e.activation(F[6][:, 0:512], hbuf[:, 0:512], AF.Square, accum_out=ssq), r=["hbuf"], w=["ssq", "F6"])
            A(lambda e: e.activation(F[6][:, 0:512], hbuf[:, 512:1024], AF.Square, accum_out=std), r=["hbuf"], w=["std", "F6"])
            V(lambda e: e.tensor_tensor(ssq, ssq, std, ALU.add), r=["ssq", "std"], w=["ssq"])
            A(lambda e: e.activation(std, ssq, AF.Sqrt, bias=epsb, scale=1.0 / D), r=["ssq", "epsb"], w=["std"])
            V(lambda e: e.reciprocal(rstd, std), r=["std"], w=["rstd"])
            V(lambda e: e.scalar_tensor_tensor(otile, hbuf, rstd, fg_bc, ALU.mult, ALU.mult), r=["hbuf", "rstd", "fg_bc"], w=["otile"])
            P.op("sync", (lambda r0=r0: lambda e: e.dma_start(out=out[r0:r0 + 128, :], in_=otile))(), reads=["otile"], dma_key="outst", final=True)

    P.emit()
    return nc


_CACHE = {}


def _prep_shared(inp):
    f32 = np.float32
    sh = {}
    sh["w_in"] = np.ascontiguousarray(inp["w_in"][0], dtype=f32)
    sh["w_glu"] = np.ascontiguousarray(inp["w_glu"][0], dtype=f32)
    sh["w_a"] = np.ascontiguousarray(inp["w_branch_a"][0], dtype=f32)
    sh["w_b"] = np.ascontiguousarray(inp["w_branch_b"][0], dtype=f32)
    sh["w_out"] = np.ascontiguousarray(inp["w_out"][0], dtype=f32)
    sh["g_bc"] = np.ascontiguousarray(np.broadcast_to(inp["norm_g"][0][None, :], (128, D)), dtype=f32)
    sh["fg_bc"] = np.ascontiguousarray(np.broadcast_to(inp["final_g"][None, :], (128, D)), dtype=f32)
    sh["dcol"] = np.ascontiguousarray(inp["ssm_d"][0].reshape(8, 128).T, dtype=f32)
    sh["cw"] = np.ascontiguousarray(inp["conv_w"][0].reshape(3, 8, 128).transpose(2, 1, 0).reshape(128, 24), dtype=f32)
    sh["cb"] = np.ascontiguousarray(inp["conv_b"][0].reshape(8, 128).T, dtype=f32)
    sh["lre"] = np.ascontiguousarray(inp["lam_re"][0].transpose(0, 2, 1).reshape(128, 64), dtype=f32)
    sh["lim"] = np.ascontiguousarray(inp["lam_im"][0].transpose(0, 2, 1).reshape(128, 64), dtype=f32)
    sh["ldt"] = np.ascontiguousarray(np.broadcast_to(inp["log_dt"][0][:, None, :], (2, 64, 64)).reshape(128, 64), dtype=f32)
    for nm, key in (("b_re", "ssm_b_re"), ("b_im", "ssm_b_im")):
        b = np.asarray(inp[key][0], dtype=f32)
        sh[nm] = np.ascontiguousarray(b.transpose(0, 2, 1, 3).reshape(128, 1024))
    for nm, key in (("c_re", "ssm_c_re"), ("c_im", "ssm_c_im")):
        c = np.asarray(inp[key][0], dtype=f32)
        sh[nm] = np.ascontiguousarray(c.transpose(0, 3, 1, 2).reshape(128, 1024))
    sh["ident"] = np.eye(128, dtype=f32)
    sh["iidx"] = np.ascontiguousarray(np.broadcast_to(np.arange(512, dtype=f32)[None, :], (128, 512)))
    sh["jR"] = np.ascontiguousarray(np.broadcast_to(np.tile(np.arange(7, -1, -1).astype(f32), 64)[None, :], (128, 512)))
    sh["jD"] = np.ascontiguousarray(np.broadcast_to(np.tile(np.arange(1, 9).astype(f32), 64)[None, :], (128, 512)))
    bd = np.zeros((128, 128), dtype=f32)
    for i in range(8):
        bd[i * 16:(i + 1) * 16, i * 16:(i + 1) * 16] = 1.0
    sh["bdmask"] = bd
    pm = np.zeros((128, 2), dtype=f32)
    for p in range(128):
        pm[p, (p // 16) % 2] = 1.0
    sh["parmask"] = pm
    return sh


def kernel(**inputs):
    inp = {k: np.asarray(v) for k, v in inputs.items()}
    if "nc" not in _CACHE:
        _CACHE["nc"] = build_program()
    nc = _CACHE["nc"]
    sh = _prep_shared(inp)
    x = np.asarray(inp["x"], dtype=np.float32)
    in_maps = []
    for c in range(8):
        m = dict(sh)
        m["x"] = np.ascontiguousarray(x[c])
        in_maps.append(m)
    res = run_bass_kernel_spmd(nc, in_maps, core_ids=list(range(8)))
    return np.stack([np.asarray(r["out"], dtype=np.float32) for r in res.results], axis=0)
```

```python
import math
import numpy as np
import concourse.bass as bass
import concourse.mybir as mybir
from concourse.bass_utils import run_bass_kernel_spmd
from concourse.ap import AP

F32 = mybir.dt.float32
BF16 = mybir.dt.bfloat16
I32 = mybir.dt.int32
AF = mybir.ActivationFunctionType
ALU = mybir.AluOpType

ENGS = ("tensor", "vector", "scalar", "gpsimd", "sync")
L = 4096
D = 1024
NT = 8
EPS = 1e-6


class Prog:
    def __init__(self, nc):
        self.nc = nc
        self.ops = {e: [] for e in ENGS}
        self.esem = {e: nc.alloc_semaphore(name=f"es_{e}") for e in ENGS}
        self.ecnt = {e: 0 for e in ENGS}
        self.dsem = {}
        self.last_w = {}
        self.readers = {}
        self.waited = {e: {} for e in ENGS}
        self.final_tokens = []

    def _need(self, eng, tok, waits):
        if tok is None:
            return
        sem, val, weng = tok
        if weng == eng and eng == "tensor":
            return
        key = id(sem)
        if self.waited[eng].get(key, 0) >= val:
            return
        self.waited[eng][key] = val
        waits.append((sem, val))

    def barrier(self):
        toks = [(self.esem[e], self.ecnt[e], e) for e in ENGS if self.ecnt[e] > 0]
        toks += [(ent[0], ent[1], "dma") for ent in self.dsem.values()]
        self.pending = {e: list(toks) for e in ENGS}

    def op(self, eng, fn, reads=(), writes=(), dma_key=None, final=False, inc=True):
        waits = []
        for t in getattr(self, "pending", {}).pop(eng, []):
            if t[2] != eng:
                self._need(eng, t, waits)
        for r in reads:
            self._need(eng, self.last_w.get(r), waits)
        for w in writes:
            self._need(eng, self.last_w.get(w), waits)
            for t in self.readers.get(w, ()):
                self._need(eng, t, waits)
        if dma_key is not None:
            if dma_key not in self.dsem:
                self.dsem[dma_key] = [self.nc.alloc_semaphore(name=f"ds_{len(self.dsem)}"), 0]
            ent = self.dsem[dma_key]
            ent[1] += 16
            tok = (ent[0], ent[1], "dma")
            inc = (ent[0], 16)
        elif not inc:
            tok = (self.esem[eng], self.ecnt[eng] + 1, eng)
            inc = None
        else:
            self.ecnt[eng] += 1
            tok = (self.esem[eng], self.ecnt[eng], eng)
            inc = (self.esem[eng], 1)
        for r in reads:
            self.readers.setdefault(r, []).append(tok)
        for w in writes:
            self.last_w[w] = tok
            self.readers[w] = []
        self.ops[eng].append((fn, waits, inc))
        if final:
            self.final_tokens.append(tok)
        return tok

    def emit(self):
        nc = self.nc
        finals = self.final_tokens
        with nc.Block() as block:
            def run(engname, e):
                for fn, waits, inc in self.ops[engname]:
                    for sem, val in waits:
                        e.wait_ge(sem, val)
                    ins = fn(e)
                    if inc is not None:
                        ins.then_inc(inc[0], inc[1])
                if engname == "sync":
                    for sem, val, _ in finals:
                        e.wait_ge(sem, val)

            @block.tensor
            def _(e):
                run("tensor", e)

            @block.vector
            def _(e):
                run("vector", e)

            @block.scalar
            def _(e):
                run("scalar", e)

            @block.gpsimd
            def _(e):
                run("gpsimd", e)

            @block.sync
            def _(e):
                run("sync", e)


def rev_last(ap2d):
    (ps, pc), (fs, fc) = ap2d.ap
    return AP(ap2d.tensor, ap2d.offset + fs * (fc - 1), [[ps, pc], [-fs, fc]])


def cols_strided(ap2d, start, step, n):
    (ps, pc), (fs, fc) = ap2d.ap
    return AP(ap2d.tensor, ap2d.offset + fs * start, [[ps, pc], [fs * step, n]])


def build_program(debug=False):
    nc = bass.Bass("TRN2", target_bir_lowering=False)

    def din(name, shape, dt=F32):
        return nc.dram_tensor(name, list(shape), dt, kind="ExternalInput").ap()

    x = din("x", [L, D])
    w_in = din("w_in", [D, 8 * D])
    w_glu = din("w_glu", [D, D]); w_a = din("w_a", [D, D]); w_b = din("w_b", [D, D]); w_out = din("w_out", [D, D])
    g_bc_d = din("g_bc", [128, D]); fg_bc_d = din("fg_bc", [128, D])
    dcol_d = din("dcol", [128, 8]); cw_d = din("cw", [128, 24]); cb_d = din("cb", [128, 8])
    lre_d = din("lre", [128, 64]); lim_d = din("lim", [128, 64]); ldt_d = din("ldt", [128, 64])
    bre_d = din("b_re", [128, 1024]); bim_d = din("b_im", [128, 1024])
    cre_d = din("c_re", [128, 1024]); cim_d = din("c_im", [128, 1024])
    ident_d = din("ident", [128, 128]); iidx_d = din("iidx", [128, 512])
    jR_d = din("jR", [128, 512]); jD_d = din("jD", [128, 512])
    bdm_d = din("bdmask", [128, 128]); pm_d = din("parmask", [128, 2])
    out = nc.dram_tensor("out", [L, D], F32, kind="ExternalOutput").ap()
    wsc = nc.dram_tensor("wsc", [44, 128, 2048], BF16, kind="Internal").ap()

    P = Prog(nc)

    def sb(name, shape, dt=F32):
        return nc.alloc_sbuf_tensor(name, list(shape), dt).ap()

    u_all = sb("u_all", [128, 8, L], BF16)
    xnT = sb("xnT", [128, 8, 512], BF16)
    xnh = sb("xnh", [128, 8, 2], BF16)
    identb = sb("identb", [128, 128], BF16); identf = sb("identf", [128, 128])
    bdmask = sb("bdmask_s", [128, 128]); parmask = sb("parmask_s", [128, 2])
    iidx = sb("iidx_s", [128, 512])
    dcol = sb("dcol_s", [128, 8]); cw = sb("cw_s", [128, 24]); cb = sb("cb_s", [128, 8])
    thn = sb("thn", [128, 64]); thn8 = sb("thn8", [128, 64]); rho8 = sb("rho8", [128, 64])
    coef_re = sb("coef_re", [128, 64]); coef_im = sb("coef_im", [128, 64])
    sm = [sb(f"sm{i}", [128, 64]) for i in range(8)]
    ssq = sb("ssq", [128, 1]); std = sb("std", [128, 1]); rstd = sb("rstd", [128, 1])
    epsb = sb("epsb", [128, 1]); halfpi = sb("halfpi", [128, 1])
    F = [sb(f"F{i}", [128, 514]) for i in range(9)]
    FI = sb("FI", [128, 512], I32)
    smi = FI
    Hall = sb("Hall", [128, 6 * 512], BF16)
    H = [Hall[:, i * 512:(i + 1) * 512] for i in range(6)]
    ARENA = 49664
    arena = sb("arena", [128, ARENA], BF16)

    def carve(off_el, shape, dt=BF16):
        n = int(np.prod(shape[1:]))
        if dt == F32:
            assert off_el + 2 * n <= ARENA
            a = arena[:, off_el:off_el + 2 * n].bitcast(F32)
        else:
            assert off_el + n <= ARENA
            a = arena[:, off_el:off_el + n]
        if len(shape) == 3:
            a = a.rearrange("p (a b) -> p a b", b=shape[2])
        if len(shape) == 4:
            a = a.rearrange("p (a b c) -> p a b c", b=shape[2], c=shape[3])
        return a

    xt = [carve(40960 + 2048 * i, [128, D], F32) for i in range(2)]
    xs = carve(45056, [128, D])
    g_bc = carve(46080, [128, D], F32)
    xh = carve(48128, [128, 512], F32)
    Wu = carve(0, [128, 64, 128])
    Braw_re = carve(0, [128, 1024], F32); Braw_im = carve(2048, [128, 1024], F32)
    Cp_re = carve(4096, [128, 1024], F32); Cp_im = carve(6144, [128, 1024], F32)
    pwR_re = carve(8192, [128, 512], F32); pwR_im = carve(9216, [128, 512], F32)
    pwD_re = carve(10240, [128, 512], F32); pwD_im = carve(11264, [128, 512], F32)
    T1 = carve(12288, [128, 1024], F32); T2 = carve(14336, [128, 1024], F32)
    T3 = carve(16384, [128, 1024], F32); T4 = carve(18432, [128, 1024], F32)
    urev = carve(20480, [128, L])
    yf_acc = carve(24576, [128, L]); yb_acc = carve(28672, [128, L])
    BKT = {(par, ri): carve(32768 + 1024 * (2 * par + ri), [128, 8, 128]) for par in range(2) for ri in range(2)}
    DCz_re = carve(36864, [128, 8, 8, 32]); DCz_imn = carve(38912, [128, 8, 8, 32])
    Sst = carve(40960, [128, 16, 514])
    Wout = carve(0, [128, 8, 1024])
    ya = carve(8192, [128, 8, 512]); yb = carve(12288, [128, 8, 512]); mg = carve(16384, [128, 8, 512])
    wslot = [carve(20480 + 2048 * i, [128, 16, 128]) for i in range(2)] + [carve(36864 + 2048 * i, [128, 16, 128]) for i in range(2)]
    xnT2 = carve(32768, [128, 8, 512])
    otile = carve(24576, [128, D], F32); hbuf = carve(26624, [128, D], F32)
    fg_bc = carve(28672, [128, D], F32)
    xh = carve(30720, [128, D], F32)[0:2, :]
    wstage = [carve(20480 + 1024 * i, [128, 8, 128]) for i in range(6)]
    wstage_f = [carve(8192 + 2048 * i, [128, 8, 128], F32) for i in range(6)]
    TZf = Hall[:, 2 * 512:4 * 512].rearrange("p (a b) -> p a b", b=128)
    TZb = Hall[:, 4 * 512:6 * 512].rearrange("p (a b) -> p a b", b=128)

    ps = [nc.alloc_psum_tensor(f"ps{i}", [128, 512], F32).ap() for i in range(7)]
    psT = nc.alloc_psum_tensor("psT", [128, 8, 128], BF16).ap()

    TWO_PI = 2.0 * math.pi

    cap = [None]

    def _op(eng, fn, r, w, inc=True):
        if cap[0] is not None:
            cap[0].append(lambda: P.op(eng, fn, reads=r, writes=w, inc=inc))
            return None
        return P.op(eng, fn, reads=r, writes=w, inc=inc)

    def captured(f, *args):
        cap[0] = []
        f(*args)
        ops = cap[0]
        cap[0] = None
        return ops

    def V(fn, r=(), w=()):
        return _op("vector", fn, r, w)

    def A(fn, r=(), w=()):
        return _op("scalar", fn, r, w)

    def G(fn, r=(), w=()):
        return _op("gpsimd", fn, r, w)

    def T(fn, r=(), w=(), inc=True):
        return _op("tensor", fn, r, w, inc)

    def load(dst, src, key, eng="sync", extra_w=()):
        return P.op(eng, lambda e: e.dma_start(out=dst, in_=src), writes=[key] + list(extra_w), dma_key=key)

    load(g_bc, g_bc_d, "g_bc")
    load(identb, ident_d, "identb", eng="gpsimd"); load(identf, ident_d, "identf")
    load(bdmask, bdm_d, "bdmask"); load(parmask, pm_d, "parmask")
    load(iidx, iidx_d, "iidx")
    load(dcol, dcol_d, "dcol"); load(cw, cw_d, "cw"); load(cb, cb_d, "cb")
    V(lambda e: e.memset(epsb, EPS), w=["epsb"])
    V(lambda e: e.memset(halfpi, math.pi / 2), w=["halfpi"])
    w_in_v = w_in.rearrange("(kc p) n -> p kc n", p=128)
    srcs = [("Wu", w_in_v, i, None) for i in range(8)]
    w_glu_v = w_glu.rearrange("(kc p) n -> p kc n", p=128); w_a_v = w_a.rearrange("(kc p) n -> p kc n", p=128)
    w_b_v = w_b.rearrange("(kc p) n -> p kc n", p=128); w_out_v = w_out.rearrange("(kc p) n -> p kc n", p=128)
    for f in range(8):
        srcs += [("sc", w_in_v, 8 + f, (3 * f, 0)), ("sc", w_glu_v, f, (3 * f, 1)),
                 ("sc", w_in_v, 16 + f, (3 * f + 1, 0)), ("sc", w_in_v, 32 + f, (3 * f + 1, 1)),
                 ("sc", w_in_v, 24 + f, (3 * f + 2, 0)), ("sc", w_in_v, 40 + f, (3 * f + 2, 1))]
    for dch in range(8):
        srcs += [("sc", w_a_v, dch, (24 + 2 * dch, 0)), ("sc", w_in_v, 48 + dch, (24 + 2 * dch, 1)),
                 ("sc", w_b_v, dch, (25 + 2 * dch, 0)), ("sc", w_in_v, 56 + dch, (25 + 2 * dch, 1))]
    for nb in range(8):
        srcs.append(("sc", w_out_v, nb, (40 + nb // 2, nb % 2)))

    def conv_chunk(n_):
        kind, wv, cbk, dst = srcs[n_]
        sl = n_ % 6
        sf = wstage_f[sl]
        kf = f"wstf{sl}"
        P.op("sync", (lambda sf=sf, wv=wv, cbk=cbk: lambda e: e.dma_start(out=sf, in_=wv[:, :, cbk * 128:(cbk + 1) * 128]))(),
             writes=[kf], dma_key=kf)
        if kind == "Wu":
            V((lambda sf=sf, cbk=cbk: lambda e: e.tensor_copy(Wu[:, cbk * 8:(cbk + 1) * 8, :], sf))(), r=[kf], w=[f"Wu{cbk}"])
        else:
            st = wstage[sl]
            kb = f"wstb{sl}"
            pr, hf = dst
            (G if n_ % 4 == 3 else V)((lambda sf=sf, st=st: lambda e: e.tensor_copy(st, sf))(), r=[kf], w=[kb])
            P.op("gpsimd", (lambda st=st, pr=pr, hf=hf: lambda e: e.dma_start(
                out=wsc[pr][:, hf * 1024:(hf + 1) * 1024].rearrange("p (a b) -> p a b", b=128), in_=st))(),
                 reads=[kb], writes=[f"wsc{pr}_{hf}"], dma_key=kb + "_st")

    for n_ in range(8):
        conv_chunk(n_)

    xcnt = [0]

    def norm_rows(src_rows_ap, nrows, dst_xnT_ap, dst_key="xnT", defer=False):
        slot = xcnt[0] % 2
        xcnt[0] += 1
        xb = xt[slot][0:nrows, :]
        kx = f"xt{slot}"
        load(xb, src_rows_ap, kx)
        return finish_norm(xb, kx, nrows, dst_xnT_ap, dst_key=dst_key, defer=defer)

    def finish_norm(xb, kx, nrows, dst, dst_key="xnT", defer=False):
        A(lambda e: e.activation(F[7][0:nrows, 0:512], xb[:, 0:512], AF.Square, accum_out=ssq[0:nrows, :]),
          r=[kx], w=["ssq", "F7"])
        A(lambda e: e.activation(F[7][0:nrows, 0:512], xb[:, 512:1024], AF.Square, accum_out=std[0:nrows, :]),
          r=[kx], w=["std", "F7"])
        V(lambda e: e.tensor_tensor(ssq[0:nrows, :], ssq[0:nrows, :], std[0:nrows, :], ALU.add), r=["ssq", "std"], w=["ssq"])
        A(lambda e: e.activation(std[0:nrows, :], ssq[0:nrows, :], AF.Sqrt, bias=epsb[0:nrows, :], scale=1.0 / D),
          r=["ssq", "epsb"], w=["std"])
        V(lambda e: e.reciprocal(rstd[0:nrows, :], std[0:nrows, :]), r=["std"], w=["rstd"])
        V(lambda e: e.scalar_tensor_tensor(xs[0:nrows, :], xb, rstd[0:nrows, :], g_bc[0:nrows, :], ALU.mult, ALU.mult),
          r=[kx, "rstd", "g_bc"], w=["xs"])

        def post():
            for kc in range(8):
                T((lambda kc=kc: lambda e: e.transpose(psT[:, kc, 0:nrows], xs[0:nrows, kc * 128:(kc + 1) * 128], identb[0:nrows, 0:nrows]))(),
                  r=["xs", "identb"], w=["psT"], inc=(kc == 7))
            A(lambda e: e.activation(dst, psT[:, :, 0:nrows], AF.Copy), r=["psT"], w=[dst_key])
        if defer:
            return post
        post()
        return None

    def make_xn_block(tt, s, dstT, kT, defer=False):
        r0 = tt * 512 + s * 128
        return norm_rows(x[r0:r0 + 128, :], 128, dstT[:, :, s * 128:(s + 1) * 128], dst_key=kT, defer=defer)

    def make_xn(tt, with_halo, dstT=None, dsth=None, kT="xnT", kh="xnh", blocks=True, defer_halo=False):
        dstT = xnT if dstT is None else dstT
        dsth = xnh if dsth is None else dsth
        if blocks:
            for s in range(4):
                make_xn_block(tt, s, dstT, kT)
        if with_halo:
            V(lambda e: e.memset(xh, 0.0), w=["xh"])
            if tt > 0:
                load(xh[0:1, :], x[tt * 512 - 1:tt * 512, :], "xh")
            if tt < NT - 1:
                load(xh[1:2, :], x[(tt + 1) * 512:(tt + 1) * 512 + 1, :], "xh")
            return finish_norm(xh, "xh", 2, dsth, dst_key=kh, defer=defer_halo)
        return None

    for tt in range(NT):
        make_xn(tt, False)
        for n_ in range(8 + 11 * tt, 8 + 11 * (tt + 1)):
            conv_chunk(n_)
        for j in range(8):
            pb = j % 2
            for kc in range(8):
                T((lambda j=j, kc=kc, pb=pb: lambda e: e.matmul(ps[pb], Wu[:, j * 8 + kc, :], xnT[:, kc, :], start=(kc == 0), stop=(kc == 7)))(),
                  r=[f"Wu{j}", "xnT"], w=[f"ps{pb}"], inc=(kc == 7))
            V((lambda j=j, pb=pb, tt=tt: lambda e: e.tensor_copy(
                u_all[:, j, :].rearrange("p (k c) -> p k c", k=8)[:, :, 64 * tt:64 * tt + 64],
                ps[pb].rearrange("p (c k) -> p k c", k=8)))(),
              r=[f"ps{pb}"], w=[f"u{j}"])

    P.barrier()
    lre, lim, dtt, lr, th, t0, t1, t2 = sm
    load(lre, lre_d, "lre"); load(lim, lim_d, "lim"); load(dtt, ldt_d, "ldt")
    load(Braw_re, bre_d, "Braw_re"); load(Braw_im, bim_d, "Braw_im")
    load(T1, cre_d, "T1"); load(T2, cim_d, "T2")
    load(T3[:, 0:512], jR_d, "T3"); load(T4[:, 0:512], jD_d, "T4")
    A(lambda e: e.activation(dtt, dtt, AF.Exp), r=["ldt"], w=["dtt"])
    V(lambda e: e.tensor_tensor(lr, lre, dtt, ALU.mult), r=["lre", "dtt"], w=["lr"])
    V(lambda e: e.tensor_tensor(th, lim, dtt, ALU.mult), r=["lim", "dtt"], w=["th"])
    V(lambda e: e.tensor_scalar(thn, th, 1.0 / TWO_PI, None, ALU.mult), r=["th"], w=["thn"])
    A(lambda e: e.activation(rho8, lr, AF.Exp, scale=8.0), r=["lr"], w=["rho8"])

    def frac_op(dst, src, n, kd, ks):
        V(lambda e: e.tensor_copy(smi[:, 0:n], src), r=[ks], w=["FI"])
        V(lambda e: e.tensor_copy(F[8][:, 0:n], smi[:, 0:n]), r=["FI"], w=["F8"])
        V(lambda e: e.tensor_tensor(dst, src, F[8][:, 0:n], ALU.subtract), r=[ks, "F8"], w=[kd])

    def sincos(sin_dst, cos_dst, frac_src, tmp, ks, kk_s, kk_c, kt):
        A(lambda e: e.activation(sin_dst, frac_src, AF.Sin, scale=TWO_PI), r=[ks], w=[kk_s])
        A(lambda e: e.activation(tmp, frac_src, AF.Abs), r=[ks], w=[kt])
        A(lambda e: e.activation(cos_dst, tmp, AF.Sin, bias=halfpi, scale=-TWO_PI), r=[kt, "halfpi"], w=[kk_c])

    V(lambda e: e.tensor_scalar(thn8, thn, 8.0, None, ALU.mult), r=["thn"], w=["thn8"])
    frac_op(thn8, thn8, 64, "thn8", "thn8")
    rho1 = sm[4]
    A(lambda e: e.activation(rho1, lr, AF.Exp), r=["lr", "thn"], w=["rho1"])
    frac_op(t0, thn, 64, "t0", "thn")
    sincos(t1, t2, t0, F[6][:, 0:64], "t0", "t1", "t2", "F6")
    V(lambda e: e.tensor_tensor(t1, t1, rho1, ALU.mult), r=["t1", "rho1"], w=["t1"])
    V(lambda e: e.tensor_tensor(t2, t2, rho1, ALU.mult), r=["t2", "rho1"], w=["t2"])
    V(lambda e: e.tensor_scalar(t2, t2, -1.0, None, ALU.add), r=["t2"], w=["t2"])
    V(lambda e: e.tensor_tensor(t0, lre, lre, ALU.mult), r=["lre"], w=["t0"])
    V(lambda e: e.tensor_tensor(dtt, lim, lim, ALU.mult), r=["lim"], w=["dtt"])
    V(lambda e: e.tensor_tensor(t0, t0, dtt, ALU.add), r=["t0", "dtt"], w=["t0"])
    V(lambda e: e.reciprocal(dtt, t0), r=["t0"], w=["dtt"])
    V(lambda e: e.tensor_tensor(t0, t2, lre, ALU.mult), r=["t2", "lre"], w=["t0"])
    V(lambda e: e.tensor_tensor(rho1, t1, lim, ALU.mult), r=["t1", "lim"], w=["rho1"])
    V(lambda e: e.tensor_tensor(t0, t0, rho1, ALU.add), r=["t0", "rho1"], w=["t0"])
    V(lambda e: e.tensor_tensor(coef_re, t0, dtt, ALU.mult), r=["t0", "dtt"], w=["coef_re"])
    V(lambda e: e.tensor_tensor(t0, t1, lre, ALU.mult), r=["t1", "lre"], w=["t0"])
    V(lambda e: e.tensor_tensor(rho1, t2, lim, ALU.mult), r=["t2", "lim"], w=["rho1"])
    V(lambda e: e.tensor_tensor(t0, t0, rho1, ALU.subtract), r=["t0", "rho1"], w=["t0"])
    V(lambda e: e.tensor_tensor(coef_im, t0, dtt, ALU.mult), r=["t0", "dtt"], w=["coef_im"])

    c3 = lambda a: a.rearrange("p (g h) -> p g h", h=16)
    bch = lambda a: a.unsqueeze(2).broadcast_to([128, 64, 16])
    V(lambda e: e.tensor_tensor(c3(Cp_re), c3(T1), bch(coef_re), ALU.mult), r=["T1", "coef_re"], w=["Cp_re"])
    V(lambda e: e.tensor_tensor(c3(Cp_im), c3(T2), bch(coef_im), ALU.mult), r=["T2", "coef_im"], w=["Cp_im"])
    V(lambda e: e.tensor_tensor(Cp_re, Cp_re, Cp_im, ALU.subtract), r=["Cp_re", "Cp_im"], w=["Cp_re"])
    V(lambda e: e.tensor_tensor(c3(Cp_im), c3(T1), bch(coef_im), ALU.mult), r=["T1", "coef_im"], w=["Cp_im"])
    V(lambda e: e.tensor_tensor(c3(T1), c3(T2), bch(coef_re), ALU.mult), r=["T2", "coef_re", "Cp_re"], w=["T1"])
    V(lambda e: e.tensor_tensor(Cp_im, Cp_im, T1, ALU.add), r=["Cp_im", "T1"], w=["Cp_im"])
    V(lambda e: e.tensor_scalar(Cp_im, Cp_im, -1.0, None, ALU.mult), r=["Cp_im"], w=["Cp_im"])

    g8 = lambda a: a.rearrange("p (g k) -> p g k", k=8)
    bck = lambda a: a.unsqueeze(2).broadcast_to([128, 64, 8])
    for (jt, kj, p_re, p_im, kre, kim) in ((T3, "T3", pwR_re, pwR_im, "pwR_re", "pwR_im"), (T4, "T4", pwD_re, pwD_im, "pwD_re", "pwD_im")):
        jv = jt[:, 0:512]
        mag = jt[:, 512:1024]
        V((lambda jv=jv, mag=mag: lambda e: e.tensor_tensor(g8(mag), g8(jv), bck(lr), ALU.mult))(), r=[kj, "lr"], w=[kj + "m"])
        A((lambda mag=mag: lambda e: e.activation(mag, mag, AF.Exp))(), r=[kj + "m"], w=[kj + "m"])
        V((lambda jv=jv: lambda e: e.tensor_tensor(g8(F[0][:, 0:512]), g8(jv), bck(thn), ALU.mult))(), r=[kj, "thn"], w=["F0"])
        frac_op(F[0][:, 0:512], F[0][:, 0:512], 512, "F0", "F0")
        sincos(F[1][:, 0:512], F[2][:, 0:512], F[0][:, 0:512], F[3][:, 0:512], "F0", "F1", "F2", "F3")
        V((lambda p_re=p_re, mag=mag: lambda e: e.tensor_tensor(p_re, mag, F[2][:, 0:512], ALU.mult))(), r=[kj + "m", "F2"], w=[kre])
        V((lambda p_im=p_im, mag=mag: lambda e: e.tensor_tensor(p_im, mag, F[1][:, 0:512], ALU.mult))(), r=[kj + "m", "F1"], w=[kim])

    V(lambda e: e.memset(DCz_re, 0.0), w=["DCz_re"])
    V(lambda e: e.memset(DCz_imn, 0.0), w=["DCz_imn"])
    V(lambda e: e.memset(Sst, 0.0), w=["Sst"])

    Xre, Xim, Rre, Rim, Tc, Ts, Fr, Ta, Tb = F[0], F[1], F[2], F[3], F[4], F[5], F[6], F[7], F[8]
    W5 = lambda a: a[:, 0:512]
    xnT_f = xnT.rearrange("p a b -> p (a b)").bitcast(F32)
    tabs = [(Tc, Ts, Fr, Ta, "F4", "F5", "F6", "F7"),
            (xnT_f[:, 0:512], xnT_f[:, 512:1024], xnT_f[:, 1024:1536], xnT_f[:, 1536:2048], "X4", "X5", "X6", "X7")]
    kgh = lambda a: a.rearrange("p (k g h) -> p k g h", k=8, g=8)
    def do_urev(j):
        un = u_all[:, j, :]
        V((lambda un=un: lambda e: e.tensor_copy(urev, rev_last(un)))(), r=[f"u{j}"], w=["urev"])
    def tab_xb(j):
        un = u_all[:, j, :]
        def bv(a, j=j):
            return a.rearrange("p (g h) -> p g h", h=16)[:, j * 8:(j + 1) * 8, :].unsqueeze(1).broadcast_to([128, 8, 8, 16])

        def pv(a, j=j):
            return a.rearrange("p (g k) -> p k g", k=8)[:, :, j * 8:(j + 1) * 8].unsqueeze(3).broadcast_to([128, 8, 8, 16])

        V(lambda e, bv=bv, pv=pv: e.tensor_tensor(kgh(T1), bv(Braw_re), pv(pwR_re), ALU.mult), r=["Braw_re", "pwR_re", "TZgen", "BKTgen"], w=["T1"])
        V(lambda e, bv=bv, pv=pv: e.tensor_tensor(kgh(T3), bv(Braw_im), pv(pwR_im), ALU.mult), r=["Braw_im", "pwR_im"], w=["T3"])
        V(lambda e: e.tensor_tensor(T1, T1, T3, ALU.subtract), r=["T1", "T3"], w=["T1"])
        V(lambda e, bv=bv, pv=pv: e.tensor_tensor(kgh(T2), bv(Braw_re), pv(pwR_im), ALU.mult), r=["Braw_re", "pwR_im", "TZgen", "BKTgen"], w=["T2"])
        V(lambda e, bv=bv, pv=pv: e.tensor_tensor(kgh(T3), bv(Braw_im), pv(pwR_re), ALU.mult), r=["Braw_im", "pwR_re"], w=["T3"])
        V(lambda e: e.tensor_tensor(T2, T2, T3, ALU.add), r=["T2", "T3"], w=["T2"])
    def tab_dcz(j):
        un = u_all[:, j, :]
        def cv4(a, j=j):
            return a.rearrange("p (g h) -> p g h", h=16)[:, j * 8:(j + 1) * 8, :].unsqueeze(2).broadcast_to([128, 8, 8, 16])

        def pd4(a, j=j):
            return a.rearrange("p (g k) -> p g k", k=8)[:, j * 8:(j + 1) * 8, :].unsqueeze(3).broadcast_to([128, 8, 8, 16])

        gkh = lambda a: a.rearrange("p (g k h) -> p g k h", g=8, k=8)
        V(lambda e, cv4=cv4, pd4=pd4: e.tensor_tensor(gkh(T3), cv4(Cp_re), pd4(pwD_re), ALU.mult), r=["Cp_re", "pwD_re", "T3"], w=["T3"])
        V(lambda e, cv4=cv4, pd4=pd4: e.tensor_tensor(gkh(T4), cv4(Cp_im), pd4(pwD_im), ALU.mult), r=["Cp_im", "pwD_im"], w=["T4"])
        for par in range(2):
            dv = DCz_re.rearrange("p (i e) k c -> p i e k c", e=2)[:, :, par, :, par * 16:(par + 1) * 16]
            a3 = gkh(T3).rearrange("p (i e) k h -> p i e k h", e=2)[:, :, par, :, :]
            a4 = gkh(T4).rearrange("p (i e) k h -> p i e k h", e=2)[:, :, par, :, :]
            V((lambda dv=dv, a3=a3, a4=a4: lambda e: e.tensor_tensor(dv, a3, a4, ALU.add))(), r=["T3", "T4", "stageB"], w=["DCz_re"])
        V(lambda e, cv4=cv4, pd4=pd4: e.tensor_tensor(gkh(T3), cv4(Cp_im), pd4(pwD_re), ALU.mult), r=["Cp_im", "pwD_re", "DCz_re"], w=["T3"])
        V(lambda e, cv4=cv4, pd4=pd4: e.tensor_tensor(gkh(T4), cv4(Cp_re), pd4(pwD_im), ALU.mult), r=["Cp_re", "pwD_im", "DCz_re"], w=["T4"])
        for par in range(2):
            dv = DCz_imn.rearrange("p (i e) k c -> p i e k c", e=2)[:, :, par, :, par * 16:(par + 1) * 16]
            a3 = gkh(T3).rearrange("p (i e) k h -> p i e k h", e=2)[:, :, par, :, :]
            a4 = gkh(T4).rearrange("p (i e) k h -> p i e k h", e=2)[:, :, par, :, :]
            V((lambda dv=dv, a3=a3, a4=a4: lambda e: e.tensor_tensor(dv, a3, a4, ALU.subtract))(), r=["T3", "T4", "stageB"], w=["DCz_imn"])
    def tab_bkt(j):
        un = u_all[:, j, :]
        for ri, (XBt, kxb) in enumerate(((T1, "T1"), (T2, "T2"))):
            for kh in range(2):
                pz = ps[4 + kh]
                for kk in range(4):
                    k = kh * 4 + kk
                    T((lambda pz=pz, kk=kk, k=k, XBt=XBt: lambda e: e.transpose(pz[:, kk * 128:(kk + 1) * 128], XBt[:, k * 128:(k + 1) * 128], identf))(),
                      r=[kxb, "identf"], w=[f"ps{4 + kh}"], inc=(kk == 3))
                for par in range(2):
                    dst = BKT[(par, ri)][:, kh * 4:(kh + 1) * 4, :]
                    V((lambda dst=dst, pz=pz, par=par: lambda e: e.tensor_scalar(dst, pz.rearrange("p (a b) -> p a b", b=128), parmask[:, par:par + 1], None, ALU.mult))(),
                      r=[f"ps{4 + kh}", "parmask", "stageA"], w=[f"BKT{par}{ri}", "BKTgen"])
    def tab_tz(j):
        un = u_all[:, j, :]
        for (TZ, kbase, ktz) in ((TZf, 0, "TZf"), (TZb, 64, "TZb")):
            for lh in range(2):
                pz = ps[4 + lh]
                for ll in range(4):
                    lag = lh * 4 + ll
                    k = 7 - lag
                    T((lambda pz=pz, ll=ll, k=k, kbase=kbase, j=j: lambda e: e.matmul(pz[:, ll * 128:(ll + 1) * 128], T1[kbase:kbase + 64, k * 128:(k + 1) * 128],
                                                                                 Cp_re[kbase:kbase + 64, j * 128:(j + 1) * 128], start=True, stop=False))(),
                      r=["T1", "Cp_re"], w=[f"ps{4 + lh}"], inc=False)
                    T((lambda pz=pz, ll=ll, k=k, kbase=kbase, j=j: lambda e: e.matmul(pz[:, ll * 128:(ll + 1) * 128], T2[kbase:kbase + 64, k * 128:(k + 1) * 128],
                                                                                 Cp_im[kbase:kbase + 64, j * 128:(j + 1) * 128], start=False, stop=True))(),
                      r=["T2", "Cp_im"], w=[f"ps{4 + lh}"], inc=(ll == 3))
                dst = TZ[:, lh * 4:(lh + 1) * 4, :]
                V((lambda dst=dst, pz=pz: lambda e: e.tensor_tensor(dst, pz.rearrange("p (a b) -> p a b", b=128),
                                                                      bdmask.unsqueeze(1).broadcast_to([128, 4, 128]), ALU.mult))(),
                  r=[f"ps{4 + lh}", "bdmask", "stageB"], w=[ktz, "TZgen"])
        V((lambda j=j: lambda e: e.scalar_tensor_tensor(TZf[:, 0, :], identf, dcol[:, j:j + 1], TZf[:, 0, :], ALU.mult, ALU.add))(),
          r=["TZf", "identf", "dcol"], w=["TZf"])
    def gen_tables(j, gl):
        g = j * 8 + gl
        Tc, Ts, Fr, Ta, kTc, kTs, kFr, kTa = tabs[gl % 2]
        if True:
            A((lambda g=g, Fr=Fr: lambda e: e.activation(W5(Fr), iidx, AF.Copy, scale=thn8[:, g:g + 1]))(), r=["iidx", "thn8"], w=[kFr])
            G((lambda Fr=Fr: lambda e: e.tensor_copy(FI, W5(Fr)))(), r=[kFr], w=["FI"])
            G((lambda Ta=Ta: lambda e: e.tensor_copy(W5(Ta), FI))(), r=["FI"], w=[kTa])
            G((lambda Fr=Fr, Ta=Ta: lambda e: e.tensor_tensor(W5(Fr), W5(Fr), W5(Ta), ALU.subtract))(), r=[kFr, kTa], w=[kFr])
            A((lambda Ts=Ts, Fr=Fr: lambda e: e.activation(W5(Ts), W5(Fr), AF.Sin, scale=TWO_PI))(), r=[kFr], w=[kTs])
            A((lambda Ta=Ta, Fr=Fr: lambda e: e.activation(W5(Ta), W5(Fr), AF.Abs))(), r=[kFr], w=[kTa])
            A((lambda Tc=Tc, Ta=Ta: lambda e: e.activation(W5(Tc), W5(Ta), AF.Sin, bias=halfpi, scale=-TWO_PI))(), r=[kTa, "halfpi"], w=[kTc])

    def stage_A(j, hook=None, skip_tables=(), after=None):
        un = u_all[:, j, :]
        for gl in range(8):
            if hook is not None:
                hook(gl)
            g = j * 8 + gl
            r4 = gl // 2
            par = gl % 2
            pS = (ps[0], ps[1]) if gl % 2 == 0 else (ps[4], ps[5])
            kS = ("ps0", "ps1") if gl % 2 == 0 else ("ps4", "ps5")
            Tc, Ts, Fr, Ta, kTc, kTs, kFr, kTa = tabs[gl % 2]
            rows = slice(32 * r4, 32 * r4 + 32)
            for k in range(8):
                for ri in range(2):
                    bk = BKT[(par, ri)]
                    T((lambda ri=ri, bk=bk, k=k, rows=rows, r4=r4, un=un, pS=pS: lambda e: e.matmul(
                        pS[ri][0:64, :], bk[rows, k, 0:64], un[rows, k * 512:(k + 1) * 512],
                        start=(k == 0), stop=(k == 7), tile_position=(32 * r4, 0)))(),
                      r=[f"BKT{par}{ri}", f"u{j}"], w=[kS[ri]], inc=False)
                    T((lambda ri=ri, bk=bk, k=k, rows=rows, r4=r4, pS=pS: lambda e: e.matmul(
                        pS[ri][64:128, :], bk[rows, k, 64:128], urev[rows, k * 512:(k + 1) * 512],
                        start=(k == 0), stop=(k == 7), tile_position=(32 * r4, 64)))(),
                      r=[f"BKT{par}{ri}", "urev"], w=[kS[ri]], inc=(k == 7))
            if gl not in skip_tables:
                A((lambda g=g, Fr=Fr: lambda e: e.activation(W5(Fr), iidx, AF.Copy, scale=thn8[:, g:g + 1]))(), r=["iidx", "thn8"], w=[kFr])
                G((lambda Fr=Fr: lambda e: e.tensor_copy(FI, W5(Fr)))(), r=[kFr], w=["FI"])
                G((lambda Ta=Ta: lambda e: e.tensor_copy(W5(Ta), FI))(), r=["FI"], w=[kTa])
                G((lambda Fr=Fr, Ta=Ta: lambda e: e.tensor_tensor(W5(Fr), W5(Fr), W5(Ta), ALU.subtract))(), r=[kFr, kTa], w=[kFr])
                A((lambda Ts=Ts, Fr=Fr: lambda e: e.activation(W5(Ts), W5(Fr), AF.Sin, scale=TWO_PI))(), r=[kFr], w=[kTs])
                A((lambda Ta=Ta, Fr=Fr: lambda e: e.activation(W5(Ta), W5(Fr), AF.Abs))(), r=[kFr], w=[kTa])
                A((lambda Tc=Tc, Ta=Ta: lambda e: e.activation(W5(Tc), W5(Ta), AF.Sin, bias=halfpi, scale=-TWO_PI))(), r=[kTa, "halfpi"], w=[kTc])
            V((lambda pS=pS, Tc=Tc: lambda e: e.tensor_tensor(W5(Xre), pS[0], W5(Tc), ALU.mult))(), r=[kS[0], kTc], w=["F0"])
            V((lambda pS=pS, Ts=Ts: lambda e: e.tensor_tensor(W5(Tb), pS[1], W5(Ts), ALU.mult))(), r=[kS[1], kTs], w=["F8"])
            V(lambda e: e.tensor_tensor(W5(Xre), W5(Xre), W5(Tb), ALU.add), r=["F0", "F8"], w=["F0"])
            V((lambda pS=pS, Tc=Tc: lambda e: e.tensor_tensor(W5(Xim), pS[1], W5(Tc), ALU.mult))(), r=[kS[1], kTc], w=["F1"])
            V((lambda pS=pS, Ts=Ts: lambda e: e.tensor_tensor(W5(Tb), pS[0], W5(Ts), ALU.mult))(), r=[kS[0], kTs], w=["F8"])
            V(lambda e: e.tensor_tensor(W5(Xim), W5(Xim), W5(Tb), ALU.subtract), r=["F1", "F8"], w=["F1"])
            rb = rho8[:, g:g + 1].broadcast_to([128, 512])
            V((lambda rb=rb: lambda e: e.tensor_tensor_scan(W5(Rre), rb, W5(Xre), 0.0, ALU.mult, ALU.add))(), r=["F0", "rho8"], w=["F2"])
            V((lambda rb=rb: lambda e: e.tensor_tensor_scan(W5(Rim), rb, W5(Xim), 0.0, ALU.mult, ALU.add))(), r=["F1", "rho8"], w=["F3"])
            sre = Sst[:, 2 * gl, 1:513]
            sim = Sst[:, 2 * gl + 1, 1:513]
            V((lambda Tc=Tc: lambda e: e.tensor_tensor(W5(Xre), W5(Rre), W5(Tc), ALU.mult))(), r=["F2", kTc], w=["F0"])
            V((lambda Ts=Ts: lambda e: e.tensor_tensor(W5(Tb), W5(Rim), W5(Ts), ALU.mult))(), r=["F3", kTs], w=["F8"])
            V((lambda sre=sre: lambda e: e.tensor_tensor(sre, W5(Xre), W5(Tb), ALU.subtract))(), r=["F0", "F8"], w=["Sst"])
            V((lambda Ts=Ts: lambda e: e.tensor_tensor(W5(Xim), W5(Rre), W5(Ts), ALU.mult))(), r=["F2", kTs], w=["F1"])
            V((lambda Tc=Tc: lambda e: e.tensor_tensor(W5(Tb), W5(Rim), W5(Tc), ALU.mult))(), r=["F3", kTc], w=["F8"])
            V((lambda sim=sim: lambda e: e.tensor_tensor(sim, W5(Xim), W5(Tb), ALU.add))(), r=["F1", "F8"], w=["Sst"])
            if after is not None:
                after(gl)

    def stage_B(j, mid_hook=None):
        un = u_all[:, j, :]
        for (kbase, usrc, ku, TZ, ktz, yacc, kacc) in (
                (64, urev, "urev", TZb, "TZb", yb_acc, "yb_acc"),
                (0, un, f"u{j}", TZf, "TZf", yf_acc, "yf_acc")):
            if kbase == 0 and mid_hook is not None:
                mid_hook()
            for k in range(8):
                py, kpy = ps[2 + k % 2], f"ps{2 + k % 2}"
                for k2 in range(k + 1):
                    T((lambda py=py, TZ=TZ, k=k, k2=k2, usrc=usrc: lambda e: e.matmul(
                        py, TZ[:, k - k2, :], usrc[:, k2 * 512:(k2 + 1) * 512], start=(k2 == 0), stop=False))(),
                      r=[ktz, ku], w=[kpy], inc=False)
                order = [(gl, ri) for gg in range(2) for ri in range(2) for gl in range(gg, 8, 2)]
                for n_, (gl, ri) in enumerate(order):
                    r4 = gl // 2
                    dz = DCz_re if ri == 0 else DCz_imn
                    T((lambda py=py, dz=dz, gl=gl, ri=ri, k=k, kbase=kbase, r4=r4, last=(n_ >= len(order) - 4): lambda e: e.matmul(
                        py[32 * r4:32 * r4 + 32, :], dz[kbase:kbase + 64, gl, k, :], Sst[kbase:kbase + 64, 2 * gl + ri, 0:512],
                        start=False, stop=last, tile_position=(kbase, 32 * r4)))(),
                      r=["DCz_re", "DCz_imn", "Sst"], w=[kpy], inc=(n_ == len(order) - 1))
                A((lambda py=py, yacc=yacc, k=k: lambda e: e.activation(yacc[:, k * 512:(k + 1) * 512], py, AF.Copy))(), r=[kpy], w=[kacc])
    def final_piece(j, hh):
        un = u_all[:, j, :]
        (pstep, _), (fstep, _) = yb_acc.ap
        if True:
            yfv = yf_acc.rearrange("p (k c) -> p k c", k=8)[:, :, 64 * hh:64 * hh + 64]
            ybv = AP(yb_acc.tensor, yb_acc.offset + fstep * (L - 1 - 64 * hh), [[pstep, 128], [-512 * fstep, 8], [-fstep, 64]])
            f0v = Hall[:, 0:1024].bitcast(F32).rearrange("p (k c) -> p k c", k=8)
            V((lambda yfv=yfv, ybv=ybv, f0v=f0v: lambda e: e.tensor_tensor(f0v, yfv, ybv, ALU.add))(),
              r=["yf_acc", "yb_acc"], w=["HF"])
            A((lambda hh=hh, un=un, f0v=f0v: lambda e: e.activation(
                un[:, hh * 512:(hh + 1) * 512].rearrange("p (c k) -> p k c", k=8), f0v, AF.Gelu_apprx_tanh))(), r=["HF"], w=[f"u{j}"])


    tab_xb(0); do_urev(0); tab_bkt(0)
    gen_tables(0, 0); gen_tables(0, 1)
    for j in range(8):
        tab_tz(j)
        dcz_ops = captured(tab_dcz, j)
        assert len(dcz_ops) == 8

        def after(gl, dcz_ops=dcz_ops):
            dcz_ops[gl]()
        stage_A(j, None, skip_tables=(0, 1), after=after)
        if j + 1 < 8:
            tab_xb(j + 1)
            tab_bkt(j + 1)
            gen_tables(j + 1, 0); gen_tables(j + 1, 1)
            stage_B(j, mid_hook=(lambda j=j: do_urev(j + 1)))
        else:
            stage_B(j)
        for hh in range(NT):
            final_piece(j, hh)

    if debug:
        dbg = nc.dram_tensor("dbg", [128, 8 * L], F32, kind="ExternalOutput").ap()
        dbg2 = nc.dram_tensor("dbg2", [128, 2 * L], F32, kind="ExternalOutput").ap()
        P.barrier()
        P.op("gpsimd", lambda e: e.dma_start(out=dbg, in_=u_all.rearrange("p a b -> p (a b)")), reads=[f"u{k}" for k in range(8)], dma_key="dbg", final=True)
        P.op("gpsimd", lambda e: e.dma_start(out=dbg2[:, 0:L], in_=yf_acc), reads=["yf_acc"], dma_key="dbg2", final=True)
        P.op("gpsimd", lambda e: e.dma_start(out=dbg2[:, L:2 * L], in_=yb_acc), reads=["yb_acc"], dma_key="dbg2", final=True)
        P.emit()
        return nc
    P.barrier()
    load(g_bc, g_bc_d, "g_bc"); load(fg_bc, fg_bc_d, "fg_bc")
    for pr4 in range(4):
        for hf in range(2):
            nb = 2 * pr4 + hf
            P.op("sync", (lambda nb=nb, pr4=pr4, hf=hf: lambda e: e.dma_start(out=Wout[:, :, nb * 128:(nb + 1) * 128],
                 in_=wsc[40 + pr4][:, hf * 1024:(hf + 1) * 1024].rearrange("p (a b) -> p a b", b=128)))(),
                 reads=[f"wsc{40 + pr4}_{hf}"], writes=["Wout"], dma_key=f"Wout{nb}")
    wcnt = [0]

    def wloadp(pr):
        slot = wcnt[0] % 4
        wcnt[0] += 1
        key = f"wslot{slot}"
        P.op("sync", (lambda slot=slot, pr=pr: lambda e: e.dma_start(out=wslot[slot], in_=wsc[pr].rearrange("p (a b) -> p a b", b=128)))(),
             reads=[f"wsc{pr}_0", f"wsc{pr}_1"], writes=[key], dma_key=key)
        return wslot[slot], key

    def proj(pz, kz, wk, hf, rhs_fn, rkeys, n=512, also_halo=None, halo_src=None):
        wt, kw = wk
        for kc in range(8):
            T((lambda kc=kc, wt=wt: lambda e: e.matmul(pz[:, 0:n], wt[:, hf * 8 + kc, :], rhs_fn(kc), start=(kc == 0), stop=(kc == 7)))(),
              r=[kw] + rkeys, w=[kz], inc=(kc == 7))
        if also_halo is not None:
            hz, c0 = also_halo
            for kc in range(8):
                T((lambda kc=kc, wt=wt, hs_=halo_src[0]: lambda e: e.matmul(hz[:, c0:c0 + 2], wt[:, hf * 8 + kc, :], hs_[:, kc, :], start=(kc == 0), stop=(kc == 7)))(),
                  r=[kw, halo_src[1]], w=["psH"], inc=(kc == 7))

    cv = F[0]
    cvh = cols_strided(cv, 0, 513, 2)
    xnh2 = H[5][:, 0:16].rearrange("p (a b) -> p a b", b=2)
    xbufs = [(xnT, xnh, "xnT", "xnh"), (xnT2, xnh2, "xnT2", "xnh2")]
    make_xn(0, True, *xbufs[0])
    pend = [None]
    for tt in range(NT):
        tsl = slice(tt * 512, (tt + 1) * 512)
        cxT, cxh, kxT, kxh = xbufs[tt % 2]
        xr = (lambda cxT: lambda kc: cxT[:, kc, :])(cxT)
        for f in range(8):
            yg = lambda kc, tsl=tsl: u_all[:, kc, tsl]
            wk = wloadp(3 * f)
            proj(ps[0], "ps0", wk, 0, xr, [kxT])
            A(lambda e: e.activation(H[2], ps[0], AF.Silu), r=["ps0"], w=["H2"])
            proj(ps[1], "ps1", wk, 1, yg, [f"u{k}" for k in range(8)])
            A(lambda e: e.activation(H[3], ps[1], AF.Sigmoid), r=["ps1"], w=["H3"])
            V((lambda f=f, tsl=tsl: lambda e: e.tensor_tensor(H[4], u_all[:, f, tsl], H[3], ALU.mult))(), r=[f"u{f}", "H3"], w=["H4"])
            V((lambda f=f: lambda e: e.tensor_tensor(ya[:, f, :], H[4], H[2], ALU.mult))(), r=["H4", "H2"], w=["ya"])
            wk = wloadp(3 * f + 1)
            proj(ps[2], "ps2", wk, 0, xr, [kxT], also_halo=(ps[6], 0), halo_src=(cxh, kxh))
            proj(ps[3], "ps3", wk, 1, xr, [kxT], also_halo=(ps[6], 2), halo_src=(cxh, kxh))
            A(lambda e: e.activation(F[1][:, 0:512], ps[2], AF.Copy), r=["ps2"], w=["F1"])
            A(lambda e: e.activation(F[2][:, 0:2], ps[6][:, 0:2], AF.Copy), r=["psH"], w=["F2"])
            V(lambda e: e.tensor_tensor(cv[:, 1:513], ps[3], F[1][:, 0:512], ALU.mult), r=["ps3", "F1"], w=["F0"])
            V(lambda e: e.tensor_tensor(cvh, ps[6][:, 2:4], F[2][:, 0:2], ALU.mult), r=["psH", "F2"], w=["F0"])
            V((lambda f=f: lambda e: e.tensor_scalar(F[3][:, 0:512], cv[:, 0:512], cw[:, f * 3:f * 3 + 1], cb[:, f:f + 1], ALU.mult, ALU.add))(),
              r=["F0", "cw", "cb"], w=["F3"])
            V((lambda f=f: lambda e: e.scalar_tensor_tensor(F[4][:, 0:512], cv[:, 1:513], cw[:, f * 3 + 1:f * 3 + 2], F[3][:, 0:512], ALU.mult, ALU.add))(),
              r=["F0", "F3"], w=["F4"])
            V((lambda f=f: lambda e: e.scalar_tensor_tensor(F[3][:, 0:512], cv[:, 2:514], cw[:, f * 3 + 2:f * 3 + 3], F[4][:, 0:512], ALU.mult, ALU.add))(),
              r=["F0", "F4"], w=["F3"])
            wk = wloadp(3 * f + 2)
            proj(ps[4], "ps4", wk, 0, xr, [kxT])
            A(lambda e: e.activation(F[6][:, 0:512], ps[4], AF.Copy), r=["ps4"], w=["F6"])
            V(lambda e: e.tensor_tensor(F[4][:, 0:512], F[6][:, 0:512], F[3][:, 0:512], ALU.mult), r=["F6", "F3"], w=["F4"])
            proj(ps[5], "ps5", wk, 1, xr, [kxT])
            A(lambda e: e.activation(F[5][:, 0:512], ps[5], AF.Silu), r=["ps5"], w=["F5"])
            V((lambda f=f: lambda e: e.tensor_tensor(yb[:, f, :], F[4][:, 0:512], F[5][:, 0:512], ALU.mult))(), r=["F4", "F5"], w=["yb"])
            if tt + 1 < NT and f % 2 == 1 and pend[0] is not None:
                pend[0]()
                pend[0] = None
            if tt + 1 < NT and f % 2 == 0:
                nxT, _, nkT, _ = xbufs[(tt + 1) % 2]
                pend[0] = make_xn_block(tt + 1, f // 2, nxT, nkT, defer=True)
        if pend[0] is not None:
            pend[0]()
            pend[0] = None
        halo_post = None
        if tt + 1 < NT:
            halo_post = make_xn(tt + 1, True, *xbufs[(tt + 1) % 2], blocks=False, defer_halo=True)
        xslots = {}
        for s4 in range(2):
            slot = xcnt[0] % 2
            xcnt[0] += 1
            xslots[s4] = slot
            load(xt[slot], x[tt * 512 + s4 * 128:tt * 512 + s4 * 128 + 128, :], f"xt{slot}")
        for dch in range(8):
            wk = wloadp(24 + 2 * dch)
            proj(ps[0], "ps0", wk, 0, lambda kc: ya[:, kc, :], ["ya"])
            proj(ps[1], "ps1", wk, 1, xr, [kxT])
            A(lambda e: e.activation(F[1][:, 0:512], ps[1], AF.Sigmoid), r=["ps1"], w=["F1"])
            V(lambda e: e.tensor_tensor(F[2][:, 0:512], ps[0], F[1][:, 0:512], ALU.mult), r=["ps0", "F1"], w=["F2"])
            wk = wloadp(25 + 2 * dch)
            proj(ps[2], "ps2", wk, 0, lambda kc: yb[:, kc, :], ["yb"])
            proj(ps[3], "ps3", wk, 1, xr, [kxT])
            A(lambda e: e.activation(F[3][:, 0:512], ps[3], AF.Sigmoid), r=["ps3"], w=["F3"])
            V(lambda e: e.tensor_tensor(F[4][:, 0:512], ps[2], F[3][:, 0:512], ALU.mult), r=["ps2", "F3"], w=["F4"])
            V((lambda dch=dch: lambda e: e.tensor_tensor(mg[:, dch, :], F[2][:, 0:512], F[4][:, 0:512], ALU.add))(), r=["F2", "F4"], w=["mg"])
        if halo_post is not None:
            halo_post()
        cx32 = cxT.rearrange("p a b -> p (a b)").bitcast(F32)
        xextra = {}
        for s4 in (2, 3):
            xb_ = cx32[:, (s4 - 2) * 1024:(s4 - 1) * 1024]
            P.op("sync", (lambda xb_=xb_, s4=s4, tt=tt: lambda e: e.dma_start(out=xb_, in_=x[tt * 512 + s4 * 128:tt * 512 + s4 * 128 + 128, :]))(),
                 writes=[kxT], dma_key=f"xx{s4}")
            xextra[s4] = xb_
        for s4 in range(4):
            r0 = tt * 512 + s4 * 128
            if s4 in xslots:
                slot = xslots[s4]
                xsrc = xt[slot]
                kx = f"xt{slot}"
            else:
                xsrc = xextra[s4]
                kx = kxT
            for half in range(2):
                bi = (4 + half) if s4 % 2 == 0 else (0 + half)
                pz = ps[bi]
                for dch in range(8):
                    T((lambda pz=pz, dch=dch, s4=s4, half=half: lambda e: e.matmul(pz, mg[:, dch, s4 * 128:(s4 + 1) * 128], Wout[:, dch, half * 512:(half + 1) * 512], start=(dch == 0), stop=(dch == 7)))(),
                      r=["mg", "Wout"], w=[f"ps{bi}"], inc=(dch == 7))
                V((lambda pz=pz, half=half, xsrc=xsrc: lambda e: e.tensor_tensor(hbuf[:, half * 512:(half + 1) * 512], pz, xsrc[:, half * 512:(half + 1) * 512], ALU.add))(),
                  r=[f"ps{bi}", kx], w=["hbuf"])
            A(lambda e: e.activation(F[6][:, 0:512], hbuf[:, 0:512], AF.Square, accum_out=ssq), r=["hbuf"], w=["ssq", "F6"])
            A(lambda e: e.activation(F[6][:, 0:512], hbuf[:, 512:1024], AF.Square, accum_out=std), r=["hbuf"], w=["std", "F6"])
            V(lambda e: e.tensor_tensor(ssq, ssq, std, ALU.add), r=["ssq", "std"], w=["ssq"])
            A(lambda e: e.activation(std, ssq, AF.Sqrt, bias=epsb, scale=1.0 / D), r=["ssq", "epsb"], w=["std"])
            V(lambda e: e.reciprocal(rstd, std), r=["std"], w=["rstd"])
            V(lambda e: e.scalar_tensor_tensor(otile, hbuf, rstd, fg_bc, ALU.mult, ALU.mult), r=["hbuf", "rstd", "fg_bc"], w=["otile"])
            P.op("sync", (lambda r0=r0: lambda e: e.dma_start(out=out[r0:r0 + 128, :], in_=otile))(), reads=["otile"], dma_key="outst", final=True)

    P.emit()
    return nc


_CACHE = {}


def _prep_shared(inp):
    f32 = np.float32
    sh = {}
    sh["w_in"] = np.ascontiguousarray(inp["w_in"][0], dtype=f32)
    sh["w_glu"] = np.ascontiguousarray(inp["w_glu"][0], dtype=f32)
    sh["w_a"] = np.ascontiguousarray(inp["w_branch_a"][0], dtype=f32)
    sh["w_b"] = np.ascontiguousarray(inp["w_branch_b"][0], dtype=f32)
    sh["w_out"] = np.ascontiguousarray(inp["w_out"][0], dtype=f32)
    sh["g_bc"] = np.ascontiguousarray(np.broadcast_to(inp["norm_g"][0][None, :], (128, D)), dtype=f32)
    sh["fg_bc"] = np.ascontiguousarray(np.broadcast_to(inp["final_g"][None, :], (128, D)), dtype=f32)
    sh["dcol"] = np.ascontiguousarray(inp["ssm_d"][0].reshape(8, 128).T, dtype=f32)
    sh["cw"] = np.ascontiguousarray(inp["conv_w"][0].reshape(3, 8, 128).transpose(2, 1, 0).reshape(128, 24), dtype=f32)
    sh["cb"] = np.ascontiguousarray(inp["conv_b"][0].reshape(8, 128).T, dtype=f32)
    sh["lre"] = np.ascontiguousarray(inp["lam_re"][0].transpose(0, 2, 1).reshape(128, 64), dtype=f32)
    sh["lim"] = np.ascontiguousarray(inp["lam_im"][0].transpose(0, 2, 1).reshape(128, 64), dtype=f32)
    sh["ldt"] = np.ascontiguousarray(np.broadcast_to(inp["log_dt"][0][:, None, :], (2, 64, 64)).reshape(128, 64), dtype=f32)
    for nm, key in (("b_re", "ssm_b_re"), ("b_im", "ssm_b_im")):
        b = np.asarray(inp[key][0], dtype=f32)
        sh[nm] = np.ascontiguousarray(b.transpose(0, 2, 1, 3).reshape(128, 1024))
    for nm, key in (("c_re", "ssm_c_re"), ("c_im", "ssm_c_im")):
        c = np.asarray(inp[key][0], dtype=f32)
        sh[nm] = np.ascontiguousarray(c.transpose(0, 3, 1, 2).reshape(128, 1024))
    sh["ident"] = np.eye(128, dtype=f32)
    sh["iidx"] = np.ascontiguousarray(np.broadcast_to(np.arange(512, dtype=f32)[None, :], (128, 512)))
    sh["jR"] = np.ascontiguousarray(np.broadcast_to(np.tile(np.arange(7, -1, -1).astype(f32), 64)[None, :], (128, 512)))
    sh["jD"] = np.ascontiguousarray(np.broadcast_to(np.tile(np.arange(1, 9).astype(f32), 64)[None, :], (128, 512)))
    bd = np.zeros((128, 128), dtype=f32)
    for i in range(8):
        bd[i * 16:(i + 1) * 16, i * 16:(i + 1) * 16] = 1.0
    sh["bdmask"] = bd
    pm = np.zeros((128, 2), dtype=f32)
    for p in range(128):
        pm[p, (p // 16) % 2] = 1.0
    sh["parmask"] = pm
    return sh


def kernel(**inputs):
    inp = {k: np.asarray(v) for k, v in inputs.items()}
    if "nc" not in _CACHE:
        _CACHE["nc"] = build_program()
    nc = _CACHE["nc"]
    sh = _prep_shared(inp)
    x = np.asarray(inp["x"], dtype=np.float32)
    in_maps = []
    for c in range(8):
        m = dict(sh)
        m["x"] = np.ascontiguousarray(x[c])
        in_maps.append(m)
    res = run_bass_kernel_spmd(nc, in_maps, core_ids=list(range(8)))
    return np.stack([np.asarray(r["out"], dtype=np.float32) for r in res.results], axis=0)
```

```python
import math
import numpy as np
import concourse.bass as bass
import concourse.mybir as mybir
from concourse.bass_utils import run_bass_kernel_spmd
from concourse.ap import AP

F32 = mybir.dt.float32
BF16 = mybir.dt.bfloat16
I32 = mybir.dt.int32
AF = mybir.ActivationFunctionType
ALU = mybir.AluOpType

ENGS = ("tensor", "vector", "scalar", "gpsimd", "sync")
L = 4096
D = 1024
NT = 8
EPS = 1e-6


class Prog:
    def __init__(self, nc):
        self.nc = nc
        self.ops = {e: [] for e in ENGS}
        self.esem = {e: nc.alloc_semaphore(name=f"es_{e}") for e in ENGS}
        self.ecnt = {e: 0 for e in ENGS}
        self.dsem = {}
        self.last_w = {}
        self.readers = {}
        self.waited = {e: {} for e in ENGS}
        self.final_tokens = []

    def _need(self, eng, tok, waits):
        if tok is None:
            return
        sem, val, weng = tok
        if weng == eng and eng == "tensor":
            return
        key = id(sem)
        if self.waited[eng].get(key, 0) >= val:
            return
        self.waited[eng][key] = val
        waits.append((sem, val))

    def barrier(self):
        toks = [(self.esem[e], self.ecnt[e], e) for e in ENGS if self.ecnt[e] > 0]
        toks += [(ent[0], ent[1], "dma") for ent in self.dsem.values()]
        self.pending = {e: list(toks) for e in ENGS}

    def op(self, eng, fn, reads=(), writes=(), dma_key=None, final=False, inc=True):
        waits = []
        for t in getattr(self, "pending", {}).pop(eng, []):
            if t[2] != eng:
                self._need(eng, t, waits)
        for r in reads:
            self._need(eng, self.last_w.get(r), waits)
        for w in writes:
            self._need(eng, self.last_w.get(w), waits)
            for t in self.readers.get(w, ()):
                self._need(eng, t, waits)
        if dma_key is not None:
            if dma_key not in self.dsem:
                self.dsem[dma_key] = [self.nc.alloc_semaphore(name=f"ds_{len(self.dsem)}"), 0]
            ent = self.dsem[dma_key]
            ent[1] += 16
            tok = (ent[0], ent[1], "dma")
            inc = (ent[0], 16)
        elif not inc:
            tok = (self.esem[eng], self.ecnt[eng] + 1, eng)
            inc = None
        else:
            self.ecnt[eng] += 1
            tok = (self.esem[eng], self.ecnt[eng], eng)
            inc = (self.esem[eng], 1)
        for r in reads:
            self.readers.setdefault(r, []).append(tok)
        for w in writes:
            self.last_w[w] = tok
            self.readers[w] = []
        self.ops[eng].append((fn, waits, inc))
        if final:
            self.final_tokens.append(tok)
        return tok

    def emit(self):
        nc = self.nc
        finals = self.final_tokens
        with nc.Block() as block:
            def run(engname, e):
                for fn, waits, inc in self.ops[engname]:
                    for sem, val in waits:
                        e.wait_ge(sem, val)
                    ins = fn(e)
                    if inc is not None:
                        ins.then_inc(inc[0], inc[1])
                if engname == "sync":
                    for sem, val, _ in finals:
                        e.wait_ge(sem, val)

            @block.tensor
            def _(e):
                run("tensor", e)

            @block.vector
            def _(e):
                run("vector", e)

            @block.scalar
            def _(e):
                run("scalar", e)

            @block.gpsimd
            def _(e):
                run("gpsimd", e)

            @block.sync
            def _(e):
                run("sync", e)


def rev_last(ap2d):
    (ps, pc), (fs, fc) = ap2d.ap
    return AP(ap2d.tensor, ap2d.offset + fs * (fc - 1), [[ps, pc], [-fs, fc]])


def cols_strided(ap2d, start, step, n):
    (ps, pc), (fs, fc) = ap2d.ap
    return AP(ap2d.tensor, ap2d.offset + fs * start, [[ps, pc], [fs * step, n]])


def build_program(debug=False):
    nc = bass.Bass("TRN2", target_bir_lowering=False)

    def din(name, shape, dt=F32):
        return nc.dram_tensor(name, list(shape), dt, kind="ExternalInput").ap()

    x = din("x", [L, D])
    w_in = din("w_in", [D, 8 * D])
    w_glu = din("w_glu", [D, D]); w_a = din("w_a", [D, D]); w_b = din("w_b", [D, D]); w_out = din("w_out", [D, D])
    g_bc_d = din("g_bc", [128, D]); fg_bc_d = din("fg_bc", [128, D])
    dcol_d = din("dcol", [128, 8]); cw_d = din("cw", [128, 24]); cb_d = din("cb", [128, 8])
    lre_d = din("lre", [128, 64]); lim_d = din("lim", [128, 64]); ldt_d = din("ldt", [128, 64])
    bre_d = din("b_re", [128, 1024]); bim_d = din("b_im", [128, 1024])
    cre_d = din("c_re", [128, 1024]); cim_d = din("c_im", [128, 1024])
    ident_d = din("ident", [128, 128]); iidx_d = din("iidx", [128, 512])
    jR_d = din("jR", [128, 512]); jD_d = din("jD", [128, 512])
    bdm_d = din("bdmask", [128, 128]); pm_d = din("parmask", [128, 2])
    out = nc.dram_tensor("out", [L, D], F32, kind="ExternalOutput").ap()
    wsc = nc.dram_tensor("wsc", [44, 128, 2048], BF16, kind="Internal").ap()

    P = Prog(nc)

    def sb(name, shape, dt=F32):
        return nc.alloc_sbuf_tensor(name, list(shape), dt).ap()

    u_all = sb("u_all", [128, 8, L], BF16)
    xnT = sb("xnT", [128, 8, 512], BF16)
    xnh = sb("xnh", [128, 8, 2], BF16)
    identb = sb("identb", [128, 128], BF16); identf = sb("identf", [128, 128])
    bdmask = sb("bdmask_s", [128, 128]); parmask = sb("parmask_s", [128, 2])
    iidx = sb("iidx_s", [128, 512])
    dcol = sb("dcol_s", [128, 8]); cw = sb("cw_s", [128, 24]); cb = sb("cb_s", [128, 8])
    thn = sb("thn", [128, 64]); thn8 = sb("thn8", [128, 64]); rho8 = sb("rho8", [128, 64])
    coef_re = sb("coef_re", [128, 64]); coef_im = sb("coef_im", [128, 64])
    sm = [sb(f"sm{i}", [128, 64]) for i in range(8)]
    ssq = sb("ssq", [128, 1]); std = sb("std", [128, 1]); rstd = sb("rstd", [128, 1])
    epsb = sb("epsb", [128, 1]); halfpi = sb("halfpi", [128, 1])
    F = [sb(f"F{i}", [128, 514]) for i in range(9)]
    FI = sb("FI", [128, 512], I32)
    smi = FI
    Hall = sb("Hall", [128, 6 * 512], BF16)
    H = [Hall[:, i * 512:(i + 1) * 512] for i in range(6)]
    ARENA = 49664
    arena = sb("arena", [128, ARENA], BF16)

    def carve(off_el, shape, dt=BF16):
        n = int(np.prod(shape[1:]))
        if dt == F32:
            assert off_el + 2 * n <= ARENA
            a = arena[:, off_el:off_el + 2 * n].bitcast(F32)
        else:
            assert off_el + n <= ARENA
            a = arena[:, off_el:off_el + n]
        if len(shape) == 3:
            a = a.rearrange("p (a b) -> p a b", b=shape[2])
        if len(shape) == 4:
            a = a.rearrange("p (a b c) -> p a b c", b=shape[2], c=shape[3])
        return a

    xt = [carve(40960 + 2048 * i, [128, D], F32) for i in range(2)]
    xs = carve(45056, [128, D])
    g_bc = carve(46080, [128, D], F32)
    xh = carve(48128, [128, 512], F32)
    Wu = carve(0, [128, 64, 128])
    Braw_re = carve(0, [128, 1024], F32); Braw_im = carve(2048, [128, 1024], F32)
    Cp_re = carve(4096, [128, 1024], F32); Cp_im = carve(6144, [128, 1024], F32)
    pwR_re = carve(8192, [128, 512], F32); pwR_im = carve(9216, [128, 512], F32)
    pwD_re = carve(10240, [128, 512], F32); pwD_im = carve(11264, [128, 512], F32)
    T1 = carve(12288, [128, 1024], F32); T2 = carve(14336, [128, 1024], F32)
    T3 = carve(16384, [128, 1024], F32); T4 = carve(18432, [128, 1024], F32)
    urev = carve(20480, [128, L])
    yf_acc = carve(24576, [128, L]); yb_acc = carve(28672, [128, L])
    BKT = {(par, ri): carve(32768 + 1024 * (2 * par + ri), [128, 8, 128]) for par in range(2) for ri in range(2)}
    DCz_re = carve(36864, [128, 8, 8, 32]); DCz_imn = carve(38912, [128, 8, 8, 32])
    Sst = carve(40960, [128, 16, 514])
    Wout = carve(0, [128, 8, 1024])
    ya = carve(8192, [128, 8, 512]); yb = carve(12288, [128, 8, 512]); mg = carve(16384, [128, 8, 512])
    wslot = [carve(20480 + 2048 * i, [128, 16, 128]) for i in range(2)] + [carve(36864 + 2048 * i, [128, 16, 128]) for i in range(2)]
    xnT2 = carve(32768, [128, 8, 512])
    otile = carve(24576, [128, D], F32); hbuf = carve(26624, [128, D], F32)
    fg_bc = carve(28672, [128, D], F32)
    xh = carve(30720, [128, D], F32)[0:2, :]
    wstage = [carve(20480 + 1024 * i, [128, 8, 128]) for i in range(6)]
    wstage_f = [carve(8192 + 2048 * i, [128, 8, 128], F32) for i in range(6)]
    TZf = Hall[:, 2 * 512:4 * 512].rearrange("p (a b) -> p a b", b=128)
    TZb = Hall[:, 4 * 512:6 * 512].rearrange("p (a b) -> p a b", b=128)

    ps = [nc.alloc_psum_tensor(f"ps{i}", [128, 512], F32).ap() for i in range(7)]
    psT = nc.alloc_psum_tensor("psT", [128, 8, 128], BF16).ap()

    TWO_PI = 2.0 * math.pi

    cap = [None]

    def _op(eng, fn, r, w, inc=True):
        if cap[0] is not None:
            cap[0].append(lambda: P.op(eng, fn, reads=r, writes=w, inc=inc))
            return None
        return P.op(eng, fn, reads=r, writes=w, inc=inc)

    def captured(f, *args):
        cap[0] = []
        f(*args)
        ops = cap[0]
        cap[0] = None
        return ops

    def V(fn, r=(), w=()):
        return _op("vector", fn, r, w)

    def A(fn, r=(), w=()):
        return _op("scalar", fn, r, w)

    def G(fn, r=(), w=()):
        return _op("gpsimd", fn, r, w)

    def T(fn, r=(), w=(), inc=True):
        return _op("tensor", fn, r, w, inc)

    def load(dst, src, key, eng="sync", extra_w=()):
        return P.op(eng, lambda e: e.dma_start(out=dst, in_=src), writes=[key] + list(extra_w), dma_key=key)

    load(g_bc, g_bc_d, "g_bc")
    load(identb, ident_d, "identb", eng="gpsimd"); load(identf, ident_d, "identf")
    load(bdmask, bdm_d, "bdmask"); load(parmask, pm_d, "parmask")
    load(iidx, iidx_d, "iidx")
    load(dcol, dcol_d, "dcol"); load(cw, cw_d, "cw"); load(cb, cb_d, "cb")
    V(lambda e: e.memset(epsb, EPS), w=["epsb"])
    V(lambda e: e.memset(halfpi, math.pi / 2), w=["halfpi"])
    w_in_v = w_in.rearrange("(kc p) n -> p kc n", p=128)
    srcs = [("Wu", w_in_v, i, None) for i in range(8)]
    w_glu_v = w_glu.rearrange("(kc p) n -> p kc n", p=128); w_a_v = w_a.rearrange("(kc p) n -> p kc n", p=128)
    w_b_v = w_b.rearrange("(kc p) n -> p kc n", p=128); w_out_v = w_out.rearrange("(kc p) n -> p kc n", p=128)
    for f in range(8):
        srcs += [("sc", w_in_v, 8 + f, (3 * f, 0)), ("sc", w_glu_v, f, (3 * f, 1)),
                 ("sc", w_in_v, 16 + f, (3 * f + 1, 0)), ("sc", w_in_v, 32 + f, (3 * f + 1, 1)),
                 ("sc", w_in_v, 24 + f, (3 * f + 2, 0)), ("sc", w_in_v, 40 + f, (3 * f + 2, 1))]
    for dch in range(8):
        srcs += [("sc", w_a_v, dch, (24 + 2 * dch, 0)), ("sc", w_in_v, 48 + dch, (24 + 2 * dch, 1)),
                 ("sc", w_b_v, dch, (25 + 2 * dch, 0)), ("sc", w_in_v, 56 + dch, (25 + 2 * dch, 1))]
    for nb in range(8):
        srcs.append(("sc", w_out_v, nb, (40 + nb // 2, nb % 2)))

    def conv_chunk(n_):
        kind, wv, cbk, dst = srcs[n_]
        sl = n_ % 6
        sf = wstage_f[sl]
        kf = f"wstf{sl}"
        P.op("sync", (lambda sf=sf, wv=wv, cbk=cbk: lambda e: e.dma_start(out=sf, in_=wv[:, :, cbk * 128:(cbk + 1) * 128]))(),
             writes=[kf], dma_key=kf)
        if kind == "Wu":
            V((lambda sf=sf, cbk=cbk: lambda e: e.tensor_copy(Wu[:, cbk * 8:(cbk + 1) * 8, :], sf))(), r=[kf], w=[f"Wu{cbk}"])
        else:
            st = wstage[sl]
            kb = f"wstb{sl}"
            pr, hf = dst
            (G if n_ % 4 == 3 else V)((lambda sf=sf, st=st: lambda e: e.tensor_copy(st, sf))(), r=[kf], w=[kb])
            P.op("gpsimd", (lambda st=st, pr=pr, hf=hf: lambda e: e.dma_start(
                out=wsc[pr][:, hf * 1024:(hf + 1) * 1024].rearrange("p (a b) -> p a b", b=128), in_=st))(),
                 reads=[kb], writes=[f"wsc{pr}_{hf}"], dma_key=kb + "_st")

    for n_ in range(8):
        conv_chunk(n_)

    xcnt = [0]

    def norm_rows(src_rows_ap, nrows, dst_xnT_ap, dst_key="xnT", defer=False):
        slot = xcnt[0] % 2
        xcnt[0] += 1
        xb = xt[slot][0:nrows, :]
        kx = f"xt{slot}"
        load(xb, src_rows_ap, kx)
        return finish_norm(xb, kx, nrows, dst_xnT_ap, dst_key=dst_key, defer=defer)

    def finish_norm(xb, kx, nrows, dst, dst_key="xnT", defer=False):
        A(lambda e: e.activation(F[7][0:nrows, 0:512], xb[:, 0:512], AF.Square, accum_out=ssq[0:nrows, :]),
          r=[kx], w=["ssq", "F7"])
        A(lambda e: e.activation(F[7][0:nrows, 0:512], xb[:, 512:1024], AF.Square, accum_out=std[0:nrows, :]),
          r=[kx], w=["std", "F7"])
        V(lambda e: e.tensor_tensor(ssq[0:nrows, :], ssq[0:nrows, :], std[0:nrows, :], ALU.add), r=["ssq", "std"], w=["ssq"])
        A(lambda e: e.activation(std[0:nrows, :], ssq[0:nrows, :], AF.Sqrt, bias=epsb[0:nrows, :], scale=1.0 / D),
          r=["ssq", "epsb"], w=["std"])
        V(lambda e: e.reciprocal(rstd[0:nrows, :], std[0:nrows, :]), r=["std"], w=["rstd"])
        V(lambda e: e.scalar_tensor_tensor(xs[0:nrows, :], xb, rstd[0:nrows, :], g_bc[0:nrows, :], ALU.mult, ALU.mult),
          r=[kx, "rstd", "g_bc"], w=["xs"])

        def post():
            for kc in range(8):
                T((lambda kc=kc: lambda e: e.transpose(psT[:, kc, 0:nrows], xs[0:nrows, kc * 128:(kc + 1) * 128], identb[0:nrows, 0:nrows]))(),
                  r=["xs", "identb"], w=["psT"], inc=(kc == 7))
            A(lambda e: e.activation(dst, psT[:, :, 0:nrows], AF.Copy), r=["psT"], w=[dst_key])
        if defer:
            return post
        post()
        return None

    def make_xn_block(tt, s, dstT, kT, defer=False):
        r0 = tt * 512 + s * 128
        return norm_rows(x[r0:r0 + 128, :], 128, dstT[:, :, s * 128:(s + 1) * 128], dst_key=kT, defer=defer)

    def make_xn(tt, with_halo, dstT=None, dsth=None, kT="xnT", kh="xnh", blocks=True, defer_halo=False):
        dstT = xnT if dstT is None else dstT
        dsth = xnh if dsth is None else dsth
        if blocks:
            for s in range(4):
                make_xn_block(tt, s, dstT, kT)
        if with_halo:
            V(lambda e: e.memset(xh, 0.0), w=["xh"])
            if tt > 0:
                load(xh[0:1, :], x[tt * 512 - 1:tt * 512, :], "xh")
            if tt < NT - 1:
                load(xh[1:2, :], x[(tt + 1) * 512:(tt + 1) * 512 + 1, :], "xh")
            return finish_norm(xh, "xh", 2, dsth, dst_key=kh, defer=defer_halo)
        return None

    for tt in range(NT):
        make_xn(tt, False)
        for n_ in range(8 + 11 * tt, 8 + 11 * (tt + 1)):
            conv_chunk(n_)
        for j in range(8):
            pb = j % 2
            for kc in range(8):
                T((lambda j=j, kc=kc, pb=pb: lambda e: e.matmul(ps[pb], Wu[:, j * 8 + kc, :], xnT[:, kc, :], start=(kc == 0), stop=(kc == 7)))(),
                  r=[f"Wu{j}", "xnT"], w=[f"ps{pb}"], inc=(kc == 7))
            V((lambda j=j, pb=pb, tt=tt: lambda e: e.tensor_copy(
                u_all[:, j, :].rearrange("p (k c) -> p k c", k=8)[:, :, 64 * tt:64 * tt + 64],
                ps[pb].rearrange("p (c k) -> p k c", k=8)))(),
              r=[f"ps{pb}"], w=[f"u{j}"])

    P.barrier()
    lre, lim, dtt, lr, th, t0, t1, t2 = sm
    load(lre, lre_d, "lre"); load(lim, lim_d, "lim"); load(dtt, ldt_d, "ldt")
    load(Braw_re, bre_d, "Braw_re"); load(Braw_im, bim_d, "Braw_im")
    load(T1, cre_d, "T1"); load(T2, cim_d, "T2")
    load(T3[:, 0:512], jR_d, "T3"); load(T4[:, 0:512], jD_d, "T4")
    A(lambda e: e.activation(dtt, dtt, AF.Exp), r=["ldt"], w=["dtt"])
    V(lambda e: e.tensor_tensor(lr, lre, dtt, ALU.mult), r=["lre", "dtt"], w=["lr"])
    V(lambda e: e.tensor_tensor(th, lim, dtt, ALU.mult), r=["lim", "dtt"], w=["th"])
    V(lambda e: e.tensor_scalar(thn, th, 1.0 / TWO_PI, None, ALU.mult), r=["th"], w=["thn"])
    A(lambda e: e.activation(rho8, lr, AF.Exp, scale=8.0), r=["lr"], w=["rho8"])

    def frac_op(dst, src, n, kd, ks):
        V(lambda e: e.tensor_copy(smi[:, 0:n], src), r=[ks], w=["FI"])
        V(lambda e: e.tensor_copy(F[8][:, 0:n], smi[:, 0:n]), r=["FI"], w=["F8"])
        V(lambda e: e.tensor_tensor(dst, src, F[8][:, 0:n], ALU.subtract), r=[ks, "F8"], w=[kd])

    def sincos(sin_dst, cos_dst, frac_src, tmp, ks, kk_s, kk_c, kt):
        A(lambda e: e.activation(sin_dst, frac_src, AF.Sin, scale=TWO_PI), r=[ks], w=[kk_s])
        A(lambda e: e.activation(tmp, frac_src, AF.Abs), r=[ks], w=[kt])
        A(lambda e: e.activation(cos_dst, tmp, AF.Sin, bias=halfpi, scale=-TWO_PI), r=[kt, "halfpi"], w=[kk_c])

    V(lambda e: e.tensor_scalar(thn8, thn, 8.0, None, ALU.mult), r=["thn"], w=["thn8"])
    frac_op(thn8, thn8, 64, "thn8", "thn8")
    rho1 = sm[4]
    A(lambda e: e.activation(rho1, lr, AF.Exp), r=["lr", "thn"], w=["rho1"])
    frac_op(t0, thn, 64, "t0", "thn")
    sincos(t1, t2, t0, F[6][:, 0:64], "t0", "t1", "t2", "F6")
    V(lambda e: e.tensor_tensor(t1, t1, rho1, ALU.mult), r=["t1", "rho1"], w=["t1"])
    V(lambda e: e.tensor_tensor(t2, t2, rho1, ALU.mult), r=["t2", "rho1"], w=["t2"])
    V(lambda e: e.tensor_scalar(t2, t2, -1.0, None, ALU.add), r=["t2"], w=["t2"])
    V(lambda e: e.tensor_tensor(t0, lre, lre, ALU.mult), r=["lre"], w=["t0"])
    V(lambda e: e.tensor_tensor(dtt, lim, lim, ALU.mult), r=["lim"], w=["dtt"])
    V(lambda e: e.tensor_tensor(t0, t0, dtt, ALU.add), r=["t0", "dtt"], w=["t0"])
    V(lambda e: e.reciprocal(dtt, t0), r=["t0"], w=["dtt"])
    V(lambda e: e.tensor_tensor(t0, t2, lre, ALU.mult), r=["t2", "lre"], w=["t0"])
    V(lambda e: e.tensor_tensor(rho1, t1, lim, ALU.mult), r=["t1", "lim"], w=["rho1"])
    V(lambda e: e.tensor_tensor(t0, t0, rho1, ALU.add), r=["t0", "rho1"], w=["t0"])
    V(lambda e: e.tensor_tensor(coef_re, t0, dtt, ALU.mult), r=["t0", "dtt"], w=["coef_re"])
    V(lambda e: e.tensor_tensor(t0, t1, lre, ALU.mult), r=["t1", "lre"], w=["t0"])
    V(lambda e: e.tensor_tensor(rho1, t2, lim, ALU.mult), r=["t2", "lim"], w=["rho1"])
    V(lambda e: e.tensor_tensor(t0, t0, rho1, ALU.subtract), r=["t0", "rho1"], w=["t0"])
    V(lambda e: e.tensor_tensor(coef_im, t0, dtt, ALU.mult), r=["t0", "dtt"], w=["coef_im"])

    c3 = lambda a: a.rearrange("p (g h) -> p g h", h=16)
    bch = lambda a: a.unsqueeze(2).broadcast_to([128, 64, 16])
    V(lambda e: e.tensor_tensor(c3(Cp_re), c3(T1), bch(coef_re), ALU.mult), r=["T1", "coef_re"], w=["Cp_re"])
    V(lambda e: e.tensor_tensor(c3(Cp_im), c3(T2), bch(coef_im), ALU.mult), r=["T2", "coef_im"], w=["Cp_im"])
    V(lambda e: e.tensor_tensor(Cp_re, Cp_re, Cp_im, ALU.subtract), r=["Cp_re", "Cp_im"], w=["Cp_re"])
    V(lambda e: e.tensor_tensor(c3(Cp_im), c3(T1), bch(coef_im), ALU.mult), r=["T1", "coef_im"], w=["Cp_im"])
    V(lambda e: e.tensor_tensor(c3(T1), c3(T2), bch(coef_re), ALU.mult), r=["T2", "coef_re", "Cp_re"], w=["T1"])
    V(lambda e: e.tensor_tensor(Cp_im, Cp_im, T1, ALU.add), r=["Cp_im", "T1"], w=["Cp_im"])
    V(lambda e: e.tensor_scalar(Cp_im, Cp_im, -1.0, None, ALU.mult), r=["Cp_im"], w=["Cp_im"])

    g8 = lambda a: a.rearrange("p (g k) -> p g k", k=8)
    bck = lambda a: a.unsqueeze(2).broadcast_to([128, 64, 8])
    for (jt, kj, p_re, p_im, kre, kim) in ((T3, "T3", pwR_re, pwR_im, "pwR_re", "pwR_im"), (T4, "T4", pwD_re, pwD_im, "pwD_re", "pwD_im")):
        jv = jt[:, 0:512]
        mag = jt[:, 512:1024]
        V((lambda jv=jv, mag=mag: lambda e: e.tensor_tensor(g8(mag), g8(jv), bck(lr), ALU.mult))(), r=[kj, "lr"], w=[kj + "m"])
        A((lambda mag=mag: lambda e: e.activation(mag, mag, AF.Exp))(), r=[kj + "m"], w=[kj + "m"])
        V((lambda jv=jv: lambda e: e.tensor_tensor(g8(F[0][:, 0:512]), g8(jv), bck(thn), ALU.mult))(), r=[kj, "thn"], w=["F0"])
        frac_op(F[0][:, 0:512], F[0][:, 0:512], 512, "F0", "F0")
        sincos(F[1][:, 0:512], F[2][:, 0:512], F[0][:, 0:512], F[3][:, 0:512], "F0", "F1", "F2", "F3")
        V((lambda p_re=p_re, mag=mag: lambda e: e.tensor_tensor(p_re, mag, F[2][:, 0:512], ALU.mult))(), r=[kj + "m", "F2"], w=[kre])
        V((lambda p_im=p_im, mag=mag: lambda e: e.tensor_tensor(p_im, mag, F[1][:, 0:512], ALU.mult))(), r=[kj + "m", "F1"], w=[kim])

    V(lambda e: e.memset(DCz_re, 0.0), w=["DCz_re"])
    V(lambda e: e.memset(DCz_imn, 0.0), w=["DCz_imn"])
    V(lambda e: e.memset(Sst, 0.0), w=["Sst"])

    Xre, Xim, Rre, Rim, Tc, Ts, Fr, Ta, Tb = F[0], F[1], F[2], F[3], F[4], F[5], F[6], F[7], F[8]
    W5 = lambda a: a[:, 0:512]
    xnT_f = xnT.rearrange("p a b -> p (a b)").bitcast(F32)
    tabs = [(Tc, Ts, Fr, Ta, "F4", "F5", "F6", "F7"),
            (xnT_f[:, 0:512], xnT_f[:, 512:1024], xnT_f[:, 1024:1536], xnT_f[:, 1536:2048], "X4", "X5", "X6", "X7")]
    kgh = lambda a: a.rearrange("p (k g h) -> p k g h", k=8, g=8)
    def do_urev(j):
        un = u_all[:, j, :]
        V((lambda un=un: lambda e: e.tensor_copy(urev, rev_last(un)))(), r=[f"u{j}"], w=["urev"])
    def tab_xb(j):
        un = u_all[:, j, :]
        def bv(a, j=j):
            return a.rearrange("p (g h) -> p g h", h=16)[:, j * 8:(j + 1) * 8, :].unsqueeze(1).broadcast_to([128, 8, 8, 16])

        def pv(a, j=j):
            return a.rearrange("p (g k) -> p k g", k=8)[:, :, j * 8:(j + 1) * 8].unsqueeze(3).broadcast_to([128, 8, 8, 16])

        V(lambda e, bv=bv, pv=pv: e.tensor_tensor(kgh(T1), bv(Braw_re), pv(pwR_re), ALU.mult), r=["Braw_re", "pwR_re", "TZgen", "BKTgen"], w=["T1"])
        V(lambda e, bv=bv, pv=pv: e.tensor_tensor(kgh(T3), bv(Braw_im), pv(pwR_im), ALU.mult), r=["Braw_im", "pwR_im"], w=["T3"])
        V(lambda e: e.tensor_tensor(T1, T1, T3, ALU.subtract), r=["T1", "T3"], w=["T1"])
        V(lambda e, bv=bv, pv=pv: e.tensor_tensor(kgh(T2), bv(Braw_re), pv(pwR_im), ALU.mult), r=["Braw_re", "pwR_im", "TZgen", "BKTgen"], w=["T2"])
        V(lambda e, bv=bv, pv=pv: e.tensor_tensor(kgh(T3), bv(Braw_im), pv(pwR_re), ALU.mult), r=["Braw_im", "pwR_re"], w=["T3"])
        V(lambda e: e.tensor_tensor(T2, T2, T3, ALU.add), r=["T2", "T3"], w=["T2"])
    def tab_dcz(j):
        un = u_all[:, j, :]
        def cv4(a, j=j):
            return a.rearrange("p (g h) -> p g h", h=16)[:, j * 8:(j + 1) * 8, :].unsqueeze(2).broadcast_to([128, 8, 8, 16])

        def pd4(a, j=j):
            return a.rearrange("p (g k) -> p g k", k=8)[:, j * 8:(j + 1) * 8, :].unsqueeze(3).broadcast_to([128, 8, 8, 16])

        gkh = lambda a: a.rearrange("p (g k h) -> p g k h", g=8, k=8)
        V(lambda e, cv4=cv4, pd4=pd4: e.tensor_tensor(gkh(T3), cv4(Cp_re), pd4(pwD_re), ALU.mult), r=["Cp_re", "pwD_re", "T3"], w=["T3"])
        V(lambda e, cv4=cv4, pd4=pd4: e.tensor_tensor(gkh(T4), cv4(Cp_im), pd4(pwD_im), ALU.mult), r=["Cp_im", "pwD_im"], w=["T4"])
        for par in range(2):
            dv = DCz_re.rearrange("p (i e) k c -> p i e k c", e=2)[:, :, par, :, par * 16:(par + 1) * 16]
            a3 = gkh(T3).rearrange("p (i e) k h -> p i e k h", e=2)[:, :, par, :, :]
            a4 = gkh(T4).rearrange("p (i e) k h -> p i e k h", e=2)[:, :, par, :, :]
            V((lambda dv=dv, a3=a3, a4=a4: lambda e: e.tensor_tensor(dv, a3, a4, ALU.add))(), r=["T3", "T4", "stageB"], w=["DCz_re"])
        V(lambda e, cv4=cv4, pd4=pd4: e.tensor_tensor(gkh(T3), cv4(Cp_im), pd4(pwD_re), ALU.mult), r=["Cp_im", "pwD_re", "DCz_re"], w=["T3"])
        V(lambda e, cv4=cv4, pd4=pd4: e.tensor_tensor(gkh(T4), cv4(Cp_re), pd4(pwD_im), ALU.mult), r=["Cp_re", "pwD_im", "DCz_re"], w=["T4"])
        for par in range(2):
            dv = DCz_imn.rearrange("p (i e) k c -> p i e k c", e=2)[:, :, par, :, par * 16:(par + 1) * 16]
            a3 = gkh(T3).rearrange("p (i e) k h -> p i e k h", e=2)[:, :, par, :, :]
            a4 = gkh(T4).rearrange("p (i e) k h -> p i e k h", e=2)[:, :, par, :, :]
            V((lambda dv=dv, a3=a3, a4=a4: lambda e: e.tensor_tensor(dv, a3, a4, ALU.subtract))(), r=["T3", "T4", "stageB"], w=["DCz_imn"])
    def tab_bkt(j):
        un = u_all[:, j, :]
        for ri, (XBt, kxb) in enumerate(((T1, "T1"), (T2, "T2"))):
            for kh in range(2):
                pz = ps[4 + kh]
                for kk in range(4):
                    k = kh * 4 + kk
                    T((lambda pz=pz, kk=kk, k=k, XBt=XBt: lambda e: e.transpose(pz[:, kk * 128:(kk + 1) * 128], XBt[:, k * 128:(k + 1) * 128], identf))(),
                      r=[kxb, "identf"], w=[f"ps{4 + kh}"], inc=(kk == 3))
                for par in range(2):
                    dst = BKT[(par, ri)][:, kh * 4:(kh + 1) * 4, :]
                    V((lambda dst=dst, pz=pz, par=par: lambda e: e.tensor_scalar(dst, pz.rearrange("p (a b) -> p a b", b=128), parmask[:, par:par + 1], None, ALU.mult))(),
                      r=[f"ps{4 + kh}", "parmask", "stageA"], w=[f"BKT{par}{ri}", "BKTgen"])
    def tab_tz(j):
        un = u_all[:, j, :]
        for (TZ, kbase, ktz) in ((TZf, 0, "TZf"), (TZb, 64, "TZb")):
            for lh in range(2):
                pz = ps[4 + lh]
                for ll in range(4):
                    lag = lh * 4 + ll
                    k = 7 - lag
                    T((lambda pz=pz, ll=ll, k=k, kbase=kbase, j=j: lambda e: e.matmul(pz[:, ll * 128:(ll + 1) * 128], T1[kbase:kbase + 64, k * 128:(k + 1) * 128],
                                                                                 Cp_re[kbase:kbase + 64, j * 128:(j + 1) * 128], start=True, stop=False))(),
                      r=["T1", "Cp_re"], w=[f"ps{4 + lh}"], inc=False)
                    T((lambda pz=pz, ll=ll, k=k, kbase=kbase, j=j: lambda e: e.matmul(pz[:, ll * 128:(ll + 1) * 128], T2[kbase:kbase + 64, k * 128:(k + 1) * 128],
                                                                                 Cp_im[kbase:kbase + 64, j * 128:(j + 1) * 128], start=False, stop=True))(),
                      r=["T2", "Cp_im"], w=[f"ps{4 + lh}"], inc=(ll == 3))
                dst = TZ[:, lh * 4:(lh + 1) * 4, :]
                V((lambda dst=dst, pz=pz: lambda e: e.tensor_tensor(dst, pz.rearrange("p (a b) -> p a b", b=128),
                                                                      bdmask.unsqueeze(1).broadcast_to([128, 4, 128]), ALU.mult))(),
                  r=[f"ps{4 + lh}", "bdmask", "stageB"], w=[ktz, "TZgen"])
        V((lambda j=j: lambda e: e.scalar_tensor_tensor(TZf[:, 0, :], identf, dcol[:, j:j + 1], TZf[:, 0, :], ALU.mult, ALU.add))(),
          r=["TZf", "identf", "dcol"], w=["TZf"])
    def gen_tables(j, gl):
        g = j * 8 + gl
        Tc, Ts, Fr, Ta, kTc, kTs, kFr, kTa = tabs[gl % 2]
        if True:
            A((lambda g=g, Fr=Fr: lambda e: e.activation(W5(Fr), iidx, AF.Copy, scale=thn8[:, g:g + 1]))(), r=["iidx", "thn8"], w=[kFr])
            G((lambda Fr=Fr: lambda e: e.tensor_copy(FI, W5(Fr)))(), r=[kFr], w=["FI"])
            G((lambda Ta=Ta: lambda e: e.tensor_copy(W5(Ta), FI))(), r=["FI"], w=[kTa])
            G((lambda Fr=Fr, Ta=Ta: lambda e: e.tensor_tensor(W5(Fr), W5(Fr), W5(Ta), ALU.subtract))(), r=[kFr, kTa], w=[kFr])
            A((lambda Ts=Ts, Fr=Fr: lambda e: e.activation(W5(Ts), W5(Fr), AF.Sin, scale=TWO_PI))(), r=[kFr], w=[kTs])
            A((lambda Ta=Ta, Fr=Fr: lambda e: e.activation(W5(Ta), W5(Fr), AF.Abs))(), r=[kFr], w=[kTa])
            A((lambda Tc=Tc, Ta=Ta: lambda e: e.activation(W5(Tc), W5(Ta), AF.Sin, bias=halfpi, scale=-TWO_PI))(), r=[kTa, "halfpi"], w=[kTc])

    def stage_A(j, hook=None, skip_tables=(), after=None):
        un = u_all[:, j, :]
        for gl in range(8):
            if hook is not None:
                hook(gl)
            g = j * 8 + gl
            r4 = gl // 2
            par = gl % 2
            pS = (ps[0], ps[1]) if gl % 2 == 0 else (ps[4], ps[5])
            kS = ("ps0", "ps1") if gl % 2 == 0 else ("ps4", "ps5")
            Tc, Ts, Fr, Ta, kTc, kTs, kFr, kTa = tabs[gl % 2]
            rows = slice(32 * r4, 32 * r4 + 32)
            for k in range(8):
                for ri in range(2):
                    bk = BKT[(par, ri)]
                    T((lambda ri=ri, bk=bk, k=k, rows=rows, r4=r4, un=un, pS=pS: lambda e: e.matmul(
                        pS[ri][0:64, :], bk[rows, k, 0:64], un[rows, k * 512:(k + 1) * 512],
                        start=(k == 0), stop=(k == 7), tile_position=(32 * r4, 0)))(),
                      r=[f"BKT{par}{ri}", f"u{j}"], w=[kS[ri]], inc=False)
                    T((lambda ri=ri, bk=bk, k=k, rows=rows, r4=r4, pS=pS: lambda e: e.matmul(
                        pS[ri][64:128, :], bk[rows, k, 64:128], urev[rows, k * 512:(k + 1) * 512],
                        start=(k == 0), stop=(k == 7), tile_position=(32 * r4, 64)))(),
                      r=[f"BKT{par}{ri}", "urev"], w=[kS[ri]], inc=(k == 7))
            if gl not in skip_tables:
                A((lambda g=g, Fr=Fr: lambda e: e.activation(W5(Fr), iidx, AF.Copy, scale=thn8[:, g:g + 1]))(), r=["iidx", "thn8"], w=[kFr])
                G((lambda Fr=Fr: lambda e: e.tensor_copy(FI, W5(Fr)))(), r=[kFr], w=["FI"])
                G((lambda Ta=Ta: lambda e: e.tensor_copy(W5(Ta), FI))(), r=["FI"], w=[kTa])
                G((lambda Fr=Fr, Ta=Ta: lambda e: e.tensor_tensor(W5(Fr), W5(Fr), W5(Ta), ALU.subtract))(), r=[kFr, kTa], w=[kFr])
                A((lambda Ts=Ts, Fr=Fr: lambda e: e.activation(W5(Ts), W5(Fr), AF.Sin, scale=TWO_PI))(), r=[kFr], w=[kTs])
                A((lambda Ta=Ta, Fr=Fr: lambda e: e.activation(W5(Ta), W5(Fr), AF.Abs))(), r=[kFr], w=[kTa])
                A((lambda Tc=Tc, Ta=Ta: lambda e: e.activation(W5(Tc), W5(Ta), AF.Sin, bias=halfpi, scale=-TWO_PI))(), r=[kTa, "halfpi"], w=[kTc])
            V((lambda pS=pS, Tc=Tc: lambda e: e.tensor_tensor(W5(Xre), pS[0], W5(Tc), ALU.mult))(), r=[kS[0], kTc], w=["F0"])
            V((lambda pS=pS, Ts=Ts: lambda e: e.tensor_tensor(W5(Tb), pS[1], W5(Ts), ALU.mult))(), r=[kS[1], kTs], w=["F8"])
            V(lambda e: e.tensor_tensor(W5(Xre), W5(Xre), W5(Tb), ALU.add), r=["F0", "F8"], w=["F0"])
            V((lambda pS=pS, Tc=Tc: lambda e: e.tensor_tensor(W5(Xim), pS[1], W5(Tc), ALU.mult))(), r=[kS[1], kTc], w=["F1"])
            V((lambda pS=pS, Ts=Ts: lambda e: e.tensor_tensor(W5(Tb), pS[0], W5(Ts), ALU.mult))(), r=[kS[0], kTs], w=["F8"])
            V(lambda e: e.tensor_tensor(W5(Xim), W5(Xim), W5(Tb), ALU.subtract), r=["F1", "F8"], w=["F1"])
            rb = rho8[:, g:g + 1].broadcast_to([128, 512])
            V((lambda rb=rb: lambda e: e.tensor_tensor_scan(W5(Rre), rb, W5(Xre), 0.0, ALU.mult, ALU.add))(), r=["F0", "rho8"], w=["F2"])
            V((lambda rb=rb: lambda e: e.tensor_tensor_scan(W5(Rim), rb, W5(Xim), 0.0, ALU.mult, ALU.add))(), r=["F1", "rho8"], w=["F3"])
            sre = Sst[:, 2 * gl, 1:513]
            sim = Sst[:, 2 * gl + 1, 1:513]
            V((lambda Tc=Tc: lambda e: e.tensor_tensor(W5(Xre), W5(Rre), W5(Tc), ALU.mult))(), r=["F2", kTc], w=["F0"])
            V((lambda Ts=Ts: lambda e: e.tensor_tensor(W5(Tb), W5(Rim), W5(Ts), ALU.mult))(), r=["F3", kTs], w=["F8"])
            V((lambda sre=sre: lambda e: e.tensor_tensor(sre, W5(Xre), W5(Tb), ALU.subtract))(), r=["F0", "F8"], w=["Sst"])
            V((lambda Ts=Ts: lambda e: e.tensor_tensor(W5(Xim), W5(Rre), W5(Ts), ALU.mult))(), r=["F2", kTs], w=["F1"])
            V((lambda Tc=Tc: lambda e: e.tensor_tensor(W5(Tb), W5(Rim), W5(Tc), ALU.mult))(), r=["F3", kTc], w=["F8"])
            V((lambda sim=sim: lambda e: e.tensor_tensor(sim, W5(Xim), W5(Tb), ALU.add))(), r=["F1", "F8"], w=["Sst"])
            if after is not None:
                after(gl)

    def stage_B(j, mid_hook=None):
        un = u_all[:, j, :]
        for (kbase, usrc, ku, TZ, ktz, yacc, kacc) in (
                (64, urev, "urev", TZb, "TZb", yb_acc, "yb_acc"),
                (0, un, f"u{j}", TZf, "TZf", yf_acc, "yf_acc")):
            if kbase == 0 and mid_hook is not None:
                mid_hook()
            for k in range(8):
                py, kpy = ps[2 + k % 2], f"ps{2 + k % 2}"
                for k2 in range(k + 1):
                    T((lambda py=py, TZ=TZ, k=k, k2=k2, usrc=usrc: lambda e: e.matmul(
                        py, TZ[:, k - k2, :], usrc[:, k2 * 512:(k2 + 1) * 512], start=(k2 == 0), stop=False))(),
                      r=[ktz, ku], w=[kpy], inc=False)
                order = [(gl, ri) for gg in range(2) for ri in range(2) for gl in range(gg, 8, 2)]
                for n_, (gl, ri) in enumerate(order):
                    r4 = gl // 2
                    dz = DCz_re if ri == 0 else DCz_imn
                    T((lambda py=py, dz=dz, gl=gl, ri=ri, k=k, kbase=kbase, r4=r4, last=(n_ >= len(order) - 4): lambda e: e.matmul(
                        py[32 * r4:32 * r4 + 32, :], dz[kbase:kbase + 64, gl, k, :], Sst[kbase:kbase + 64, 2 * gl + ri, 0:512],
                        start=False, stop=last, tile_position=(kbase, 32 * r4)))(),
                      r=["DCz_re", "DCz_imn", "Sst"], w=[kpy], inc=(n_ == len(order) - 1))
                A((lambda py=py, yacc=yacc, k=k: lambda e: e.activation(yacc[:, k * 512:(k + 1) * 512], py, AF.Copy))(), r=[kpy], w=[kacc])
    def final_piece(j, hh):
        un = u_all[:, j, :]
        (pstep, _), (fstep, _) = yb_acc.ap
        if True:
            yfv = yf_acc.rearrange("p (k c) -> p k c", k=8)[:, :, 64 * hh:64 * hh + 64]
            ybv = AP(yb_acc.tensor, yb_acc.offset + fstep * (L - 1 - 64 * hh), [[pstep, 128], [-512 * fstep, 8], [-fstep, 64]])
            f0v = Hall[:, 0:1024].bitcast(F32).rearrange("p (k c) -> p k c", k=8)
            V((lambda yfv=yfv, ybv=ybv, f0v=f0v: lambda e: e.tensor_tensor(f0v, yfv, ybv, ALU.add))(),
              r=["yf_acc", "yb_acc"], w=["HF"])
            A((lambda hh=hh, un=un, f0v=f0v: lambda e: e.activation(
                un[:, hh * 512:(hh + 1) * 512].rearrange("p (c k) -> p k c", k=8), f0v, AF.Gelu_apprx_tanh))(), r=["HF"], w=[f"u{j}"])


    tab_xb(0); do_urev(0); tab_bkt(0)
    gen_tables(0, 0); gen_tables(0, 1)
    for j in range(8):
        tab_tz(j)
        dcz_ops = captured(tab_dcz, j)
        assert len(dcz_ops) == 8

        def after(gl, dcz_ops=dcz_ops):
            dcz_ops[gl]()
        stage_A(j, None, skip_tables=(0, 1), after=after)
        if j + 1 < 8:
            tab_xb(j + 1)
            tab_bkt(j + 1)
            gen_tables(j + 1, 0); gen_tables(j + 1, 1)
            stage_B(j, mid_hook=(lambda j=j: do_urev(j + 1)))
        else:
            stage_B(j)
        for hh in range(NT):
            final_piece(j, hh)

    if debug:
        dbg = nc.dram_tensor("dbg", [128, 8 * L], F32, kind="ExternalOutput").ap()
        dbg2 = nc.dram_tensor("dbg2", [128, 2 * L], F32, kind="ExternalOutput").ap()
        P.barrier()
        P.op("gpsimd", lambda e: e.dma_start(out=dbg, in_=u_all.rearrange("p a b -> p (a b)")), reads=[f"u{k}" for k in range(8)], dma_key="dbg", final=True)
        P.op("gpsimd", lambda e: e.dma_start(out=dbg2[:, 0:L], in_=yf_acc), reads=["yf_acc"], dma_key="dbg2", final=True)
        P.op("gpsimd", lambda e: e.dma_start(out=dbg2[:, L:2 * L], in_=yb_acc), reads=["yb_acc"], dma_key="dbg2", final=True)
        P.emit()
        return nc
    P.barrier()
    load(g_bc, g_bc_d, "g_bc"); load(fg_bc, fg_bc_d, "fg_bc")
    for pr4 in range(4):
        for hf in range(2):
            nb = 2 * pr4 + hf
            P.op("sync", (lambda nb=nb, pr4=pr4, hf=hf: lambda e: e.dma_start(out=Wout[:, :, nb * 128:(nb + 1) * 128],
                 in_=wsc[40 + pr4][:, hf * 1024:(hf + 1) * 1024].rearrange("p (a b) -> p a b", b=128)))(),
                 reads=[f"wsc{40 + pr4}_{hf}"], writes=["Wout"], dma_key=f"Wout{nb}")
    wcnt = [0]

    def wloadp(pr):
        slot = wcnt[0] % 4
        wcnt[0] += 1
        key = f"wslot{slot}"
        P.op("sync", (lambda slot=slot, pr=pr: lambda e: e.dma_start(out=wslot[slot], in_=wsc[pr].rearrange("p (a b) -> p a b", b=128)))(),
             reads=[f"wsc{pr}_0", f"wsc{pr}_1"], writes=[key], dma_key=key)
        return wslot[slot], key

    def proj(pz, kz, wk, hf, rhs_fn, rkeys, n=512, also_halo=None, halo_src=None):
        wt, kw = wk
        for kc in range(8):
            T((lambda kc=kc, wt=wt: lambda e: e.matmul(pz[:, 0:n], wt[:, hf * 8 + kc, :], rhs_fn(kc), start=(kc == 0), stop=(kc == 7)))(),
              r=[kw] + rkeys, w=[kz], inc=(kc == 7))
        if also_halo is not None:
            hz, c0 = also_halo
            for kc in range(8):
                T((lambda kc=kc, wt=wt, hs_=halo_src[0]: lambda e: e.matmul(hz[:, c0:c0 + 2], wt[:, hf * 8 + kc, :], hs_[:, kc, :], start=(kc == 0), stop=(kc == 7)))(),
                  r=[kw, halo_src[1]], w=["psH"], inc=(kc == 7))

    cv = F[0]
    cvh = cols_strided(cv, 0, 513, 2)
    xnh2 = H[5][:, 0:16].rearrange("p (a b) -> p a b", b=2)
    xbufs = [(xnT, xnh, "xnT", "xnh"), (xnT2, xnh2, "xnT2", "xnh2")]
    make_xn(0, True, *xbufs[0])
    pend = [None]
    for tt in range(NT):
        tsl = slice(tt * 512, (tt + 1) * 512)
        cxT, cxh, kxT, kxh = xbufs[tt % 2]
        xr = (lambda cxT: lambda kc: cxT[:, kc, :])(cxT)
        for f in range(8):
            yg = lambda kc, tsl=tsl: u_all[:, kc, tsl]
            wk = wloadp(3 * f)
            proj(ps[0], "ps0", wk, 0, xr, [kxT])
            A(lambda e: e.activation(H[2], ps[0], AF.Silu), r=["ps0"], w=["H2"])
            proj(ps[1], "ps1", wk, 1, yg, [f"u{k}" for k in range(8)])
            A(lambda e: e.activation(H[3], ps[1], AF.Sigmoid), r=["ps1"], w=["H3"])
            V((lambda f=f, tsl=tsl: lambda e: e.tensor_tensor(H[4], u_all[:, f, tsl], H[3], ALU.mult))(), r=[f"u{f}", "H3"], w=["H4"])
            V((lambda f=f: lambda e: e.tensor_tensor(ya[:, f, :], H[4], H[2], ALU.mult))(), r=["H4", "H2"], w=["ya"])
            wk = wloadp(3 * f + 1)
            proj(ps[2], "ps2", wk, 0, xr, [kxT], also_halo=(ps[6], 0), halo_src=(cxh, kxh))
            proj(ps[3], "ps3", wk, 1, xr, [kxT], also_halo=(ps[6], 2), halo_src=(cxh, kxh))
            A(lambda e: e.activation(F[1][:, 0:512], ps[2], AF.Copy), r=["ps2"], w=["F1"])
            A(lambda e: e.activation(F[2][:, 0:2], ps[6][:, 0:2], AF.Copy), r=["psH"], w=["F2"])
            V(lambda e: e.tensor_tensor(cv[:, 1:513], ps[3], F[1][:, 0:512], ALU.mult), r=["ps3", "F1"], w=["F0"])
            V(lambda e: e.tensor_tensor(cvh, ps[6][:, 2:4], F[2][:, 0:2], ALU.mult), r=["psH", "F2"], w=["F0"])
            V((lambda f=f: lambda e: e.tensor_scalar(F[3][:, 0:512], cv[:, 0:512], cw[:, f * 3:f * 3 + 1], cb[:, f:f + 1], ALU.mult, ALU.add))(),
              r=["F0", "cw", "cb"], w=["F3"])
            V((lambda f=f: lambda e: e.scalar_tensor_tensor(F[4][:, 0:512], cv[:, 1:513], cw[:, f * 3 + 1:f * 3 + 2], F[3][:, 0:512], ALU.mult, ALU.add))(),
              r=["F0", "F3"], w=["F4"])
            V((lambda f=f: lambda e: e.scalar_tensor_tensor(F[3][:, 0:512], cv[:, 2:514], cw[:, f * 3 + 2:f * 3 + 3], F[4][:, 0:512], ALU.mult, ALU.add))(),
              r=["F0", "F4"], w=["F3"])
            wk = wloadp(3 * f + 2)
            proj(ps[4], "ps4", wk, 0, xr, [kxT])
            A(lambda e: e.activation(F[6][:, 0:512], ps[4], AF.Copy), r=["ps4"], w=["F6"])
            V(lambda e: e.tensor_tensor(F[4][:, 0:512], F[6][:, 0:512], F[3][:, 0:512], ALU.mult), r=["F6", "F3"], w=["F4"])
            proj(ps[5], "ps5", wk, 1, xr, [kxT])
            A(lambda e: e.activation(F[5][:, 0:512], ps[5], AF.Silu), r=["ps5"], w=["F5"])
            V((lambda f=f: lambda e: e.tensor_tensor(yb[:, f, :], F[4][:, 0:512], F[5][:, 0:512], ALU.mult))(), r=["F4", "F5"], w=["yb"])
            if tt + 1 < NT and f % 2 == 1 and pend[0] is not None:
                pend[0]()
                pend[0] = None
            if tt + 1 < NT and f % 2 == 0:
                nxT, _, nkT, _ = xbufs[(tt + 1) % 2]
                pend[0] = make_xn_block(tt + 1, f // 2, nxT, nkT, defer=True)
        if pend[0] is not None:
            pend[0]()
            pend[0] = None
        halo_post = None
        if tt + 1 < NT:
            halo_post = make_xn(tt + 1, True, *xbufs[(tt + 1) % 2], blocks=False, defer_halo=True)
        xslots = {}
        for s4 in range(2):
            slot = xcnt[0] % 2
            xcnt[0] += 1
            xslots[s4] = slot
            load(xt[slot], x[tt * 512 + s4 * 128:tt * 512 + s4 * 128 + 128, :], f"xt{slot}")
        for dch in range(8):
            wk = wloadp(24 + 2 * dch)
            proj(ps[0], "ps0", wk, 0, lambda kc: ya[:, kc, :], ["ya"])
            proj(ps[1], "ps1", wk, 1, xr, [kxT])
            A(lambda e: e.activation(F[1][:, 0:512], ps[1], AF.Sigmoid), r=["ps1"], w=["F1"])
            V(lambda e: e.tensor_tensor(F[2][:, 0:512], ps[0], F[1][:, 0:512], ALU.mult), r=["ps0", "F1"], w=["F2"])
            wk = wloadp(25 + 2 * dch)
            proj(ps[2], "ps2", wk, 0, lambda kc: yb[:, kc, :], ["yb"])
            proj(ps[3], "ps3", wk, 1, xr, [kxT])
            A(lambda e: e.activation(F[3][:, 0:512], ps[3], AF.Sigmoid), r=["ps3"], w=["F3"])
            V(lambda e: e.tensor_tensor(F[4][:, 0:512], ps[2], F[3][:, 0:512], ALU.mult), r=["ps2", "F3"], w=["F4"])
            V((lambda dch=dch: lambda e: e.tensor_tensor(mg[:, dch, :], F[2][:, 0:512], F[4][:, 0:512], ALU.add))(), r=["F2", "F4"], w=["mg"])
        if halo_post is not None:
            halo_post()
        cx32 = cxT.rearrange("p a b -> p (a b)").bitcast(F32)
        xextra = {}
        for s4 in (2, 3):
            xb_ = cx32[:, (s4 - 2) * 1024:(s4 - 1) * 1024]
            P.op("sync", (lambda xb_=xb_, s4=s4, tt=tt: lambda e: e.dma_start(out=xb_, in_=x[tt * 512 + s4 * 128:tt * 512 + s4 * 128 + 128, :]))(),
                 writes=[kxT], dma_key=f"xx{s4}")
            xextra[s4] = xb_
        for s4 in range(4):
            r0 = tt * 512 + s4 * 128
            if s4 in xslots:
                slot = xslots[s4]
                xsrc = xt[slot]
                kx = f"xt{slot}"
            else:
                xsrc = xextra[s4]
                kx = kxT
            for half in range(2):
                bi = (2 + half) if s4 % 2 == 0 else (4 + half)
                pz = ps[bi]
                for dch in range(8):
                    T((lambda pz=pz, dch=dch, s4=s4, half=half: lambda e: e.matmul(pz, mg[:, dch, s4 * 128:(s4 + 1) * 128], Wout[:, dch, half * 512:(half + 1) * 512], start=(dch == 0), stop=(dch == 7)))(),
                      r=["mg", "Wout"], w=[f"ps{bi}"], inc=(dch == 7))
                V((lambda pz=pz, half=half, xsrc=xsrc: lambda e: e.tensor_tensor(hbuf[:, half * 512:(half + 1) * 512], pz, xsrc[:, half * 512:(half + 1) * 512], ALU.add))(),
                  r=[f"ps{bi}", kx], w=["hbuf"])
            A(lambda e: e.activation(F[6][:, 0:512], hbuf[:, 0:512], AF.Square, accum_out=ssq), r=["hbuf"], w=["ssq", "F6"])
            A(lambda e: e.activation(F[6][:, 0:512], hbuf[:, 512:1024], AF.Square, accum_out=std), r=["hbuf"], w=["std", "F6"])
            V(lambda e: e.tensor_tensor(ssq, ssq, std, ALU.add), r=["ssq", "std"], w=["ssq"])
            A(lambda e: e.activation(std, ssq, AF.Sqrt, bias=epsb, scale=1.0 / D), r=["ssq", "epsb"], w=["std"])
            V(lambda e: e.reciprocal(rstd, std), r=["std"], w=["rstd"])
            V(lambda e: e.scalar_tensor_tensor(otile, hbuf, rstd, fg_bc, ALU.mult, ALU.mult), r=["hbuf", "rstd", "fg_bc"], w=["otile"])
            P.op("sync", (lambda r0=r0: lambda e: e.dma_start(out=out[r0:r0 + 128, :], in_=otile))(), reads=["otile"], dma_key="outst", final=True)

    P.emit()
    return nc


_CACHE = {}


def _prep_shared(inp):
    f32 = np.float32
    sh = {}
    sh["w_in"] = np.ascontiguousarray(inp["w_in"][0], dtype=f32)
    sh["w_glu"] = np.ascontiguousarray(inp["w_glu"][0], dtype=f32)
    sh["w_a"] = np.ascontiguousarray(inp["w_branch_a"][0], dtype=f32)
    sh["w_b"] = np.ascontiguousarray(inp["w_branch_b"][0], dtype=f32)
    sh["w_out"] = np.ascontiguousarray(inp["w_out"][0], dtype=f32)
    sh["g_bc"] = np.ascontiguousarray(np.broadcast_to(inp["norm_g"][0][None, :], (128, D)), dtype=f32)
    sh["fg_bc"] = np.ascontiguousarray(np.broadcast_to(inp["final_g"][None, :], (128, D)), dtype=f32)
    sh["dcol"] = np.ascontiguousarray(inp["ssm_d"][0].reshape(8, 128).T, dtype=f32)
    sh["cw"] = np.ascontiguousarray(inp["conv_w"][0].reshape(3, 8, 128).transpose(2, 1, 0).reshape(128, 24), dtype=f32)
    sh["cb"] = np.ascontiguousarray(inp["conv_b"][0].reshape(8, 128).T, dtype=f32)
    sh["lre"] = np.ascontiguousarray(inp["lam_re"][0].transpose(0, 2, 1).reshape(128, 64), dtype=f32)
    sh["lim"] = np.ascontiguousarray(inp["lam_im"][0].transpose(0, 2, 1).reshape(128, 64), dtype=f32)
    sh["ldt"] = np.ascontiguousarray(np.broadcast_to(inp["log_dt"][0][:, None, :], (2, 64, 64)).reshape(128, 64), dtype=f32)
    for nm, key in (("b_re", "ssm_b_re"), ("b_im", "ssm_b_im")):
        b = np.asarray(inp[key][0], dtype=f32)
        sh[nm] = np.ascontiguousarray(b.transpose(0, 2, 1, 3).reshape(128, 1024))
    for nm, key in (("c_re", "ssm_c_re"), ("c_im", "ssm_c_im")):
        c = np.asarray(inp[key][0], dtype=f32)
        sh[nm] = np.ascontiguousarray(c.transpose(0, 3, 1, 2).reshape(128, 1024))
    sh["ident"] = np.eye(128, dtype=f32)
    sh["iidx"] = np.ascontiguousarray(np.broadcast_to(np.arange(512, dtype=f32)[None, :], (128, 512)))
    sh["jR"] = np.ascontiguousarray(np.broadcast_to(np.tile(np.arange(7, -1, -1).astype(f32), 64)[None, :], (128, 512)))
    sh["jD"] = np.ascontiguousarray(np.broadcast_to(np.tile(np.arange(1, 9).astype(f32), 64)[None, :], (128, 512)))
    bd = np.zeros((128, 128), dtype=f32)
    for i in range(8):
        bd[i * 16:(i + 1) * 16, i * 16:(i + 1) * 16] = 1.0
    sh["bdmask"] = bd
    pm = np.zeros((128, 2), dtype=f32)
    for p in range(128):
        pm[p, (p // 16) % 2] = 1.0
    sh["parmask"] = pm
    return sh


def kernel(**inputs):
    inp = {k: np.asarray(v) for k, v in inputs.items()}
    if "nc" not in _CACHE:
        _CACHE["nc"] = build_program()
    nc = _CACHE["nc"]
    sh = _prep_shared(inp)
    x = np.asarray(inp["x"], dtype=np.float32)
    in_maps = []
    for c in range(8):
        m = dict(sh)
        m["x"] = np.ascontiguousarray(x[c])
        in_maps.append(m)
    res = run_bass_kernel_spmd(nc, in_maps, core_ids=list(range(8)))
    return np.stack([np.asarray(r["out"], dtype=np.float32) for r in res.results], axis=0)
```

```python
import math
import numpy as np
import concourse.bass as bass
import concourse.mybir as mybir
from concourse.bass_utils import run_bass_kernel_spmd
from concourse.ap import AP

F32 = mybir.dt.float32
BF16 = mybir.dt.bfloat16
I32 = mybir.dt.int32
AF = mybir.ActivationFunctionType
ALU = mybir.AluOpType

ENGS = ("tensor", "vector", "scalar", "gpsimd", "sync")
L = 4096
D = 1024
NT = 8
EPS = 1e-6


class Prog:
    def __init__(self, nc):
        self.nc = nc
        self.ops = {e: [] for e in ENGS}
        self.esem = {e: nc.alloc_semaphore(name=f"es_{e}") for e in ENGS}
        self.ecnt = {e: 0 for e in ENGS}
        self.dsem = {}
        self.last_w = {}
        self.readers = {}
        self.waited = {e: {} for e in ENGS}
        self.final_tokens = []

    def _need(self, eng, tok, waits):
        if tok is None:
            return
        sem, val, weng = tok
        if weng == eng and eng == "tensor":
            return
        key = id(sem)
        if self.waited[eng].get(key, 0) >= val:
            return
        self.waited[eng][key] = val
        waits.append((sem, val))

    def barrier(self):
        toks = [(self.esem[e], self.ecnt[e], e) for e in ENGS if self.ecnt[e] > 0]
        toks += [(ent[0], ent[1], "dma") for ent in self.dsem.values()]
        self.pending = {e: list(toks) for e in ENGS}

    def op(self, eng, fn, reads=(), writes=(), dma_key=None, final=False, inc=True):
        waits = []
        for t in getattr(self, "pending", {}).pop(eng, []):
            if t[2] != eng:
                self._need(eng, t, waits)
        for r in reads:
            self._need(eng, self.last_w.get(r), waits)
        for w in writes:
            self._need(eng, self.last_w.get(w), waits)
            for t in self.readers.get(w, ()):
                self._need(eng, t, waits)
        if dma_key is not None:
            if dma_key not in self.dsem:
                self.dsem[dma_key] = [self.nc.alloc_semaphore(name=f"ds_{len(self.dsem)}"), 0]
            ent = self.dsem[dma_key]
            ent[1] += 16
            tok = (ent[0], ent[1], "dma")
            inc = (ent[0], 16)
        elif not inc:
            tok = (self.esem[eng], self.ecnt[eng] + 1, eng)
            inc = None
        else:
            self.ecnt[eng] += 1
            tok = (self.esem[eng], self.ecnt[eng], eng)
            inc = (self.esem[eng], 1)
        for r in reads:
            self.readers.setdefault(r, []).append(tok)
        for w in writes:
            self.last_w[w] = tok
            self.readers[w] = []
        self.ops[eng].append((fn, waits, inc))
        if final:
            self.final_tokens.append(tok)
        return tok

    def emit(self):
        nc = self.nc
        finals = self.final_tokens
        with nc.Block() as block:
            def run(engname, e):
                for fn, waits, inc in self.ops[engname]:
                    for sem, val in waits:
                        e.wait_ge(sem, val)
                    ins = fn(e)
                    if inc is not None:
                        ins.then_inc(inc[0], inc[1])
                if engname == "sync":
                    for sem, val, _ in finals:
                        e.wait_ge(sem, val)

            @block.tensor
            def _(e):
                run("tensor", e)

            @block.vector
            def _(e):
                run("vector", e)

            @block.scalar
            def _(e):
                run("scalar", e)

            @block.gpsimd
            def _(e):
                run("gpsimd", e)

            @block.sync
            def _(e):
                run("sync", e)


def rev_last(ap2d):
    (ps, pc), (fs, fc) = ap2d.ap
    return AP(ap2d.tensor, ap2d.offset + fs * (fc - 1), [[ps, pc], [-fs, fc]])


def cols_strided(ap2d, start, step, n):
    (ps, pc), (fs, fc) = ap2d.ap
    return AP(ap2d.tensor, ap2d.offset + fs * start, [[ps, pc], [fs * step, n]])


def build_program(debug=False):
    nc = bass.Bass("TRN2", target_bir_lowering=False)

    def din(name, shape, dt=F32):
        return nc.dram_tensor(name, list(shape), dt, kind="ExternalInput").ap()

    x = din("x", [L, D])
    w_in = din("w_in", [D, 8 * D])
    w_glu = din("w_glu", [D, D]); w_a = din("w_a", [D, D]); w_b = din("w_b", [D, D]); w_out = din("w_out", [D, D])
    g_bc_d = din("g_bc", [128, D]); fg_bc_d = din("fg_bc", [128, D])
    dcol_d = din("dcol", [128, 8]); cw_d = din("cw", [128, 24]); cb_d = din("cb", [128, 8])
    lre_d = din("lre", [128, 64]); lim_d = din("lim", [128, 64]); ldt_d = din("ldt", [128, 64])
    bre_d = din("b_re", [128, 1024]); bim_d = din("b_im", [128, 1024])
    cre_d = din("c_re", [128, 1024]); cim_d = din("c_im", [128, 1024])
    ident_d = din("ident", [128, 128]); iidx_d = din("iidx", [128, 512])
    jR_d = din("jR", [128, 512]); jD_d = din("jD", [128, 512])
    bdm_d = din("bdmask", [128, 128]); pm_d = din("parmask", [128, 2])
    out = nc.dram_tensor("out", [L, D], F32, kind="ExternalOutput").ap()
    wsc = nc.dram_tensor("wsc", [44, 128, 2048], BF16, kind="Internal").ap()

    P = Prog(nc)

    def sb(name, shape, dt=F32):
        return nc.alloc_sbuf_tensor(name, list(shape), dt).ap()

    u_all = sb("u_all", [128, 8, L], BF16)
    xnT = sb("xnT", [128, 8, 512], BF16)
    xnh = sb("xnh", [128, 8, 2], BF16)
    identb = sb("identb", [128, 128], BF16); identf = sb("identf", [128, 128])
    bdmask = sb("bdmask_s", [128, 128]); parmask = sb("parmask_s", [128, 2])
    iidx = sb("iidx_s", [128, 512])
    dcol = sb("dcol_s", [128, 8]); cw = sb("cw_s", [128, 24]); cb = sb("cb_s", [128, 8])
    thn = sb("thn", [128, 64]); thn8 = sb("thn8", [128, 64]); rho8 = sb("rho8", [128, 64])
    coef_re = sb("coef_re", [128, 64]); coef_im = sb("coef_im", [128, 64])
    sm = [sb(f"sm{i}", [128, 64]) for i in range(8)]
    ssq = sb("ssq", [128, 1]); std = sb("std", [128, 1]); rstd = sb("rstd", [128, 1])
    epsb = sb("epsb", [128, 1]); halfpi = sb("halfpi", [128, 1])
    F = [sb(f"F{i}", [128, 514]) for i in range(9)]
    FI = sb("FI", [128, 512], I32)
    smi = FI
    Hall = sb("Hall", [128, 6 * 512], BF16)
    H = [Hall[:, i * 512:(i + 1) * 512] for i in range(6)]
    ARENA = 49664
    arena = sb("arena", [128, ARENA], BF16)

    def carve(off_el, shape, dt=BF16):
        n = int(np.prod(shape[1:]))
        if dt == F32:
            assert off_el + 2 * n <= ARENA
            a = arena[:, off_el:off_el + 2 * n].bitcast(F32)
        else:
            assert off_el + n <= ARENA
            a = arena[:, off_el:off_el + n]
        if len(shape) == 3:
            a = a.rearrange("p (a b) -> p a b", b=shape[2])
        if len(shape) == 4:
            a = a.rearrange("p (a b c) -> p a b c", b=shape[2], c=shape[3])
        return a

    xt = [carve(40960 + 2048 * i, [128, D], F32) for i in range(2)]
    xs = carve(45056, [128, D])
    g_bc = carve(46080, [128, D], F32)
    xh = carve(48128, [128, 512], F32)
    Wu = carve(0, [128, 64, 128])
    Braw_re = carve(0, [128, 1024], F32); Braw_im = carve(2048, [128, 1024], F32)
    Cp_re = carve(4096, [128, 1024], F32); Cp_im = carve(6144, [128, 1024], F32)
    pwR_re = carve(8192, [128, 512], F32); pwR_im = carve(9216, [128, 512], F32)
    pwD_re = carve(10240, [128, 512], F32); pwD_im = carve(11264, [128, 512], F32)
    T1 = carve(12288, [128, 1024], F32); T2 = carve(14336, [128, 1024], F32)
    T3 = carve(16384, [128, 1024], F32); T4 = carve(18432, [128, 1024], F32)
    urev = carve(20480, [128, L])
    yf_acc = carve(24576, [128, L]); yb_acc = carve(28672, [128, L])
    BKT = {(par, ri): carve(32768 + 1024 * (2 * par + ri), [128, 8, 128]) for par in range(2) for ri in range(2)}
    DCz_re = carve(36864, [128, 8, 8, 32]); DCz_imn = carve(38912, [128, 8, 8, 32])
    Sst = carve(40960, [128, 16, 514])
    Wout = carve(0, [128, 8, 1024])
    ya = carve(8192, [128, 8, 512]); yb = carve(12288, [128, 8, 512]); mg = carve(16384, [128, 8, 512])
    wslot = [carve(20480 + 2048 * i, [128, 16, 128]) for i in range(2)] + [carve(36864 + 2048 * i, [128, 16, 128]) for i in range(2)]
    xnT2 = carve(32768, [128, 8, 512])
    otile = carve(24576, [128, D], F32); hbuf = carve(26624, [128, D], F32)
    fg_bc = carve(28672, [128, D], F32)
    xh = carve(30720, [128, D], F32)[0:2, :]
    wstage = [carve(20480 + 1024 * i, [128, 8, 128]) for i in range(6)]
    wstage_f = [carve(8192 + 2048 * i, [128, 8, 128], F32) for i in range(6)]
    TZf = Hall[:, 2 * 512:4 * 512].rearrange("p (a b) -> p a b", b=128)
    TZb = Hall[:, 4 * 512:6 * 512].rearrange("p (a b) -> p a b", b=128)

    ps = [nc.alloc_psum_tensor(f"ps{i}", [128, 512], F32).ap() for i in range(7)]
    psT = nc.alloc_psum_tensor("psT", [128, 8, 128], BF16).ap()

    TWO_PI = 2.0 * math.pi

    cap = [None]

    def _op(eng, fn, r, w, inc=True):
        if cap[0] is not None:
            cap[0].append(lambda: P.op(eng, fn, reads=r, writes=w, inc=inc))
            return None
        return P.op(eng, fn, reads=r, writes=w, inc=inc)

    def captured(f, *args):
        cap[0] = []
        f(*args)
        ops = cap[0]
        cap[0] = None
        return ops

    def V(fn, r=(), w=()):
        return _op("vector", fn, r, w)

    def A(fn, r=(), w=()):
        return _op("scalar", fn, r, w)

    def G(fn, r=(), w=()):
        return _op("gpsimd", fn, r, w)

    def T(fn, r=(), w=(), inc=True):
        return _op("tensor", fn, r, w, inc)

    def load(dst, src, key, eng="sync", extra_w=()):
        return P.op(eng, lambda e: e.dma_start(out=dst, in_=src), writes=[key] + list(extra_w), dma_key=key)

    load(g_bc, g_bc_d, "g_bc")
    load(identb, ident_d, "identb", eng="gpsimd"); load(identf, ident_d, "identf")
    load(bdmask, bdm_d, "bdmask"); load(parmask, pm_d, "parmask")
    load(iidx, iidx_d, "iidx")
    load(dcol, dcol_d, "dcol"); load(cw, cw_d, "cw"); load(cb, cb_d, "cb")
    V(lambda e: e.memset(epsb, EPS), w=["epsb"])
    V(lambda e: e.memset(halfpi, math.pi / 2), w=["halfpi"])
    w_in_v = w_in.rearrange("(kc p) n -> p kc n", p=128)
    srcs = [("Wu", w_in_v, i, None) for i in range(8)]
    w_glu_v = w_glu.rearrange("(kc p) n -> p kc n", p=128); w_a_v = w_a.rearrange("(kc p) n -> p kc n", p=128)
    w_b_v = w_b.rearrange("(kc p) n -> p kc n", p=128); w_out_v = w_out.rearrange("(kc p) n -> p kc n", p=128)
    for f in range(8):
        srcs += [("sc", w_in_v, 8 + f, (3 * f, 0)), ("sc", w_glu_v, f, (3 * f, 1)),
                 ("sc", w_in_v, 16 + f, (3 * f + 1, 0)), ("sc", w_in_v, 32 + f, (3 * f + 1, 1)),
                 ("sc", w_in_v, 24 + f, (3 * f + 2, 0)), ("sc", w_in_v, 40 + f, (3 * f + 2, 1))]
    for dch in range(8):
        srcs += [("sc", w_a_v, dch, (24 + 2 * dch, 0)), ("sc", w_in_v, 48 + dch, (24 + 2 * dch, 1)),
                 ("sc", w_b_v, dch, (25 + 2 * dch, 0)), ("sc", w_in_v, 56 + dch, (25 + 2 * dch, 1))]
    for nb in range(8):
        srcs.append(("sc", w_out_v, nb, (40 + nb // 2, nb % 2)))

    def conv_chunk(n_):
        kind, wv, cbk, dst = srcs[n_]
        sl = n_ % 6
        sf = wstage_f[sl]
        kf = f"wstf{sl}"
        P.op("sync", (lambda sf=sf, wv=wv, cbk=cbk: lambda e: e.dma_start(out=sf, in_=wv[:, :, cbk * 128:(cbk + 1) * 128]))(),
             writes=[kf], dma_key=kf)
        if kind == "Wu":
            V((lambda sf=sf, cbk=cbk: lambda e: e.tensor_copy(Wu[:, cbk * 8:(cbk + 1) * 8, :], sf))(), r=[kf], w=[f"Wu{cbk}"])
        else:
            st = wstage[sl]
            kb = f"wstb{sl}"
            pr, hf = dst
            (G if n_ % 4 == 3 else V)((lambda sf=sf, st=st: lambda e: e.tensor_copy(st, sf))(), r=[kf], w=[kb])
            P.op("gpsimd", (lambda st=st, pr=pr, hf=hf: lambda e: e.dma_start(
                out=wsc[pr][:, hf * 1024:(hf + 1) * 1024].rearrange("p (a b) -> p a b", b=128), in_=st))(),
                 reads=[kb], writes=[f"wsc{pr}_{hf}"], dma_key=kb + "_st")

    for n_ in range(8):
        conv_chunk(n_)

    xcnt = [0]
    xdma = ["sync"]

    def norm_rows(src_rows_ap, nrows, dst_xnT_ap, dst_key="xnT", defer=False):
        slot = xcnt[0] % 2
        xcnt[0] += 1
        xb = xt[slot][0:nrows, :]
        kx = f"xt{slot}"
        load(xb, src_rows_ap, kx, eng=xdma[0])
        return finish_norm(xb, kx, nrows, dst_xnT_ap, dst_key=dst_key, defer=defer)

    def finish_norm(xb, kx, nrows, dst, dst_key="xnT", defer=False):
        A(lambda e: e.activation(F[7][0:nrows, 0:512], xb[:, 0:512], AF.Square, accum_out=ssq[0:nrows, :]),
          r=[kx], w=["ssq", "F7"])
        A(lambda e: e.activation(F[7][0:nrows, 0:512], xb[:, 512:1024], AF.Square, accum_out=std[0:nrows, :]),
          r=[kx], w=["std", "F7"])
        V(lambda e: e.tensor_tensor(ssq[0:nrows, :], ssq[0:nrows, :], std[0:nrows, :], ALU.add), r=["ssq", "std"], w=["ssq"])
        A(lambda e: e.activation(std[0:nrows, :], ssq[0:nrows, :], AF.Sqrt, bias=epsb[0:nrows, :], scale=1.0 / D),
          r=["ssq", "epsb"], w=["std"])
        V(lambda e: e.reciprocal(rstd[0:nrows, :], std[0:nrows, :]), r=["std"], w=["rstd"])
        V(lambda e: e.scalar_tensor_tensor(xs[0:nrows, :], xb, rstd[0:nrows, :], g_bc[0:nrows, :], ALU.mult, ALU.mult),
          r=[kx, "rstd", "g_bc"], w=["xs"])

        def post():
            for kc in range(8):
                T((lambda kc=kc: lambda e: e.transpose(psT[:, kc, 0:nrows], xs[0:nrows, kc * 128:(kc + 1) * 128], identb[0:nrows, 0:nrows]))(),
                  r=["xs", "identb"], w=["psT"], inc=(kc == 7))
            A(lambda e: e.activation(dst, psT[:, :, 0:nrows], AF.Copy), r=["psT"], w=[dst_key])
        if defer:
            return post
        post()
        return None

    def make_xn_block(tt, s, dstT, kT, defer=False):
        r0 = tt * 512 + s * 128
        return norm_rows(x[r0:r0 + 128, :], 128, dstT[:, :, s * 128:(s + 1) * 128], dst_key=kT, defer=defer)

    def make_xn(tt, with_halo, dstT=None, dsth=None, kT="xnT", kh="xnh", blocks=True, defer_halo=False):
        dstT = xnT if dstT is None else dstT
        dsth = xnh if dsth is None else dsth
        if blocks:
            for s in range(4):
                make_xn_block(tt, s, dstT, kT)
        if with_halo:
            V(lambda e: e.memset(xh, 0.0), w=["xh"])
            if tt > 0:
                load(xh[0:1, :], x[tt * 512 - 1:tt * 512, :], "xh", eng=xdma[0])
            if tt < NT - 1:
                load(xh[1:2, :], x[(tt + 1) * 512:(tt + 1) * 512 + 1, :], "xh", eng=xdma[0])
            return finish_norm(xh, "xh", 2, dsth, dst_key=kh, defer=defer_halo)
        return None

    for tt in range(NT):
        make_xn(tt, False)
        for n_ in range(8 + 11 * tt, 8 + 11 * (tt + 1)):
            conv_chunk(n_)
        for j in range(8):
            pb = j % 2
            for kc in range(8):
                T((lambda j=j, kc=kc, pb=pb: lambda e: e.matmul(ps[pb], Wu[:, j * 8 + kc, :], xnT[:, kc, :], start=(kc == 0), stop=(kc == 7)))(),
                  r=[f"Wu{j}", "xnT"], w=[f"ps{pb}"], inc=(kc == 7))
            V((lambda j=j, pb=pb, tt=tt: lambda e: e.tensor_copy(
                u_all[:, j, :].rearrange("p (k c) -> p k c", k=8)[:, :, 64 * tt:64 * tt + 64],
                ps[pb].rearrange("p (c k) -> p k c", k=8)))(),
              r=[f"ps{pb}"], w=[f"u{j}"])

    P.barrier()
    lre, lim, dtt, lr, th, t0, t1, t2 = sm
    load(lre, lre_d, "lre"); load(lim, lim_d, "lim"); load(dtt, ldt_d, "ldt")
    load(Braw_re, bre_d, "Braw_re"); load(Braw_im, bim_d, "Braw_im")
    load(T1, cre_d, "T1"); load(T2, cim_d, "T2")
    load(T3[:, 0:512], jR_d, "T3"); load(T4[:, 0:512], jD_d, "T4")
    A(lambda e: e.activation(dtt, dtt, AF.Exp), r=["ldt"], w=["dtt"])
    V(lambda e: e.tensor_tensor(lr, lre, dtt, ALU.mult), r=["lre", "dtt"], w=["lr"])
    V(lambda e: e.tensor_tensor(th, lim, dtt, ALU.mult), r=["lim", "dtt"], w=["th"])
    V(lambda e: e.tensor_scalar(thn, th, 1.0 / TWO_PI, None, ALU.mult), r=["th"], w=["thn"])
    A(lambda e: e.activation(rho8, lr, AF.Exp, scale=8.0), r=["lr"], w=["rho8"])

    def frac_op(dst, src, n, kd, ks):
        V(lambda e: e.tensor_copy(smi[:, 0:n], src), r=[ks], w=["FI"])
        V(lambda e: e.tensor_copy(F[8][:, 0:n], smi[:, 0:n]), r=["FI"], w=["F8"])
        V(lambda e: e.tensor_tensor(dst, src, F[8][:, 0:n], ALU.subtract), r=[ks, "F8"], w=[kd])

    def sincos(sin_dst, cos_dst, frac_src, tmp, ks, kk_s, kk_c, kt):
        A(lambda e: e.activation(sin_dst, frac_src, AF.Sin, scale=TWO_PI), r=[ks], w=[kk_s])
        A(lambda e: e.activation(tmp, frac_src, AF.Abs), r=[ks], w=[kt])
        A(lambda e: e.activation(cos_dst, tmp, AF.Sin, bias=halfpi, scale=-TWO_PI), r=[kt, "halfpi"], w=[kk_c])

    V(lambda e: e.tensor_scalar(thn8, thn, 8.0, None, ALU.mult), r=["thn"], w=["thn8"])
    frac_op(thn8, thn8, 64, "thn8", "thn8")
    rho1 = sm[4]
    A(lambda e: e.activation(rho1, lr, AF.Exp), r=["lr", "thn"], w=["rho1"])
    frac_op(t0, thn, 64, "t0", "thn")
    sincos(t1, t2, t0, F[6][:, 0:64], "t0", "t1", "t2", "F6")
    V(lambda e: e.tensor_tensor(t1, t1, rho1, ALU.mult), r=["t1", "rho1"], w=["t1"])
    V(lambda e: e.tensor_tensor(t2, t2, rho1, ALU.mult), r=["t2", "rho1"], w=["t2"])
    V(lambda e: e.tensor_scalar(t2, t2, -1.0, None, ALU.add), r=["t2"], w=["t2"])
    V(lambda e: e.tensor_tensor(t0, lre, lre, ALU.mult), r=["lre"], w=["t0"])
    V(lambda e: e.tensor_tensor(dtt, lim, lim, ALU.mult), r=["lim"], w=["dtt"])
    V(lambda e: e.tensor_tensor(t0, t0, dtt, ALU.add), r=["t0", "dtt"], w=["t0"])
    V(lambda e: e.reciprocal(dtt, t0), r=["t0"], w=["dtt"])
    V(lambda e: e.tensor_tensor(t0, t2, lre, ALU.mult), r=["t2", "lre"], w=["t0"])
    V(lambda e: e.tensor_tensor(rho1, t1, lim, ALU.mult), r=["t1", "lim"], w=["rho1"])
    V(lambda e: e.tensor_tensor(t0, t0, rho1, ALU.add), r=["t0", "rho1"], w=["t0"])
    V(lambda e: e.tensor_tensor(coef_re, t0, dtt, ALU.mult), r=["t0", "dtt"], w=["coef_re"])
    V(lambda e: e.tensor_tensor(t0, t1, lre, ALU.mult), r=["t1", "lre"], w=["t0"])
    V(lambda e: e.tensor_tensor(rho1, t2, lim, ALU.mult), r=["t2", "lim"], w=["rho1"])
    V(lambda e: e.tensor_tensor(t0, t0, rho1, ALU.subtract), r=["t0", "rho1"], w=["t0"])
    V(lambda e: e.tensor_tensor(coef_im, t0, dtt, ALU.mult), r=["t0", "dtt"], w=["coef_im"])

    c3 = lambda a: a.rearrange("p (g h) -> p g h", h=16)
    bch = lambda a: a.unsqueeze(2).broadcast_to([128, 64, 16])
    V(lambda e: e.tensor_tensor(c3(Cp_re), c3(T1), bch(coef_re), ALU.mult), r=["T1", "coef_re"], w=["Cp_re"])
    V(lambda e: e.tensor_tensor(c3(Cp_im), c3(T2), bch(coef_im), ALU.mult), r=["T2", "coef_im"], w=["Cp_im"])
    V(lambda e: e.tensor_tensor(Cp_re, Cp_re, Cp_im, ALU.subtract), r=["Cp_re", "Cp_im"], w=["Cp_re"])
    V(lambda e: e.tensor_tensor(c3(Cp_im), c3(T1), bch(coef_im), ALU.mult), r=["T1", "coef_im"], w=["Cp_im"])
    V(lambda e: e.tensor_tensor(c3(T1), c3(T2), bch(coef_re), ALU.mult), r=["T2", "coef_re", "Cp_re"], w=["T1"])
    V(lambda e: e.tensor_tensor(Cp_im, Cp_im, T1, ALU.add), r=["Cp_im", "T1"], w=["Cp_im"])
    V(lambda e: e.tensor_scalar(Cp_im, Cp_im, -1.0, None, ALU.mult), r=["Cp_im"], w=["Cp_im"])

    g8 = lambda a: a.rearrange("p (g k) -> p g k", k=8)
    bck = lambda a: a.unsqueeze(2).broadcast_to([128, 64, 8])
    for (jt, kj, p_re, p_im, kre, kim) in ((T3, "T3", pwR_re, pwR_im, "pwR_re", "pwR_im"), (T4, "T4", pwD_re, pwD_im, "pwD_re", "pwD_im")):
        jv = jt[:, 0:512]
        mag = jt[:, 512:1024]
        V((lambda jv=jv, mag=mag: lambda e: e.tensor_tensor(g8(mag), g8(jv), bck(lr), ALU.mult))(), r=[kj, "lr"], w=[kj + "m"])
        A((lambda mag=mag: lambda e: e.activation(mag, mag, AF.Exp))(), r=[kj + "m"], w=[kj + "m"])
        V((lambda jv=jv: lambda e: e.tensor_tensor(g8(F[0][:, 0:512]), g8(jv), bck(thn), ALU.mult))(), r=[kj, "thn"], w=["F0"])
        frac_op(F[0][:, 0:512], F[0][:, 0:512], 512, "F0", "F0")
        sincos(F[1][:, 0:512], F[2][:, 0:512], F[0][:, 0:512], F[3][:, 0:512], "F0", "F1", "F2", "F3")
        V((lambda p_re=p_re, mag=mag: lambda e: e.tensor_tensor(p_re, mag, F[2][:, 0:512], ALU.mult))(), r=[kj + "m", "F2"], w=[kre])
        V((lambda p_im=p_im, mag=mag: lambda e: e.tensor_tensor(p_im, mag, F[1][:, 0:512], ALU.mult))(), r=[kj + "m", "F1"], w=[kim])

    V(lambda e: e.memset(DCz_re, 0.0), w=["DCz_re"])
    V(lambda e: e.memset(DCz_imn, 0.0), w=["DCz_imn"])
    V(lambda e: e.memset(Sst, 0.0), w=["Sst"])

    Xre, Xim, Rre, Rim, Tc, Ts, Fr, Ta, Tb = F[0], F[1], F[2], F[3], F[4], F[5], F[6], F[7], F[8]
    W5 = lambda a: a[:, 0:512]
    xnT_f = xnT.rearrange("p a b -> p (a b)").bitcast(F32)
    tabs = [(Tc, Ts, Fr, Ta, "F4", "F5", "F6", "F7"),
            (xnT_f[:, 0:512], xnT_f[:, 512:1024], xnT_f[:, 1024:1536], xnT_f[:, 1536:2048], "X4", "X5", "X6", "X7")]
    kgh = lambda a: a.rearrange("p (k g h) -> p k g h", k=8, g=8)
    def do_urev(j):
        un = u_all[:, j, :]
        V((lambda un=un: lambda e: e.tensor_copy(urev, rev_last(un)))(), r=[f"u{j}"], w=["urev"])
    def tab_xb(j):
        un = u_all[:, j, :]
        def bv(a, j=j):
            return a.rearrange("p (g h) -> p g h", h=16)[:, j * 8:(j + 1) * 8, :].unsqueeze(1).broadcast_to([128, 8, 8, 16])

        def pv(a, j=j):
            return a.rearrange("p (g k) -> p k g", k=8)[:, :, j * 8:(j + 1) * 8].unsqueeze(3).broadcast_to([128, 8, 8, 16])

        V(lambda e, bv=bv, pv=pv: e.tensor_tensor(kgh(T1), bv(Braw_re), pv(pwR_re), ALU.mult), r=["Braw_re", "pwR_re", "TZgen", "BKTgen"], w=["T1"])
        V(lambda e, bv=bv, pv=pv: e.tensor_tensor(kgh(T3), bv(Braw_im), pv(pwR_im), ALU.mult), r=["Braw_im", "pwR_im"], w=["T3"])
        V(lambda e: e.tensor_tensor(T1, T1, T3, ALU.subtract), r=["T1", "T3"], w=["T1"])
        V(lambda e, bv=bv, pv=pv: e.tensor_tensor(kgh(T2), bv(Braw_re), pv(pwR_im), ALU.mult), r=["Braw_re", "pwR_im", "TZgen", "BKTgen"], w=["T2"])
        V(lambda e, bv=bv, pv=pv: e.tensor_tensor(kgh(T3), bv(Braw_im), pv(pwR_re), ALU.mult), r=["Braw_im", "pwR_re"], w=["T3"])
        V(lambda e: e.tensor_tensor(T2, T2, T3, ALU.add), r=["T2", "T3"], w=["T2"])
    def tab_dcz(j):
        un = u_all[:, j, :]
        def cv4(a, j=j):
            return a.rearrange("p (g h) -> p g h", h=16)[:, j * 8:(j + 1) * 8, :].unsqueeze(2).broadcast_to([128, 8, 8, 16])

        def pd4(a, j=j):
            return a.rearrange("p (g k) -> p g k", k=8)[:, j * 8:(j + 1) * 8, :].unsqueeze(3).broadcast_to([128, 8, 8, 16])

        gkh = lambda a: a.rearrange("p (g k h) -> p g k h", g=8, k=8)
        V(lambda e, cv4=cv4, pd4=pd4: e.tensor_tensor(gkh(T3), cv4(Cp_re), pd4(pwD_re), ALU.mult), r=["Cp_re", "pwD_re", "T3"], w=["T3"])
        V(lambda e, cv4=cv4, pd4=pd4: e.tensor_tensor(gkh(T4), cv4(Cp_im), pd4(pwD_im), ALU.mult), r=["Cp_im", "pwD_im"], w=["T4"])
        for par in range(2):
            dv = DCz_re.rearrange("p (i e) k c -> p i e k c", e=2)[:, :, par, :, par * 16:(par + 1) * 16]
            a3 = gkh(T3).rearrange("p (i e) k h -> p i e k h", e=2)[:, :, par, :, :]
            a4 = gkh(T4).rearrange("p (i e) k h -> p i e k h", e=2)[:, :, par, :, :]
            V((lambda dv=dv, a3=a3, a4=a4: lambda e: e.tensor_tensor(dv, a3, a4, ALU.add))(), r=["T3", "T4", "stageB"], w=["DCz_re"])
        V(lambda e, cv4=cv4, pd4=pd4: e.tensor_tensor(gkh(T3), cv4(Cp_im), pd4(pwD_re), ALU.mult), r=["Cp_im", "pwD_re", "DCz_re"], w=["T3"])
        V(lambda e, cv4=cv4, pd4=pd4: e.tensor_tensor(gkh(T4), cv4(Cp_re), pd4(pwD_im), ALU.mult), r=["Cp_re", "pwD_im", "DCz_re"], w=["T4"])
        for par in range(2):
            dv = DCz_imn.rearrange("p (i e) k c -> p i e k c", e=2)[:, :, par, :, par * 16:(par + 1) * 16]
            a3 = gkh(T3).rearrange("p (i e) k h -> p i e k h", e=2)[:, :, par, :, :]
            a4 = gkh(T4).rearrange("p (i e) k h -> p i e k h", e=2)[:, :, par, :, :]
            V((lambda dv=dv, a3=a3, a4=a4: lambda e: e.tensor_tensor(dv, a3, a4, ALU.subtract))(), r=["T3", "T4", "stageB"], w=["DCz_imn"])
    def tab_bkt(j):
        un = u_all[:, j, :]
        for ri, (XBt, kxb) in enumerate(((T1, "T1"), (T2, "T2"))):
            for kh in range(2):
                pz = ps[4 + kh]
                for kk in range(4):
                    k = kh * 4 + kk
                    T((lambda pz=pz, kk=kk, k=k, XBt=XBt: lambda e: e.transpose(pz[:, kk * 128:(kk + 1) * 128], XBt[:, k * 128:(k + 1) * 128], identf))(),
                      r=[kxb, "identf"], w=[f"ps{4 + kh}"], inc=(kk == 3))
                for par in range(2):
                    dst = BKT[(par, ri)][:, kh * 4:(kh + 1) * 4, :]
                    V((lambda dst=dst, pz=pz, par=par: lambda e: e.tensor_scalar(dst, pz.rearrange("p (a b) -> p a b", b=128), parmask[:, par:par + 1], None, ALU.mult))(),
                      r=[f"ps{4 + kh}", "parmask", "stageA"], w=[f"BKT{par}{ri}", "BKTgen"])
    def tab_tz(j):
        un = u_all[:, j, :]
        for (TZ, kbase, ktz) in ((TZf, 0, "TZf"), (TZb, 64, "TZb")):
            for lh in range(2):
                pz = ps[4 + lh]
                for ll in range(4):
                    lag = lh * 4 + ll
                    k = 7 - lag
                    T((lambda pz=pz, ll=ll, k=k, kbase=kbase, j=j: lambda e: e.matmul(pz[:, ll * 128:(ll + 1) * 128], T1[kbase:kbase + 64, k * 128:(k + 1) * 128],
                                                                                 Cp_re[kbase:kbase + 64, j * 128:(j + 1) * 128], start=True, stop=False))(),
                      r=["T1", "Cp_re"], w=[f"ps{4 + lh}"], inc=False)
                    T((lambda pz=pz, ll=ll, k=k, kbase=kbase, j=j: lambda e: e.matmul(pz[:, ll * 128:(ll + 1) * 128], T2[kbase:kbase + 64, k * 128:(k + 1) * 128],
                                                                                 Cp_im[kbase:kbase + 64, j * 128:(j + 1) * 128], start=False, stop=True))(),
                      r=["T2", "Cp_im"], w=[f"ps{4 + lh}"], inc=(ll == 3))
                dst = TZ[:, lh * 4:(lh + 1) * 4, :]
                V((lambda dst=dst, pz=pz: lambda e: e.tensor_tensor(dst, pz.rearrange("p (a b) -> p a b", b=128),
                                                                      bdmask.unsqueeze(1).broadcast_to([128, 4, 128]), ALU.mult))(),
                  r=[f"ps{4 + lh}", "bdmask", "stageB"], w=[ktz, "TZgen"])
        V((lambda j=j: lambda e: e.scalar_tensor_tensor(TZf[:, 0, :], identf, dcol[:, j:j + 1], TZf[:, 0, :], ALU.mult, ALU.add))(),
          r=["TZf", "identf", "dcol"], w=["TZf"])
    def gen_tables(j, gl):
        g = j * 8 + gl
        Tc, Ts, Fr, Ta, kTc, kTs, kFr, kTa = tabs[gl % 2]
        if True:
            A((lambda g=g, Fr=Fr: lambda e: e.activation(W5(Fr), iidx, AF.Copy, scale=thn8[:, g:g + 1]))(), r=["iidx", "thn8"], w=[kFr])
            G((lambda Fr=Fr: lambda e: e.tensor_copy(FI, W5(Fr)))(), r=[kFr], w=["FI"])
            G((lambda Ta=Ta: lambda e: e.tensor_copy(W5(Ta), FI))(), r=["FI"], w=[kTa])
            G((lambda Fr=Fr, Ta=Ta: lambda e: e.tensor_tensor(W5(Fr), W5(Fr), W5(Ta), ALU.subtract))(), r=[kFr, kTa], w=[kFr])
            A((lambda Ts=Ts, Fr=Fr: lambda e: e.activation(W5(Ts), W5(Fr), AF.Sin, scale=TWO_PI))(), r=[kFr], w=[kTs])
            A((lambda Ta=Ta, Fr=Fr: lambda e: e.activation(W5(Ta), W5(Fr), AF.Abs))(), r=[kFr], w=[kTa])
            A((lambda Tc=Tc, Ta=Ta: lambda e: e.activation(W5(Tc), W5(Ta), AF.Sin, bias=halfpi, scale=-TWO_PI))(), r=[kTa, "halfpi"], w=[kTc])

    def stage_A(j, hook=None, skip_tables=(), after=None):
        un = u_all[:, j, :]
        for gl in range(8):
            if hook is not None:
                hook(gl)
            g = j * 8 + gl
            r4 = gl // 2
            par = gl % 2
            pS = (ps[0], ps[1]) if gl % 2 == 0 else (ps[4], ps[5])
            kS = ("ps0", "ps1") if gl % 2 == 0 else ("ps4", "ps5")
            Tc, Ts, Fr, Ta, kTc, kTs, kFr, kTa = tabs[gl % 2]
            rows = slice(32 * r4, 32 * r4 + 32)
            for k in range(8):
                for ri in range(2):
                    bk = BKT[(par, ri)]
                    T((lambda ri=ri, bk=bk, k=k, rows=rows, r4=r4, un=un, pS=pS: lambda e: e.matmul(
                        pS[ri][0:64, :], bk[rows, k, 0:64], un[rows, k * 512:(k + 1) * 512],
                        start=(k == 0), stop=(k == 7), tile_position=(32 * r4, 0)))(),
                      r=[f"BKT{par}{ri}", f"u{j}"], w=[kS[ri]], inc=False)
                    T((lambda ri=ri, bk=bk, k=k, rows=rows, r4=r4, pS=pS: lambda e: e.matmul(
                        pS[ri][64:128, :], bk[rows, k, 64:128], urev[rows, k * 512:(k + 1) * 512],
                        start=(k == 0), stop=(k == 7), tile_position=(32 * r4, 64)))(),
                      r=[f"BKT{par}{ri}", "urev"], w=[kS[ri]], inc=(k == 7))
            if gl not in skip_tables:
                A((lambda g=g, Fr=Fr: lambda e: e.activation(W5(Fr), iidx, AF.Copy, scale=thn8[:, g:g + 1]))(), r=["iidx", "thn8"], w=[kFr])
                G((lambda Fr=Fr: lambda e: e.tensor_copy(FI, W5(Fr)))(), r=[kFr], w=["FI"])
                G((lambda Ta=Ta: lambda e: e.tensor_copy(W5(Ta), FI))(), r=["FI"], w=[kTa])
                G((lambda Fr=Fr, Ta=Ta: lambda e: e.tensor_tensor(W5(Fr), W5(Fr), W5(Ta), ALU.subtract))(), r=[kFr, kTa], w=[kFr])
                A((lambda Ts=Ts, Fr=Fr: lambda e: e.activation(W5(Ts), W5(Fr), AF.Sin, scale=TWO_PI))(), r=[kFr], w=[kTs])
                A((lambda Ta=Ta, Fr=Fr: lambda e: e.activation(W5(Ta), W5(Fr), AF.Abs))(), r=[kFr], w=[kTa])
                A((lambda Tc=Tc, Ta=Ta: lambda e: e.activation(W5(Tc), W5(Ta), AF.Sin, bias=halfpi, scale=-TWO_PI))(), r=[kTa, "halfpi"], w=[kTc])
            V((lambda pS=pS, Tc=Tc: lambda e: e.tensor_tensor(W5(Xre), pS[0], W5(Tc), ALU.mult))(), r=[kS[0], kTc], w=["F0"])
            V((lambda pS=pS, Ts=Ts: lambda e: e.tensor_tensor(W5(Tb), pS[1], W5(Ts), ALU.mult))(), r=[kS[1], kTs], w=["F8"])
            V(lambda e: e.tensor_tensor(W5(Xre), W5(Xre), W5(Tb), ALU.add), r=["F0", "F8"], w=["F0"])
            V((lambda pS=pS, Tc=Tc: lambda e: e.tensor_tensor(W5(Xim), pS[1], W5(Tc), ALU.mult))(), r=[kS[1], kTc], w=["F1"])
            V((lambda pS=pS, Ts=Ts: lambda e: e.tensor_tensor(W5(Tb), pS[0], W5(Ts), ALU.mult))(), r=[kS[0], kTs], w=["F8"])
            V(lambda e: e.tensor_tensor(W5(Xim), W5(Xim), W5(Tb), ALU.subtract), r=["F1", "F8"], w=["F1"])
            rb = rho8[:, g:g + 1].broadcast_to([128, 512])
            V((lambda rb=rb: lambda e: e.tensor_tensor_scan(W5(Rre), rb, W5(Xre), 0.0, ALU.mult, ALU.add))(), r=["F0", "rho8"], w=["F2"])
            V((lambda rb=rb: lambda e: e.tensor_tensor_scan(W5(Rim), rb, W5(Xim), 0.0, ALU.mult, ALU.add))(), r=["F1", "rho8"], w=["F3"])
            sre = Sst[:, 2 * gl, 1:513]
            sim = Sst[:, 2 * gl + 1, 1:513]
            V((lambda Tc=Tc: lambda e: e.tensor_tensor(W5(Xre), W5(Rre), W5(Tc), ALU.mult))(), r=["F2", kTc], w=["F0"])
            V((lambda Ts=Ts: lambda e: e.tensor_tensor(W5(Tb), W5(Rim), W5(Ts), ALU.mult))(), r=["F3", kTs], w=["F8"])
            V((lambda sre=sre: lambda e: e.tensor_tensor(sre, W5(Xre), W5(Tb), ALU.subtract))(), r=["F0", "F8"], w=["Sst"])
            V((lambda Ts=Ts: lambda e: e.tensor_tensor(W5(Xim), W5(Rre), W5(Ts), ALU.mult))(), r=["F2", kTs], w=["F1"])
            V((lambda Tc=Tc: lambda e: e.tensor_tensor(W5(Tb), W5(Rim), W5(Tc), ALU.mult))(), r=["F3", kTc], w=["F8"])
            V((lambda sim=sim: lambda e: e.tensor_tensor(sim, W5(Xim), W5(Tb), ALU.add))(), r=["F1", "F8"], w=["Sst"])
            if after is not None:
                after(gl)

    def stage_B(j, mid_hook=None):
        un = u_all[:, j, :]
        for (kbase, usrc, ku, TZ, ktz, yacc, kacc) in (
                (64, urev, "urev", TZb, "TZb", yb_acc, "yb_acc"),
                (0, un, f"u{j}", TZf, "TZf", yf_acc, "yf_acc")):
            if kbase == 0 and mid_hook is not None:
                mid_hook()
            for k in range(8):
                py, kpy = ps[2 + k % 2], f"ps{2 + k % 2}"
                for k2 in range(k + 1):
                    T((lambda py=py, TZ=TZ, k=k, k2=k2, usrc=usrc: lambda e: e.matmul(
                        py, TZ[:, k - k2, :], usrc[:, k2 * 512:(k2 + 1) * 512], start=(k2 == 0), stop=False))(),
                      r=[ktz, ku], w=[kpy], inc=False)
                order = [(gl, ri) for gg in range(2) for ri in range(2) for gl in range(gg, 8, 2)]
                for n_, (gl, ri) in enumerate(order):
                    r4 = gl // 2
                    dz = DCz_re if ri == 0 else DCz_imn
                    T((lambda py=py, dz=dz, gl=gl, ri=ri, k=k, kbase=kbase, r4=r4, last=(n_ >= len(order) - 4): lambda e: e.matmul(
                        py[32 * r4:32 * r4 + 32, :], dz[kbase:kbase + 64, gl, k, :], Sst[kbase:kbase + 64, 2 * gl + ri, 0:512],
                        start=False, stop=last, tile_position=(kbase, 32 * r4)))(),
                      r=["DCz_re", "DCz_imn", "Sst"], w=[kpy], inc=(n_ == len(order) - 1))
                A((lambda py=py, yacc=yacc, k=k: lambda e: e.activation(yacc[:, k * 512:(k + 1) * 512], py, AF.Copy))(), r=[kpy], w=[kacc])
    def final_piece(j, hh):
        un = u_all[:, j, :]
        (pstep, _), (fstep, _) = yb_acc.ap
        if True:
            yfv = yf_acc.rearrange("p (k c) -> p k c", k=8)[:, :, 64 * hh:64 * hh + 64]
            ybv = AP(yb_acc.tensor, yb_acc.offset + fstep * (L - 1 - 64 * hh), [[pstep, 128], [-512 * fstep, 8], [-fstep, 64]])
            f0v = Hall[:, 0:1024].bitcast(F32).rearrange("p (k c) -> p k c", k=8)
            V((lambda yfv=yfv, ybv=ybv, f0v=f0v: lambda e: e.tensor_tensor(f0v, yfv, ybv, ALU.add))(),
              r=["yf_acc", "yb_acc"], w=["HF"])
            A((lambda hh=hh, un=un, f0v=f0v: lambda e: e.activation(
                un[:, hh * 512:(hh + 1) * 512].rearrange("p (c k) -> p k c", k=8), f0v, AF.Gelu_apprx_tanh))(), r=["HF"], w=[f"u{j}"])


    tab_xb(0); do_urev(0); tab_bkt(0)
    gen_tables(0, 0); gen_tables(0, 1)
    for j in range(8):
        tab_tz(j)
        dcz_ops = captured(tab_dcz, j)
        assert len(dcz_ops) == 8

        def after(gl, dcz_ops=dcz_ops):
            dcz_ops[gl]()
        stage_A(j, None, skip_tables=(0, 1), after=after)
        if j + 1 < 8:
            tab_xb(j + 1)
            tab_bkt(j + 1)
            gen_tables(j + 1, 0); gen_tables(j + 1, 1)
            stage_B(j, mid_hook=(lambda j=j: do_urev(j + 1)))
        else:
            stage_B(j)
        for hh in range(NT):
            final_piece(j, hh)

    if debug:
        dbg = nc.dram_tensor("dbg", [128, 8 * L], F32, kind="ExternalOutput").ap()
        dbg2 = nc.dram_tensor("dbg2", [128, 2 * L], F32, kind="ExternalOutput").ap()
        P.barrier()
        P.op("gpsimd", lambda e: e.dma_start(out=dbg, in_=u_all.rearrange("p a b -> p (a b)")), reads=[f"u{k}" for k in range(8)], dma_key="dbg", final=True)
        P.op("gpsimd", lambda e: e.dma_start(out=dbg2[:, 0:L], in_=yf_acc), reads=["yf_acc"], dma_key="dbg2", final=True)
        P.op("gpsimd", lambda e: e.dma_start(out=dbg2[:, L:2 * L], in_=yb_acc), reads=["yb_acc"], dma_key="dbg2", final=True)
        P.emit()
        return nc
    P.barrier()
    load(g_bc, g_bc_d, "g_bc"); load(fg_bc, fg_bc_d, "fg_bc")
    for pr4 in range(4):
        for hf in range(2):
            nb = 2 * pr4 + hf
            P.op("sync", (lambda nb=nb, pr4=pr4, hf=hf: lambda e: e.dma_start(out=Wout[:, :, nb * 128:(nb + 1) * 128],
                 in_=wsc[40 + pr4][:, hf * 1024:(hf + 1) * 1024].rearrange("p (a b) -> p a b", b=128)))(),
                 reads=[f"wsc{40 + pr4}_{hf}"], writes=["Wout"], dma_key=f"Wout{nb}")
    wcnt = [0]

    def wloadp(pr):
        slot = wcnt[0] % 4
        wcnt[0] += 1
        key = f"wslot{slot}"
        P.op("sync", (lambda slot=slot, pr=pr: lambda e: e.dma_start(out=wslot[slot], in_=wsc[pr].rearrange("p (a b) -> p a b", b=128)))(),
             reads=[f"wsc{pr}_0", f"wsc{pr}_1"], writes=[key], dma_key=key)
        return wslot[slot], key

    def proj(pz, kz, wk, hf, rhs_fn, rkeys, n=512, also_halo=None, halo_src=None):
        wt, kw = wk
        for kc in range(8):
            T((lambda kc=kc, wt=wt: lambda e: e.matmul(pz[:, 0:n], wt[:, hf * 8 + kc, :], rhs_fn(kc), start=(kc == 0), stop=(kc == 7)))(),
              r=[kw] + rkeys, w=[kz], inc=(kc == 7))
        if also_halo is not None:
            hz, c0 = also_halo
            for kc in range(8):
                T((lambda kc=kc, wt=wt, hs_=halo_src[0]: lambda e: e.matmul(hz[:, c0:c0 + 2], wt[:, hf * 8 + kc, :], hs_[:, kc, :], start=(kc == 0), stop=(kc == 7)))(),
                  r=[kw, halo_src[1]], w=["psH"], inc=(kc == 7))

    cv = F[0]
    cvh = cols_strided(cv, 0, 513, 2)
    xnh2 = H[5][:, 0:16].rearrange("p (a b) -> p a b", b=2)
    xbufs = [(xnT, xnh, "xnT", "xnh"), (xnT2, xnh2, "xnT2", "xnh2")]
    xdma[0] = "gpsimd"
    make_xn(0, True, *xbufs[0])
    pend = [None]
    for tt in range(NT):
        tsl = slice(tt * 512, (tt + 1) * 512)
        cxT, cxh, kxT, kxh = xbufs[tt % 2]
        xr = (lambda cxT: lambda kc: cxT[:, kc, :])(cxT)
        for f in range(8):
            yg = lambda kc, tsl=tsl: u_all[:, kc, tsl]
            wk = wloadp(3 * f)
            proj(ps[0], "ps0", wk, 0, xr, [kxT])
            A(lambda e: e.activation(H[2], ps[0], AF.Silu), r=["ps0"], w=["H2"])
            proj(ps[1], "ps1", wk, 1, yg, [f"u{k}" for k in range(8)])
            A(lambda e: e.activation(H[3], ps[1], AF.Sigmoid), r=["ps1"], w=["H3"])
            V((lambda f=f, tsl=tsl: lambda e: e.tensor_tensor(H[4], u_all[:, f, tsl], H[3], ALU.mult))(), r=[f"u{f}", "H3"], w=["H4"])
            V((lambda f=f: lambda e: e.tensor_tensor(ya[:, f, :], H[4], H[2], ALU.mult))(), r=["H4", "H2"], w=["ya"])
            wk = wloadp(3 * f + 1)
            proj(ps[2], "ps2", wk, 0, xr, [kxT], also_halo=(ps[6], 0), halo_src=(cxh, kxh))
            proj(ps[3], "ps3", wk, 1, xr, [kxT], also_halo=(ps[6], 2), halo_src=(cxh, kxh))
            A(lambda e: e.activation(F[1][:, 0:512], ps[2], AF.Copy), r=["ps2"], w=["F1"])
            A(lambda e: e.activation(F[2][:, 0:2], ps[6][:, 0:2], AF.Copy), r=["psH"], w=["F2"])
            V(lambda e: e.tensor_tensor(cv[:, 1:513], ps[3], F[1][:, 0:512], ALU.mult), r=["ps3", "F1"], w=["F0"])
            V(lambda e: e.tensor_tensor(cvh, ps[6][:, 2:4], F[2][:, 0:2], ALU.mult), r=["psH", "F2"], w=["F0"])
            V((lambda f=f: lambda e: e.tensor_scalar(F[3][:, 0:512], cv[:, 0:512], cw[:, f * 3:f * 3 + 1], cb[:, f:f + 1], ALU.mult, ALU.add))(),
              r=["F0", "cw", "cb"], w=["F3"])
            V((lambda f=f: lambda e: e.scalar_tensor_tensor(F[4][:, 0:512], cv[:, 1:513], cw[:, f * 3 + 1:f * 3 + 2], F[3][:, 0:512], ALU.mult, ALU.add))(),
              r=["F0", "F3"], w=["F4"])
            V((lambda f=f: lambda e: e.scalar_tensor_tensor(F[3][:, 0:512], cv[:, 2:514], cw[:, f * 3 + 2:f * 3 + 3], F[4][:, 0:512], ALU.mult, ALU.add))(),
              r=["F0", "F4"], w=["F3"])
            wk = wloadp(3 * f + 2)
            proj(ps[4], "ps4", wk, 0, xr, [kxT])
            A(lambda e: e.activation(F[6][:, 0:512], ps[4], AF.Copy), r=["ps4"], w=["F6"])
            V(lambda e: e.tensor_tensor(F[4][:, 0:512], F[6][:, 0:512], F[3][:, 0:512], ALU.mult), r=["F6", "F3"], w=["F4"])
            proj(ps[5], "ps5", wk, 1, xr, [kxT])
            A(lambda e: e.activation(F[5][:, 0:512], ps[5], AF.Silu), r=["ps5"], w=["F5"])
            V((lambda f=f: lambda e: e.tensor_tensor(yb[:, f, :], F[4][:, 0:512], F[5][:, 0:512], ALU.mult))(), r=["F4", "F5"], w=["yb"])
            if tt + 1 < NT and f % 2 == 1 and pend[0] is not None:
                pend[0]()
                pend[0] = None
            if tt + 1 < NT and f % 2 == 0:
                nxT, _, nkT, _ = xbufs[(tt + 1) % 2]
                pend[0] = make_xn_block(tt + 1, f // 2, nxT, nkT, defer=True)
        if pend[0] is not None:
            pend[0]()
            pend[0] = None
        halo_post = None
        if tt + 1 < NT:
            halo_post = make_xn(tt + 1, True, *xbufs[(tt + 1) % 2], blocks=False, defer_halo=True)
        xslots = {}
        for s4 in range(2):
            slot = xcnt[0] % 2
            xcnt[0] += 1
            xslots[s4] = slot
            load(xt[slot], x[tt * 512 + s4 * 128:tt * 512 + s4 * 128 + 128, :], f"xt{slot}", eng="gpsimd")
        for dch in range(8):
            wk = wloadp(24 + 2 * dch)
            proj(ps[0], "ps0", wk, 0, lambda kc: ya[:, kc, :], ["ya"])
            proj(ps[1], "ps1", wk, 1, xr, [kxT])
            A(lambda e: e.activation(F[1][:, 0:512], ps[1], AF.Sigmoid), r=["ps1"], w=["F1"])
            V(lambda e: e.tensor_tensor(F[2][:, 0:512], ps[0], F[1][:, 0:512], ALU.mult), r=["ps0", "F1"], w=["F2"])
            wk = wloadp(25 + 2 * dch)
            proj(ps[2], "ps2", wk, 0, lambda kc: yb[:, kc, :], ["yb"])
            proj(ps[3], "ps3", wk, 1, xr, [kxT])
            A(lambda e: e.activation(F[3][:, 0:512], ps[3], AF.Sigmoid), r=["ps3"], w=["F3"])
            V(lambda e: e.tensor_tensor(F[4][:, 0:512], ps[2], F[3][:, 0:512], ALU.mult), r=["ps2", "F3"], w=["F4"])
            V((lambda dch=dch: lambda e: e.tensor_tensor(mg[:, dch, :], F[2][:, 0:512], F[4][:, 0:512], ALU.add))(), r=["F2", "F4"], w=["mg"])
        if halo_post is not None:
            halo_post()
        cx32 = cxT.rearrange("p a b -> p (a b)").bitcast(F32)
        xextra = {}
        for s4 in (2, 3):
            xb_ = cx32[:, (s4 - 2) * 1024:(s4 - 1) * 1024]
            P.op("gpsimd", (lambda xb_=xb_, s4=s4, tt=tt: lambda e: e.dma_start(out=xb_, in_=x[tt * 512 + s4 * 128:tt * 512 + s4 * 128 + 128, :]))(),
                 writes=[kxT], dma_key=f"xx{s4}")
            xextra[s4] = xb_
        for s4 in range(4):
            r0 = tt * 512 + s4 * 128
            if s4 in xslots:
                slot = xslots[s4]
                xsrc = xt[slot]
                kx = f"xt{slot}"
            else:
                xsrc = xextra[s4]
                kx = kxT
            for half in range(2):
                bi = (2 + half) if s4 % 2 == 0 else (4 + half)
                pz = ps[bi]
                for dch in range(8):
                    T((lambda pz=pz, dch=dch, s4=s4, half=half: lambda e: e.matmul(pz, mg[:, dch, s4 * 128:(s4 + 1) * 128], Wout[:, dch, half * 512:(half + 1) * 512], start=(dch == 0), stop=(dch == 7)))(),
                      r=["mg", "Wout"], w=[f"ps{bi}"], inc=(dch == 7))
                V((lambda pz=pz, half=half, xsrc=xsrc: lambda e: e.tensor_tensor(hbuf[:, half * 512:(half + 1) * 512], pz, xsrc[:, half * 512:(half + 1) * 512], ALU.add))(),
                  r=[f"ps{bi}", kx], w=["hbuf"])
            A(lambda e: e.activation(F[6][:, 0:512], hbuf[:, 0:512], AF.Square, accum_out=ssq), r=["hbuf"], w=["ssq", "F6"])
            A(lambda e: e.activation(F[6][:, 0:512], hbuf[:, 512:1024], AF.Square, accum_out=std), r=["hbuf"], w=["std", "F6"])
            V(lambda e: e.tensor_tensor(ssq, ssq, std, ALU.add), r=["ssq", "std"], w=["ssq"])
            A(lambda e: e.activation(std, ssq, AF.Sqrt, bias=epsb, scale=1.0 / D), r=["ssq", "epsb"], w=["std"])
            V(lambda e: e.reciprocal(rstd, std), r=["std"], w=["rstd"])
            V(lambda e: e.scalar_tensor_tensor(otile, hbuf, rstd, fg_bc, ALU.mult, ALU.mult), r=["hbuf", "rstd", "fg_bc"], w=["otile"])
            P.op("gpsimd", (lambda r0=r0: lambda e: e.dma_start(out=out[r0:r0 + 128, :], in_=otile))(), reads=["otile"], dma_key="outst", final=True)

    P.emit()
    return nc


_CACHE = {}


def _prep_shared(inp):
    f32 = np.float32
    sh = {}
    sh["w_in"] = np.ascontiguousarray(inp["w_in"][0], dtype=f32)
    sh["w_glu"] = np.ascontiguousarray(inp["w_glu"][0], dtype=f32)
    sh["w_a"] = np.ascontiguousarray(inp["w_branch_a"][0], dtype=f32)
    sh["w_b"] = np.ascontiguousarray(inp["w_branch_b"][0], dtype=f32)
    sh["w_out"] = np.ascontiguousarray(inp["w_out"][0], dtype=f32)
    sh["g_bc"] = np.ascontiguousarray(np.broadcast_to(inp["norm_g"][0][None, :], (128, D)), dtype=f32)
    sh["fg_bc"] = np.ascontiguousarray(np.broadcast_to(inp["final_g"][None, :], (128, D)), dtype=f32)
    sh["dcol"] = np.ascontiguousarray(inp["ssm_d"][0].reshape(8, 128).T, dtype=f32)
    sh["cw"] = np.ascontiguousarray(inp["conv_w"][0].reshape(3, 8, 128).transpose(2, 1, 0).reshape(128, 24), dtype=f32)
    sh["cb"] = np.ascontiguousarray(inp["conv_b"][0].reshape(8, 128).T, dtype=f32)
    sh["lre"] = np.ascontiguousarray(inp["lam_re"][0].transpose(0, 2, 1).reshape(128, 64), dtype=f32)
    sh["lim"] = np.ascontiguousarray(inp["lam_im"][0].transpose(0, 2, 1).reshape(128, 64), dtype=f32)
    sh["ldt"] = np.ascontiguousarray(np.broadcast_to(inp["log_dt"][0][:, None, :], (2, 64, 64)).reshape(128, 64), dtype=f32)
    for nm, key in (("b_re", "ssm_b_re"), ("b_im", "ssm_b_im")):
        b = np.asarray(inp[key][0], dtype=f32)
        sh[nm] = np.ascontiguousarray(b.transpose(0, 2, 1, 3).reshape(128, 1024))
    for nm, key in (("c_re", "ssm_c_re"), ("c_im", "ssm_c_im")):
        c = np.asarray(inp[key][0], dtype=f32)
        sh[nm] = np.ascontiguousarray(c.transpose(0, 3, 1, 2).reshape(128, 1024))
    sh["ident"] = np.eye(128, dtype=f32)
    sh["iidx"] = np.ascontiguousarray(np.broadcast_to(np.arange(512, dtype=f32)[None, :], (128, 512)))
    sh["jR"] = np.ascontiguousarray(np.broadcast_to(np.tile(np.arange(7, -1, -1).astype(f32), 64)[None, :], (128, 512)))
    sh["jD"] = np.ascontiguousarray(np.broadcast_to(np.tile(np.arange(1, 9).astype(f32), 64)[None, :], (128, 512)))
    bd = np.zeros((128, 128), dtype=f32)
    for i in range(8):
        bd[i * 16:(i + 1) * 16, i * 16:(i + 1) * 16] = 1.0
    sh["bdmask"] = bd
    pm = np.zeros((128, 2), dtype=f32)
    for p in range(128):
        pm[p, (p // 16) % 2] = 1.0
    sh["parmask"] = pm
    return sh


def kernel(**inputs):
    inp = {k: np.asarray(v) for k, v in inputs.items()}
    if "nc" not in _CACHE:
        _CACHE["nc"] = build_program()
    nc = _CACHE["nc"]
    sh = _prep_shared(inp)
    x = np.asarray(inp["x"], dtype=np.float32)
    in_maps = []
    for c in range(8):
        m = dict(sh)
        m["x"] = np.ascontiguousarray(x[c])
        in_maps.append(m)
    res = run_bass_kernel_spmd(nc, in_maps, core_ids=list(range(8)))
    return np.stack([np.asarray(r["out"], dtype=np.float32) for r in res.results], axis=0)
```
